# Optimizing a Trainium2 kernel written in Bass

```python
import jax
import jax.numpy as jnp
from jax import lax
import numpy as np

D_MODEL = 2048
BATCH = 2
SEQ = 8192
DEPTH = 4

CHUNK = 64
Q_BLOCK = 128
MIX_WIDTH = D_MODEL
HG_WIDTH = MIX_WIDTH // 2
SB_WIDTH = MIX_WIDTH - HG_WIDTH
HG_HEAD_DIM = 128
HG_HEADS = HG_WIDTH // HG_HEAD_DIM
SB_HEAD_DIM = 128
SB_HEADS = SB_WIDTH // SB_HEAD_DIM
D_FF = -(-(8 * D_MODEL) // (3 * 256)) * 256
IN_COLS = 4 * HG_WIDTH + 3 * SB_WIDTH
IN_SPLITS = (HG_WIDTH, 2 * HG_WIDTH, 3 * HG_WIDTH, 4 * HG_WIDTH,
             4 * HG_WIDTH + SB_WIDTH, 4 * HG_WIDTH + 2 * SB_WIDTH)
N_MOD = 6
EPS = 1e-6
TINY = 1e-30

kernel_name = "hymba_hgrn2_stickbreaking_adaln_trunk"


def rms_norm(x, g):
    xf = x.astype(jnp.float32)
    y = xf * lax.rsqrt(jnp.mean(xf * xf, axis=-1, keepdims=True) + EPS)
    return (y * g.astype(jnp.float32)).astype(x.dtype)


def hgrn2_mixer(q, f_logit, v, g, lb, out_g):
    bsz, seq, _ = q.shape
    n_chunks = seq // CHUNK
    f32 = jnp.float32
    lb = lb.astype(f32)
    fl = f_logit.astype(f32)
    q_act = jax.nn.silu(q.astype(f32))
    forget = lb + (1.0 - lb) * jax.nn.sigmoid(fl)
    log_f = jnp.log(jnp.maximum(forget, TINY))
    key = (1.0 - lb) * jax.nn.sigmoid(-fl)

    def to_chunks(t):
        return t.reshape(bsz, n_chunks, CHUNK, HG_HEADS, HG_HEAD_DIM).transpose(1, 0, 3, 2, 4)

    causal = jnp.tril(jnp.ones((CHUNK, CHUNK), dtype=bool))[:, :, None]

    def chunk_step(state, inp):
        q_c, k_c, v_c, lf_c = inp
        b = jnp.cumsum(lf_c, axis=2)
        diff = b[:, :, :, None, :] - b[:, :, None, :, :]
        decay = jnp.where(causal, jnp.exp(jnp.where(causal, diff, 0.0)), 0.0)
        scores = jnp.einsum('bhtd,bhsd,bhtsd->bhts', q_c, k_c, decay)
        o_intra = jnp.einsum('bhts,bhsv->bhtv', scores, v_c)
        o_inter = jnp.einsum('bhtd,bhdv->bhtv', q_c * jnp.exp(b), state)
        b_last = b[:, :, -1, :]
        k_to_end = k_c * jnp.exp(b_last[:, :, None, :] - b)
        state = jnp.exp(b_last)[..., None] * state + jnp.einsum('bhsd,bhsv->bhdv', k_to_end, v_c)
        return state, o_intra + o_inter

    state0 = jnp.zeros((bsz, HG_HEADS, HG_HEAD_DIM, HG_HEAD_DIM), f32)
    _, o = lax.scan(chunk_step, state0,
                    (to_chunks(q_act), to_chunks(key), to_chunks(v.astype(f32)), to_chunks(log_f)))
    o = o.transpose(1, 0, 3, 2, 4).reshape(bsz, seq, HG_HEADS, HG_HEAD_DIM)
    o = rms_norm(o, out_g.reshape(HG_HEADS, HG_HEAD_DIM))
    return o.reshape(bsz, seq, HG_WIDTH) * jax.nn.silu(g.astype(f32))


def stick_breaking_mixer(q, k, v, q_g, k_g, out_g):
    bsz, seq, _ = q.shape
    f32 = jnp.float32

    def heads(t):
        return t.astype(f32).reshape(bsz, seq, SB_HEADS, SB_HEAD_DIM).transpose(0, 2, 1, 3)

    qh = rms_norm(heads(q), q_g)
    kh = rms_norm(heads(k), k_g)
    vh = heads(v)
    key_pos = jnp.arange(seq)
    scale = SB_HEAD_DIM ** -0.5

    def query_block(blk):
        q0 = blk * Q_BLOCK
        qb = lax.dynamic_slice_in_dim(qh, q0, Q_BLOCK, axis=2)
        z = jnp.einsum('bhqd,bhkd->bhqk', qb, kh) * scale
        q_pos = q0 + jnp.arange(Q_BLOCK)
        earlier = key_pos[None, :] < q_pos[:, None]
        log_keep = jnp.where(earlier, jax.nn.log_sigmoid(-z), 0.0)
        log_keep_between = lax.cumsum(log_keep, axis=3, reverse=True) - log_keep
        a = jnp.where(earlier, jnp.exp(jax.nn.log_sigmoid(z) + log_keep_between), 0.0)
        return jnp.einsum('bhqk,bhkv->bhqv', a, vh)

    o = lax.map(query_block, jnp.arange(seq // Q_BLOCK))
    o = o.transpose(1, 0, 3, 2, 4).reshape(bsz, seq, SB_HEADS, SB_HEAD_DIM)
    o = rms_norm(o, out_g.reshape(SB_HEADS, SB_HEAD_DIM))
    return o.reshape(bsz, seq, SB_WIDTH)


def setup_inputs(seed: int = 0) -> dict:
    key = jax.random.key(seed)
    ks = jax.random.split(key, 16)
    f32 = jnp.float32

    def normal(k, shape, scale):
        return jax.random.normal(k, shape, f32) * scale

    def gain(k, shape):
        return 1.0 + 0.02 * jax.random.normal(k, shape, f32)

    return {
        'x': normal(ks[0], (BATCH, SEQ, D_MODEL), 1.0),
        'c': normal(ks[1], (BATCH, D_MODEL), 1.0),
        'norm1_g': gain(ks[2], (DEPTH, D_MODEL)),
        'w_in': normal(ks[3], (DEPTH, D_MODEL, IN_COLS), D_MODEL ** -0.5),
        'hg_lb_logits': normal(ks[4], (DEPTH, HG_WIDTH), 0.5),
        'hg_out_g': gain(ks[5], (DEPTH, HG_WIDTH)),
        'sb_q_g': gain(ks[6], (DEPTH, SB_HEAD_DIM)),
        'sb_k_g': gain(ks[7], (DEPTH, SB_HEAD_DIM)),
        'sb_out_g': gain(ks[8], (DEPTH, SB_WIDTH)),
        'w_out': normal(ks[9], (DEPTH, MIX_WIDTH, D_MODEL), MIX_WIDTH ** -0.5),
        'norm2_g': gain(ks[10], (DEPTH, D_MODEL)),
        'w_ffn_in': normal(ks[11], (DEPTH, D_MODEL, 2 * D_FF), D_MODEL ** -0.5),
        'w_ffn_out': normal(ks[12], (DEPTH, D_FF, D_MODEL), D_FF ** -0.5),
        'w_ada': normal(ks[13], (DEPTH, D_MODEL, N_MOD * D_MODEL), 0.5 * D_MODEL ** -0.5),
        'b_ada': normal(ks[14], (DEPTH, N_MOD * D_MODEL), 0.01),
    }


def reference(x, c, norm1_g, w_in, hg_lb_logits, hg_out_g, sb_q_g, sb_k_g, sb_out_g,
              w_out, norm2_g, w_ffn_in, w_ffn_out, w_ada, b_ada):
    lb_soft = jax.nn.softmax(hg_lb_logits.astype(jnp.float32), axis=0)
    lower_bounds = jnp.cumsum(lb_soft, axis=0) - lb_soft[0]
    cond = jax.nn.silu(c)
    for layer in range(DEPTH):
        mod = cond @ w_ada[layer] + b_ada[layer]
        sh1, sc1, g1, sh2, sc2, g2 = [m[:, None, :] for m in jnp.split(mod, N_MOD, axis=-1)]

        h = rms_norm(x, norm1_g[layer]) * (1.0 + sc1) + sh1
        proj = h @ w_in[layer]
        hg_q, hg_f, hg_i, hg_g, sb_q, sb_k, sb_v = jnp.split(proj, IN_SPLITS, axis=-1)
        o_hg = hgrn2_mixer(hg_q, hg_f, hg_i, hg_g, lower_bounds[layer], hg_out_g[layer])
        o_sb = stick_breaking_mixer(sb_q, sb_k, sb_v, sb_q_g[layer], sb_k_g[layer], sb_out_g[layer])
        mixed = jnp.concatenate([o_hg, o_sb], axis=-1).astype(x.dtype) @ w_out[layer]
        x = x + g1 * mixed

        h = rms_norm(x, norm2_g[layer]) * (1.0 + sc2) + sh2
        gate, up = jnp.split(h @ w_ffn_in[layer], 2, axis=-1)
        x = x + g2 * ((jax.nn.silu(gate) * up) @ w_ffn_out[layer])
    return x
```

```python
import contextlib
import numpy as np
import concourse.bass as bass
import concourse.mybir as mybir
from concourse.bass_utils import run_bass_kernel_spmd

F32 = mybir.dt.float32
BF16 = mybir.dt.bfloat16
AF = mybir.ActivationFunctionType
ALU = mybir.AluOpType
AX = mybir.AxisListType

EPS = 1e-6
TINY = 1e-30
BIGNEG = -30000.0
NCORES = 8
G = 4


class Cfg:
    def __init__(self, D, S, L, DFF):
        self.D, self.S, self.L, self.DFF = D, S, L, DFF
        self.T = S // G
        self.KD = D // 128
        self.H = D // 2 // 128
        self.NF = DFF // 128
        self.TT = min(512, self.T)
        self.NT = self.T // self.TT
        self.NMB = 6 * self.KD // G
        self.NCH = self.T // 64
        self.NQB = self.T // 128
        self.KT = min(1024, self.T)
        self.NKT = self.T // self.KT


class Sched:
    ENG = ("pe", "act", "dve", "pool", "sp")

    def __init__(self, nc, stack):
        self.nc = nc
        self.stack = stack
        self.ops = {e: [] for e in self.ENG}
        self.last = {e: None for e in self.ENG}
        self.cnt = {}
        self.sems = {}
        self.seen = {e: {} for e in self.ENG}
        self.pending = {}
        for e in self.ENG:
            self._mksem(e)

    def _mksem(self, key):
        self.sems[key] = self.stack.enter_context(self.nc.semaphore("s_" + key))
        self.cnt[key] = 0

    def op(self, eng, fn):
        rec = {"fn": fn, "sem": None, "val": None, "inc": None}
        self.ops[eng].append(rec)
        self.last[eng] = rec
        return rec

    def sig(self, eng):
        rec = self.last[eng]
        assert rec is not None
        if rec["sem"] is None:
            self.cnt[eng] += 1
            rec["sem"], rec["val"], rec["inc"] = eng, self.cnt[eng], 1
        assert rec["sem"] == eng or rec["inc"] != 1, "sig on dma op"
        return (rec["sem"], rec["val"])

    def wait(self, eng, *evs):
        for ev in evs:
            if ev is None:
                continue
            if isinstance(ev, list):
                self.wait(eng, *ev)
                continue
            key, val = ev
            if self.seen[eng].get(key, 0) >= val:
                continue
            self.seen[eng][key] = val
            self.ops[eng].append({"wait": key, "val": val})

    def dma(self, q, out, in_, chan):
        if chan not in self.sems:
            self._mksem(chan)
        rec = self.op(q, lambda e: e.dma_start(out=out, in_=in_))
        self.cnt[chan] += 16
        rec["sem"], rec["val"], rec["inc"] = chan, self.cnt[chan], 16
        ev = (chan, self.cnt[chan])
        self.pending[chan] = ev
        return ev

    def dma_split(self, q, out, in_, chan, n):
        d = out.shape[1]
        n = min(n, d)
        step = (d + n - 1) // n
        ev = None
        for a in range(0, d, step):
            b = min(d, a + step)
            ev = self.dma(q, out[:, a:b], in_[:, a:b], chan)
        return ev

    def coll(self, kind, groups, in_, out, chan):
        if chan not in self.sems:
            self._mksem(chan)
        rec = self.op("pool", lambda e: e.collective_compute(
            kind, ALU.bypass, replica_groups=groups, ins=[in_], outs=[out]))
        self.cnt[chan] += 1
        rec["sem"], rec["val"], rec["inc"] = chan, self.cnt[chan], None
        ev = (chan, self.cnt[chan])
        self.pending[chan] = ev
        return ev

    def barrier(self):
        evs = []
        for e in ("pe", "act", "dve", "pool"):
            rec = self.last[e]
            if rec is not None and (rec["sem"] is None or rec["sem"] == e):
                evs.append(self.sig(e))
        evs += list(self.pending.values())
        for e in self.ENG:
            self.wait(e, *evs)
        return evs

    def emit(self, block):
        def mk(eng):
            def body(e):
                for rec in self.ops[eng]:
                    if "wait" in rec:
                        e.wait_ge(self.sems[rec["wait"]], rec["val"])
                    else:
                        ins = rec["fn"](e)
                        if rec["sem"] is not None:
                            if rec["inc"] is None:
                                ins.then_inc(self.sems[rec["sem"]])
                            else:
                                ins.then_inc(self.sems[rec["sem"]], rec["inc"])
            return body
        block.tensor(mk("pe"))
        block.scalar(mk("act"))
        block.vector(mk("dve"))
        block.gpsimd(mk("pool"))
        block.sync(mk("sp"))


def build(cfg, dbg=()):
    D, T, L, KD, H, NF, TT, NT, NMB = cfg.D, cfg.T, cfg.L, cfg.KD, cfg.H, cfg.NF, cfg.TT, cfg.NT, cfg.NMB
    NCH, NQB, KT, NKT = cfg.NCH, cfg.NQB, cfg.KT, cfg.NKT
    HW = H * 128
    nc = bass.Bass("TRN2", target_bir_lowering=False)
    stack = contextlib.ExitStack()

    def din(name, shape, dt=F32):
        return nc.dram_tensor(name, list(shape), dt, kind="ExternalInput").ap()

    def dscr(name, shape, dt=F32):
        return nc.dram_tensor(name, list(shape), dt).ap()

    xT = din("xT", [128, KD, T])
    cT = din("cT", [128, KD, 2])
    wada = din("wada", [L, NMB, 128, KD * 128])
    bada = din("bada", [128, L * NMB * 2])
    n1g = din("n1g", [128, L * KD])
    n2g = din("n2g", [128, L * KD])
    hgog = din("hgog", [128, L * H])
    sbog = din("sbog", [128, L * H])
    sbqg = din("sbqg", [128, L])
    sbkg = din("sbkg", [128, L])
    lbl = din("lbl", [128, L * H])
    win_fm = din("win_fm", [L, 5 * H, 128, KD * 128])
    win_tm = din("win_tm", [L, 2, 128, KD * HW])
    wout = din("wout", [L, KD, 128, KD * 128])
    wfi = din("wfi", [L, NF, 128, 2 * KD * 128])
    wfo = din("wfo", [L, KD, 128, NF * 128])
    flags = din("flags", [128, 32])
    cmask_hg = din("cmask_hg", [128, 128])
    cmask_sb = din("cmask_sb", [128, 128])
    cident = din("cident", [128, 128])
    cscan = din("cscan", [128, min(T, 1024)])
    yT = nc.dram_tensor("yT", [128, KD, T], F32, kind="ExternalOutput").ap()
    dbg_out = {}

    xs = dscr("xs", [128, KD, T])
    d_qa = dscr("d_qa", [H, 128, T])
    d_kk = dscr("d_kk", [H, 128, T])
    d_lf = dscr("d_lf", [H, 128, T])
    d_gs = dscr("d_gs", [H, 128, T], BF16)
    d_vhg = dscr("d_vhg", [T, HW], BF16)
    d_qn = dscr("d_qn", [H, 128, T], BF16)
    d_kloc = [dscr(f"d_kloc{h}", [128, T], BF16) for h in range(H)]
    d_kall = [dscr(f"d_kall{h}", [G * 128, T], BF16) for h in range(H)]
    d_vloc = [dscr(f"d_vloc{h}", [T, 128], BF16) for h in range(H)]
    d_vall = [dscr(f"d_vall{h}", [G * T, 128], BF16) for h in range(H)]
    d_stloc = dscr("d_stloc", [H * 128, 132])
    d_stall = dscr("d_stall", [G * H * 128, 132])
    d_oloc = dscr("d_oloc", [H, 128, T])
    d_qh = dscr("d_qh", [H, 128, T], BF16)
    d_act = dscr("d_act", [NF, 128, T], BF16)
    d_modsrc = dscr("d_modsrc", [128, L * NMB * 2])
    d_modall = dscr("d_modall", [G * 128, L * NMB * 2])

    S = Sched(nc, stack)

    def sb(name, shape, dt=F32):
        return stack.enter_context(nc.sbuf_tensor(name, list(shape), dt))

    def ps(name, shape, dt=F32):
        return stack.enter_context(nc.psum_tensor(name, list(shape), dt))

    HT = sb("HT", [128, KD, T], BF16)
    WRN = max(NF * 128, 2 * KD * 128)
    WR = [sb(f"WR{i}", [128, WRN], BF16) for i in range(2)]
    BIGN = 18432
    BIG = sb("BIG", [128, BIGN], F32)
    BIGB = BIG[:, :].bitcast(BF16)
    NSTG = 4
    STG = [sb(f"STG{i}", [128, 512], F32) for i in range(NSTG)]
    NSTB = 4
    STB = [sb(f"STB{i}", [128, 512], BF16) for i in range(NSTB)]
    RSTD = [sb(f"RSTD{i}", [128, 512], F32) for i in range(2)]
    c_flags = sb("c_flags", [128, 32])
    c_mhg = sb("c_mhg", [128, 128])
    c_msb = sb("c_msb", [128, 128])
    c_idf = sb("c_idf", [128, 128])
    c_id = sb("c_id", [128, 128], BF16)
    c_ones = sb("c_ones", [128, 128], BF16)
    c_scan = sb("c_scan", [128, min(T, 1024)])
    p_n1g = sb("p_n1g", [128, L * KD])
    p_n2g = sb("p_n2g", [128, L * KD])
    p_hgog = sb("p_hgog", [128, L * H])
    p_sbog = sb("p_sbog", [128, L * H])
    p_sbqg = sb("p_sbqg", [128, L])
    p_sbkg = sb("p_sbkg", [128, L])
    p_lbl = sb("p_lbl", [128, L * H])
    p_bada = sb("p_bada", [128, L * NMB * 2])
    p_cT = sb("p_cT", [128, KD * 2])
    p_cond = sb("p_cond", [128, KD * 2], BF16)
    modloc = sb("modloc", [128, L * NMB * 2])
    modall = sb("modall", [128, G, L * NMB * 2])
    modl = sb("modl", [128, L, 6 * KD])
    lb_e = sb("lb_e", [128, L * H])
    lb_s = sb("lb_s", [128, H])
    lb_lb = sb("lb_lb", [128, L * H])
    lb_om = sb("lb_om", [128, L * H])
    lb_nom = sb("lb_nom", [128, L * H])
    lb_tm = sb("lb_tm", [128, L * H])
    colA = sb("colA", [128, 2 * KD])
    colq = sb("colq", [128, 2])
    c_eps = sb("c_eps", [128, 1])
    c_onesf = sb("c_onesf", [128, 1024])
    S32 = sb("S32", [128, 128])
    Sb = sb("Sb", [128, 128], BF16)
    STALL = sb("STALL", [128, G, 132])
    ACC = sb("ACC", [128, 128])
    SIN = sb("SIN", [128, 128])
    DTOT = sb("DTOT", [128, 4])
    RC = sb("RC", [128, 32])
    MD = sb("MD", [128, G, 128])
    NB = sb("NB", [128, 64])
    ATs = [sb(f"AT{i}", [128, 1024], BF16) for i in range(2)]
    As = [sb(f"A{i}", [128, 1024], BF16) for i in range(2)]
    ONb = sb("ONb", [128, 128], BF16)

    PS = [ps(f"PS{i}", [128, 1024]) for i in range(3)]
    PST = ps("PST", [128, 2048], BF16)
    psb = [PS[i // 2][:, (i % 2) * 512:(i % 2) * 512 + 512] for i in range(6)]

    SP, PE, ACT, DVE, POOL = "sp", "pe", "act", "dve", "pool"

    evs = []
    for dst, src in ((c_flags, flags), (c_mhg, cmask_hg), (c_msb, cmask_sb), (c_idf, cident), (c_scan, cscan),
                     (p_n1g, n1g), (p_n2g, n2g), (p_hgog, hgog), (p_sbog, sbog), (p_sbqg, sbqg),
                     (p_sbkg, sbkg), (p_lbl, lbl), (p_bada, bada)):
        evs.append(S.dma(SP, dst[:, :], src[:, :], "ld0"))
    evs.append(S.dma(SP, p_cT[:, :], cT.rearrange("p k b -> p (k b)"), "ld0"))
    ev_x0 = S.dma_split(SP, xs[:, :, :], xT[:, :, :], "ldx", KD)
    ld0 = evs[-1]
    for e in (ACT, DVE, POOL, PE):
        S.wait(e, ld0)
    S.op(DVE, lambda e: e.tensor_copy(out=c_id[:, :], in_=c_idf[:, :]))
    S.op(DVE, lambda e: e.memset(c_ones[:, :], 1.0))
    S.op(DVE, lambda e: e.memset(c_eps[:, :], EPS))
    S.op(DVE, lambda e: e.memset(c_onesf[:, :], 1.0))
    S.op(DVE, lambda e: e.memset(DTOT[:, :], 0.0))
    for o in range(G):
        S.op(DVE, lambda e, o=o: e.tensor_scalar(out=MD[:, o, :], in0=c_msb[:, :], scalar1=c_flags[:, 8 + o:9 + o],
                                                 scalar2=c_flags[:, 4 + o:5 + o], op0=ALU.mult, op1=ALU.add))
    ev_const = S.sig(DVE)
    S.op(ACT, lambda e: e.activation(out=p_cond[:, :], in_=p_cT[:, :], func=AF.Silu))
    ev_cond = S.sig(ACT)
    S.op(ACT, lambda e: e.activation(out=lb_e[:, :], in_=p_lbl[:, :], func=AF.Exp))
    ev = S.sig(ACT)
    S.wait(DVE, ev)
    S.op(DVE, lambda e: e.tensor_copy(out=lb_s[:, :], in_=lb_e[:, 0:H]))
    for l in range(1, L):
        S.wait(DVE, S.sig(DVE))
        S.op(DVE, lambda e, l=l: e.tensor_add(out=lb_s[:, :], in0=lb_s[:, :], in1=lb_e[:, l * H:(l + 1) * H]))
    S.wait(DVE, S.sig(DVE))
    S.op(DVE, lambda e: e.reciprocal(out=lb_s[:, :], in_=lb_s[:, :]))
    S.wait(DVE, S.sig(DVE))
    S.op(DVE, lambda e: e.memset(lb_lb[:, 0:H], 0.0))
    for l in range(1, L):
        S.wait(DVE, S.sig(DVE))
        S.op(DVE, lambda e, l=l: e.tensor_mul(out=lb_e[:, l * H:(l + 1) * H], in0=lb_e[:, l * H:(l + 1) * H], in1=lb_s[:, :]))
        S.wait(DVE, S.sig(DVE))
        S.op(DVE, lambda e, l=l: e.tensor_add(out=lb_lb[:, l * H:(l + 1) * H], in0=lb_lb[:, (l - 1) * H:l * H],
                                              in1=lb_e[:, l * H:(l + 1) * H]))
    S.wait(DVE, S.sig(DVE))
    S.op(DVE, lambda e: e.tensor_scalar(out=lb_om[:, :], in0=lb_lb[:, :], scalar1=-1.0, scalar2=1.0, op0=ALU.mult, op1=ALU.add))
    S.op(DVE, lambda e: e.tensor_scalar(out=lb_tm[:, :], in0=lb_lb[:, :], scalar1=-1.0, scalar2=TINY, op0=ALU.mult, op1=ALU.add))
    S.wait(DVE, S.sig(DVE))
    S.op(DVE, lambda e: e.tensor_scalar(out=lb_nom[:, :], in0=lb_om[:, :], scalar1=-1.0, scalar2=None, op0=ALU.mult))
    ev_lb = S.sig(DVE)

    S.wait(PE, ev_cond)
    wfree = [None, None]
    widx = [0]

    def wload(src_ap, ncols):
        i = widx[0] % 2
        widx[0] += 1
        S.wait(POOL, wfree[i])
        ev = S.dma(POOL, WR[i][:, 0:ncols], src_ap, f"w{i}")
        return i, ev

    pmod = psb[0]
    for l in range(L):
        for jb in range(NMB):
            i, ev = wload(wada[l, jb], KD * 128)
            S.wait(PE, ev)
            col = (l * NMB + jb) * 2
            for kc in range(KD):
                S.op(PE, lambda e, i=i, kc=kc, col=col: e.matmul(
                    pmod[:, col:col + 2], lhsT=WR[i][:, kc * 128:(kc + 1) * 128], rhs=p_cond[:, kc * 2:kc * 2 + 2],
                    start=(kc == 0), stop=(kc == KD - 1)))
            wfree[i] = S.sig(PE)
    S.wait(DVE, S.sig(PE))
    NM = L * NMB * 2
    S.op(DVE, lambda e: e.tensor_add(out=modloc[:, :], in0=pmod[:, 0:NM], in1=p_bada[:, :]))
    ev = S.sig(DVE)
    S.wait(SP, ev)
    ev = S.dma(SP, d_modsrc[:, :], modloc[:, :], "mod")
    S.wait(POOL, ev)
    ev = S.coll("AllGather", [[0, 1, 2, 3], [4, 5, 6, 7]], d_modsrc, d_modall, "cc")
    S.wait(SP, ev)
    ev = S.dma(SP, modall[:, :, :], d_modall.rearrange("(r p) f -> p r f", p=128), "mod")
    S.wait(DVE, ev)
    for r in range(G):
        src = modall[:, r, :].rearrange("p (l j b) -> p l j b", l=L, j=NMB, b=2)
        dst = modl[:, :, r * NMB:(r + 1) * NMB]
        S.op(DVE, lambda e, src=src, dst=dst: e.tensor_scalar(out=dst, in0=src[:, :, :, 0], scalar1=c_flags[:, 24:25],
                                                              scalar2=None, op0=ALU.mult))
        S.wait(DVE, S.sig(DVE))
        S.op(DVE, lambda e, src=src, dst=dst: e.scalar_tensor_tensor(out=dst, in0=src[:, :, :, 1], scalar=c_flags[:, 25:26],
                                                                     in1=dst, op0=ALU.mult, op1=ALU.add))
    ev_mod = S.sig(DVE)
    S.wait(SP, ev_x0)
    S.barrier()

    ones_col = c_ones

    def rstd_from(pss, out, n, width, evp=None):
        S.wait(ACT, evp)
        S.op(ACT, lambda e: e.activation(out=out, in_=pss, func=AF.Ln, scale=1.0 / n, bias=c_eps[:, 0:1]))
        S.wait(ACT, S.sig(ACT))
        S.op(ACT, lambda e: e.activation(out=out, in_=out, func=AF.Exp, scale=-0.5))
        return S.sig(ACT)

    def scan_T(out, d0tile, d1):
        PW = min(T, 1024)
        ev = None
        for pc in range(T // PW):
            a = pc * PW
            S.wait(DVE, ev)
            ini = 0.0 if pc == 0 else out[:, a - 1:a]
            S.op(DVE, lambda e, a=a, ini=ini: e.tensor_tensor_scan(out=out[:, a:a + PW], data0=d0tile[:, 0:PW], data1=d1[:, a:a + PW],
                                                                   initial=ini, op0=ALU.mult, op1=ALU.add))
            ev = S.sig(DVE)
        return ev

    stg_free = [None] * NSTG
    stg_i = [0]
    stb_free = [None] * NSTB
    stb_i = [0]

    def get_stg(eng):
        i = stg_i[0] % NSTG
        stg_i[0] += 1
        S.wait(eng, stg_free[i])
        return i

    def get_stb(eng):
        i = stb_i[0] % NSTB
        stb_i[0] += 1
        S.wait(eng, stb_free[i])
        return i

    psfree = [None] * 6
    psidx = [0]

    def get_ps(k=4):
        i = psidx[0] % k
        psidx[0] += 1
        S.wait(PE, psfree[i])
        return i

    def norm_to_HT(l, which, g_sb):
        base = 0 if which == 0 else 3 * KD
        A = colA[:, which * KD:(which + 1) * KD]
        S.op(DVE, lambda e: e.scalar_tensor_tensor(out=A, in0=modl[:, l, base + KD:base + 2 * KD], scalar=1.0,
                                                   in1=g_sb[:, l * KD:(l + 1) * KD], op0=ALU.add, op1=ALU.mult))
        evA = S.sig(DVE)
        XT = BIG[:, 0:KD * TT].rearrange("p (k t) -> p k t", k=KD)
        SQ = BIGB[:, 2 * KD * TT:2 * KD * TT + KD * TT].rearrange("p (k t) -> p k t", k=KD)
        xt_free = None
        for tt in range(NT):
            ts = slice(tt * TT, (tt + 1) * TT)
            S.wait(SP, xt_free)
            ev = S.dma_split(SP, XT, xs[:, :, ts], "xt", 4)
            S.wait(ACT, ev)
            S.wait(DVE, ev)
            S.op(ACT, lambda e: e.activation(out=SQ, in_=XT, func=AF.Square))
            evq = S.sig(ACT)
            pi = get_ps()
            S.wait(PE, evq, ev_const)
            for kc in range(KD):
                S.op(PE, lambda e, kc=kc, pi=pi: e.matmul(psb[pi][:, 0:TT], lhsT=c_ones[:, :], rhs=SQ[:, kc, :],
                                                          start=(kc == 0), stop=(kc == KD - 1)))
            evp = S.sig(PE)
            rs = RSTD[tt % 2]
            evr = rstd_from(psb[pi][:, 0:TT], rs[:, 0:TT], D, TT, evp)
            psfree[pi] = evr
            S.wait(DVE, evr)
            for kc in range(KD):
                S.op(DVE, lambda e, kc=kc, rs=rs: e.tensor_mul(out=XT[:, kc, :], in0=XT[:, kc, :], in1=rs[:, 0:TT]))
            evm = S.sig(DVE)
            S.wait(ACT, evm, evA, ev_mod)
            for kc in range(KD):
                S.op(ACT, lambda e, kc=kc, ts=ts: e.activation(out=HT[:, kc, ts], in_=XT[:, kc, :], func=AF.Identity,
                                                               scale=A[:, kc:kc + 1],
                                                               bias=modl[:, l, base + kc:base + kc + 1]))
            xt_free = S.sig(ACT)
        return xt_free

    def gemm_fm(wsrc_fn, nblk, nk, rhs_fn, ntt, tw, evac_fn, wcols):
        for blk in range(nblk):
            i, ev = wload(wsrc_fn(blk), wcols)
            S.wait(PE, ev)
            for tt in range(ntt):
                pi = get_ps()
                for kc in range(nk):
                    S.op(PE, lambda e, i=i, kc=kc, tt=tt, pi=pi: e.matmul(
                        psb[pi][:, 0:tw], lhsT=WR[i][:, kc * 128:(kc + 1) * 128], rhs=rhs_fn(kc, tt),
                        start=(kc == 0), stop=(kc == nk - 1)))
                evp = S.sig(PE)
                if tt == ntt - 1:
                    wfree[i] = evp
                psfree[pi] = evac_fn(blk, tt, psb[pi][:, 0:tw], evp)

    for l in range(L):
        evn = norm_to_HT(l, 0, p_n1g)
        S.wait(PE, evn)
        S.op(DVE, lambda e, l=l: e.tensor_scalar(out=colq[:, 0:1], in0=p_sbqg[:, l:l + 1], scalar1=128.0 ** -0.5, scalar2=None, op0=ALU.mult))
        S.op(DVE, lambda e, l=l: e.tensor_copy(out=colq[:, 1:2], in_=p_sbkg[:, l:l + 1]))
        ev_colq = S.sig(DVE)

        def evac_in(blk, tt, pss, evp, l=l):
            kind, h = blk // H, blk % H
            ts = slice(tt * TT, (tt + 1) * TT)
            lh = l * H + h
            if kind == 0:
                si = get_stg(ACT)
                S.wait(ACT, evp)
                S.op(ACT, lambda e: e.activation(out=STG[si][:, 0:TT], in_=pss, func=AF.Silu))
                ev = S.sig(ACT)
                S.wait(SP, ev)
                stg_free[si] = S.dma(SP, d_qa[h, :, ts], STG[si][:, 0:TT], f"sg{si}")
                return ev
            if kind == 1:
                si = get_stb(ACT)
                S.wait(ACT, evp)
                S.op(ACT, lambda e: e.activation(out=STB[si][:, 0:TT], in_=pss, func=AF.Silu))
                ev = S.sig(ACT)
                S.wait(SP, ev)
                stb_free[si] = S.dma(SP, d_gs[h, :, ts], STB[si][:, 0:TT], f"sb{si}")
                return ev
            if kind == 2:
                s0 = get_stg(ACT)
                S.wait(ACT, evp, ev_lb)
                S.op(ACT, lambda e: e.activation(out=STG[s0][:, 0:TT], in_=pss, func=AF.Sigmoid))
                ev0 = S.sig(ACT)
                s1 = get_stg(DVE)
                s2 = get_stg(DVE)
                S.wait(DVE, ev0)
                S.op(DVE, lambda e: e.tensor_scalar(out=STG[s1][:, 0:TT], in0=STG[s0][:, 0:TT], scalar1=lb_om[:, lh:lh + 1],
                                                    scalar2=lb_tm[:, lh:lh + 1], op0=ALU.mult, op1=ALU.max))
                ev1 = S.sig(DVE)
                S.op(DVE, lambda e: e.tensor_scalar(out=STG[s2][:, 0:TT], in0=STG[s0][:, 0:TT], scalar1=lb_nom[:, lh:lh + 1],
                                                    scalar2=lb_om[:, lh:lh + 1], op0=ALU.mult, op1=ALU.add))
                ev2 = S.sig(DVE)
                stg_free[s0] = ev2
                S.wait(SP, ev2)
                stg_free[s2] = S.dma(SP, d_kk[h, :, ts], STG[s2][:, 0:TT], f"sg{s2}")
                S.wait(ACT, ev1)
                S.op(ACT, lambda e: e.activation(out=STG[s1][:, 0:TT], in_=STG[s1][:, 0:TT], func=AF.Ln,
                                                 bias=lb_lb[:, lh:lh + 1], scale=1.0))
                ev3 = S.sig(ACT)
                S.wait(SP, ev3)
                stg_free[s1] = S.dma(SP, d_lf[h, :, ts], STG[s1][:, 0:TT], f"sg{s1}")
                return ev0
            sq = get_stb(ACT)
            S.wait(ACT, evp)
            S.op(ACT, lambda e: e.activation(out=STB[sq][:, 0:TT], in_=pss, func=AF.Square))
            ev0 = S.sig(ACT)
            S.wait(PE, ev0)
            S.wait(PE, psfree[4])
            S.op(PE, lambda e: e.matmul(psb[4][:, 0:TT], lhsT=c_ones[:, :], rhs=STB[sq][:, 0:TT], start=True, stop=True))
            ev1 = S.sig(PE)
            stb_free[sq] = ev1
            rs = RSTD[0]
            evr = rstd_from(psb[4][:, 0:TT], rs[:, 0:TT], 128.0, TT, ev1)
            psfree[4] = evr
            si = get_stg(DVE)
            S.wait(DVE, evr, evp)
            S.op(DVE, lambda e: e.tensor_mul(out=STG[si][:, 0:TT], in0=pss, in1=rs[:, 0:TT]))
            ev2 = S.sig(DVE)
            so = get_stb(ACT)
            S.wait(ACT, ev2, ev_colq)
            S.op(ACT, lambda e: e.activation(out=STB[so][:, 0:TT], in_=STG[si][:, 0:TT], func=AF.Copy,
                                             scale=colq[:, kind - 3:kind - 2]))
            ev3 = S.sig(ACT)
            stg_free[si] = ev3
            S.wait(SP, ev3)
            if kind == 3:
                stb_free[so] = S.dma(SP, d_qn[h, :, ts], STB[so][:, 0:TT], f"sb{so}")
            else:
                stb_free[so] = S.dma(SP, d_kloc[h][:, ts], STB[so][:, 0:TT], f"sb{so}")
            return ev2

        gemm_fm(lambda blk, l=l: win_fm[l, blk], 5 * H, KD, lambda kc, tt: HT[:, kc, tt * TT:(tt + 1) * TT], NT, TT,
                evac_in, KD * 128)

        WV = BIGB[:, 0:KD * HW].rearrange("p (k c) -> p k c", k=KD)
        CW = min(512, HW)
        wv_free = S.sig(ACT)
        for which, dst in ((0, d_vhg), (1, None)):
            S.wait(POOL, wv_free, S.sig(PE) if S.last[PE]["sem"] in (None, PE) else None)
            ev = S.dma_split(POOL, WV, win_tm[l, which].rearrange("p (k c) -> p k c", k=KD), "wv", KD)
            S.wait(PE, ev)
            for tb in range(T // 128):
                for cg in range(HW // CW):
                    pi = get_ps()
                    for kc in range(KD):
                        S.op(PE, lambda e, kc=kc, tb=tb, cg=cg, pi=pi: e.matmul(
                            psb[pi][:, 0:CW], lhsT=HT[:, kc, tb * 128:(tb + 1) * 128], rhs=WV[:, kc, cg * CW:(cg + 1) * CW],
                            start=(kc == 0), stop=(kc == KD - 1)))
                    evp = S.sig(PE)
                    si = get_stb(DVE)
                    S.wait(DVE, evp)
                    S.op(DVE, lambda e, si=si, pi=pi: e.tensor_copy(out=STB[si][:, 0:CW], in_=psb[pi][:, 0:CW]))
                    ev = S.sig(DVE)
                    psfree[pi] = ev
                    S.wait(SP, ev)
                    if dst is not None:
                        stb_free[si] = S.dma(SP, dst[tb * 128:(tb + 1) * 128, cg * CW:(cg + 1) * CW], STB[si][:, 0:CW], f"sb{si}")
                    else:
                        for hh in range(CW // 128):
                            stb_free[si] = S.dma(SP, d_vloc[cg * (CW // 128) + hh][tb * 128:(tb + 1) * 128, :],
                                                 STB[si][:, hh * 128:(hh + 1) * 128], f"sb{si}")
            wv_free = S.sig(PE)
        S.barrier()
        if "B" in dbg and l == 0:
            break

        GR = [[0, 1, 2, 3], [4, 5, 6, 7]]
        ev_kall = [S.coll("AllGather", GR, d_kloc[h], d_kall[h], "cc") for h in range(H)][-1]
        ev_vall = [S.coll("AllGather", GR, d_vloc[h], d_vall[h], "cc") for h in range(H)][-1]

        if "C0" in dbg and l == 0:
            S.barrier()
            break
        QA = BIG[:, 0:T]; KK = BIG[:, T:2 * T]; LF = BIG[:, 2 * T:3 * T]; X1 = BIG[:, 3 * T:4 * T]; X2 = BIG[:, 4 * T:5 * T]
        X3 = LF
        OT = X1
        bo = 10 * T
        QTl = BIGB[:, bo:bo + T]; KTl = BIGB[:, bo + T:bo + 2 * T]; KE = BIGB[:, bo + 2 * T:bo + 3 * T]
        KTOK = BIGB[0:64, bo + 3 * T:bo + 5 * T].rearrange("p (n c) -> p n c", c=128)
        VH = BIGB[0:64, bo + 5 * T:bo + 7 * T].rearrange("p (n c) -> p n c", c=128)
        QH = BIGB[:, bo + 7 * T:bo + 8 * T]
        QI = QH
        assert bo + 8 * T <= 2 * BIGN
        hfree = None
        CPR = min(NCH, 16)
        for h in range(H):
            S.wait(SP, hfree)
            S.dma(SP, QA, d_qa[h], "hg")
            S.dma(SP, KK, d_kk[h], "hg")
            S.dma(SP, LF, d_lf[h], "hg")
            ev = S.dma_split(SP, VH, d_vhg.rearrange("(n p) c -> p n c", p=64)[:, :, h * 128:(h + 1) * 128], "hg", 4)
            S.wait(DVE, ev); S.wait(ACT, ev); S.wait(PE, ev)
            ev = scan_T(X1, c_onesf, LF)
            S.wait(ACT, ev)
            S.op(ACT, lambda e: e.activation(out=X1, in_=X1, func=AF.Exp))
            ev = S.sig(ACT)
            S.wait(DVE, ev)
            S.op(DVE, lambda e: e.tensor_mul(out=QH, in0=QA, in1=X1))
            S.op(DVE, lambda e: e.tensor_copy(out=DTOT[:, 0:1], in_=X1[:, T - 1:T]))
            ev = S.sig(DVE)
            S.wait(SP, ev)
            S.dma(SP, d_qh[h], QH, "hgo")
            ev_qh = S.dma(SP, d_stloc[h * 128:(h + 1) * 128, 128:132], DTOT[:, 0:4], "hgo")
            S.wait(DVE, ev)
            ev = scan_T(X1, c_scan, LF)
            S.wait(ACT, ev)
            S.op(ACT, lambda e: e.activation(out=X2, in_=X1, func=AF.Exp))
            ev_x2 = S.sig(ACT)
            S.wait(DVE, ev_x2)
            S.op(DVE, lambda e: e.tensor_copy(out=RC[:, 0:NCH], in_=X1.rearrange("p (n c) -> p n c", c=64)[:, :, 31]))
            S.wait(DVE, S.sig(DVE))
            for c in range(NCH):
                S.op(DVE, lambda e, c=c: e.tensor_scalar(out=X1[:, c * 64:(c + 1) * 64], in0=X1[:, c * 64:(c + 1) * 64],
                                                         scalar1=RC[:, c:c + 1], scalar2=None, op0=ALU.subtract))
            ev = S.sig(DVE)
            S.wait(ACT, ev)
            S.op(ACT, lambda e: e.activation(out=X3, in_=X1, func=AF.Exp))
            S.wait(ACT, S.sig(ACT))
            S.op(ACT, lambda e: e.activation(out=X1, in_=X1, func=AF.Exp, scale=-1.0))
            ev = S.sig(ACT)
            S.wait(DVE, ev, ev_qh)
            S.op(DVE, lambda e: e.tensor_mul(out=QTl, in0=QA, in1=X3))
            S.op(DVE, lambda e: e.tensor_mul(out=KTl, in0=KK, in1=X1))
            S.op(DVE, lambda e: e.tensor_mul(out=QI, in0=QA, in1=X2))
            S.wait(DVE, S.sig(DVE))
            for c in range(NCH):
                S.op(DVE, lambda e, c=c: e.tensor_scalar(out=KE[:, c * 64:(c + 1) * 64], in0=KTl[:, c * 64:(c + 1) * 64],
                                                         scalar1=X3[:, c * 64 + 63:c * 64 + 64], scalar2=None, op0=ALU.mult))
            S.op(DVE, lambda e: e.memset(S32[:, :], 0.0))
            S.op(DVE, lambda e: e.memset(Sb[:, :], 0.0))
            ev_prep = S.sig(DVE)
            S.wait(PE, ev_prep)
            ev_ktok = None
            for rnd in range(NCH // CPR):
                S.wait(PE, ev_ktok)
                for cc in range(CPR):
                    c = rnd * CPR + cc
                    S.op(PE, lambda e, c=c, cc=cc: e.transpose(out=PST[0:64, cc * 128:(cc + 1) * 128], in_=KE[:, c * 64:(c + 1) * 64], identity=c_id[:, :]))
                ev = S.sig(PE)
                S.wait(ACT, ev)
                S.op(ACT, lambda e, rnd=rnd: e.activation(out=KTOK[:, rnd * CPR:(rnd + 1) * CPR, :],
                                                          in_=PST[0:64, 0:CPR * 128].rearrange("p (n c) -> p n c", c=128), func=AF.Copy))
                ev_ktok = S.sig(ACT)
            S.wait(PE, ev_ktok)
            sc_free = [None, None]
            sct_free = [None, None]
            scs = [STB[0], STB[1]]
            u_free = [None, None]
            st_ev = ev_prep
            ev_s = ev_prep
            NBLK = T // 128
            for tb in range(NBLK):
                k2 = tb % 2
                pscore = psb[2][0:64, k2 * 128:(k2 + 1) * 128]
                S.wait(PE, sc_free[k2])
                for half in range(2):
                    b0 = tb * 128 + half * 64
                    pc = pscore[:, half * 64:(half + 1) * 64]
                    S.op(PE, lambda e, b0=b0, pc=pc: e.matmul(pc[0:32, 0:32], lhsT=KTl[:, b0:b0 + 32], rhs=QTl[:, b0:b0 + 32], start=True, stop=True))
                    S.op(PE, lambda e, b0=b0, pc=pc: e.matmul(pc[0:64, 32:64], lhsT=KTl[:, b0:b0 + 64], rhs=QTl[:, b0 + 32:b0 + 64], start=True, stop=True))
                ev = S.sig(PE)
                S.wait(DVE, ev)
                sct = scs[k2][0:64, 0:128]
                S.wait(DVE, sct_free[k2])
                S.op(DVE, lambda e, sct=sct, pscore=pscore: e.tensor_mul(out=sct, in0=pscore, in1=c_mhg[0:64, :]))
                ev = S.sig(DVE)
                S.wait(PE, ev)
                pot = psb[(tb // 4) % 2][:, (tb % 4) * 128:(tb % 4) * 128 + 128]
                if tb % 4 == 0:
                    S.wait(PE, psfree[(tb // 4) % 2])
                for half in range(2):
                    c = tb * 2 + half
                    cs = slice(c * 64, (c + 1) * 64)
                    hs = slice(half * 64, (half + 1) * 64)
                    S.op(PE, lambda e, c=c, sct=sct, pot=pot, hs=hs: e.matmul(pot[:, hs], lhsT=VH[:, c, :], rhs=sct[:, hs], start=True, stop=False))
                    if half == 1:
                        sct_free[k2] = S.sig(PE)
                    S.wait(PE, st_ev)
                    S.op(PE, lambda e, cs=cs, pot=pot, hs=hs: e.matmul(pot[:, hs], lhsT=Sb[:, :], rhs=QI[:, cs], start=False, stop=True))
                    ev_o = S.sig(PE)
                    pu = psb[3][:, (c % 2) * 128:(c % 2) * 128 + 128]
                    S.wait(PE, u_free[c % 2])
                    S.op(PE, lambda e, c=c, pu=pu: e.matmul(pu, lhsT=KTOK[:, c, :], rhs=VH[:, c, :], start=True, stop=True))
                    ev_u = S.sig(PE)
                    S.wait(DVE, ev_u, ev_o)
                    S.op(DVE, lambda e, c=c, pu=pu: e.scalar_tensor_tensor(out=S32[:, :], in0=S32[:, :], scalar=X2[:, c * 64 + 63:c * 64 + 64],
                                                                           in1=pu, op0=ALU.mult, op1=ALU.add))
                    ev_s = S.sig(DVE)
                    u_free[c % 2] = ev_s
                    S.wait(ACT, ev_s)
                    S.op(ACT, lambda e: e.activation(out=Sb[:, :], in_=S32[:, :], func=AF.Copy))
                    st_ev = S.sig(ACT)
                sc_free[k2] = ev_o
                if tb % 4 == 3 or tb == NBLK - 1:
                    nb_ = tb % 4 + 1
                    t0 = (tb - nb_ + 1) * 128
                    pbank = psb[(tb // 4) % 2][:, 0:nb_ * 128]
                    S.wait(ACT, ev_o)
                    S.op(ACT, lambda e, t0=t0, nb_=nb_, pbank=pbank: e.activation(out=OT[:, t0:t0 + nb_ * 128], in_=pbank, func=AF.Copy))
                    psfree[(tb // 4) % 2] = S.sig(ACT)
            ev = S.sig(ACT)
            S.wait(SP, ev, st_ev, ev_s)
            S.dma(SP, d_oloc[h], OT, "hgo")
            ev = S.dma(SP, d_stloc[h * 128:(h + 1) * 128, 0:128], S32[:, :], "hgo")
            hfree = [ev, S.sig(PE)]
        S.wait(POOL, hfree)
        ev_stall = S.coll("AllGather", GR, d_stloc, d_stall, "cc")
        S.barrier()

        if "C1" in dbg and l == 0:
            break
        O2 = BIG[:, 0:T]; QH2 = BIGB[:, 2 * T:3 * T]; GS2 = BIGB[:, 3 * T:4 * T]
        c2free = None
        for h in range(H):
            S.wait(SP, c2free)
            S.dma(SP, O2, d_oloc[h], "c2")
            S.dma(SP, QH2, d_qh[h], "c2")
            S.dma(SP, GS2, d_gs[h], "c2")
            ev = S.dma(SP, STALL[:, :, :], d_stall.rearrange("(g hh p) c -> hh p g c", g=G, hh=H)[h], "c2")
            S.wait(DVE, ev); S.wait(PE, ev); S.wait(ACT, ev)
            S.op(DVE, lambda e: e.memset(ACC[:, :], 0.0))
            S.op(DVE, lambda e: e.memset(SIN[:, :], 0.0))
            for i in range(G):
                S.wait(DVE, S.sig(DVE))
                S.op(DVE, lambda e, i=i: e.scalar_tensor_tensor(out=ACC[:, :], in0=ACC[:, :], scalar=STALL[:, i, 128:129], in1=STALL[:, i, 0:128],
                                                                op0=ALU.mult, op1=ALU.add))
                S.wait(DVE, S.sig(DVE))
                S.op(DVE, lambda e, i=i: e.scalar_tensor_tensor(out=SIN[:, :], in0=ACC[:, :], scalar=c_flags[:, 12 + i:13 + i], in1=SIN[:, :],
                                                                op0=ALU.mult, op1=ALU.add))
            S.wait(DVE, S.sig(DVE))
            S.op(DVE, lambda e: e.tensor_copy(out=Sb[:, :], in_=SIN[:, :]))
            ev = S.sig(DVE)
            S.wait(PE, ev)
            for tt in range(NT):
                ts = slice(tt * TT, (tt + 1) * TT)
                pi = get_ps()
                S.op(PE, lambda e, pi=pi, ts=ts: e.matmul(psb[pi][:, 0:TT], lhsT=Sb[:, :], rhs=QH2[:, ts], start=True, stop=True))
                evp = S.sig(PE)
                S.wait(DVE, evp)
                S.op(DVE, lambda e, pi=pi, ts=ts: e.tensor_add(out=O2[:, ts], in0=psb[pi][:, 0:TT], in1=O2[:, ts]))
                ev = S.sig(DVE)
                psfree[pi] = ev
                sq = get_stb(ACT)
                S.wait(ACT, ev)
                S.op(ACT, lambda e, sq=sq, ts=ts: e.activation(out=STB[sq][:, 0:TT], in_=O2[:, ts], func=AF.Square))
                ev0 = S.sig(ACT)
                S.wait(PE, ev0, psfree[4])
                S.op(PE, lambda e, sq=sq: e.matmul(psb[4][:, 0:TT], lhsT=c_ones[:, :], rhs=STB[sq][:, 0:TT], start=True, stop=True))
                ev1 = S.sig(PE)
                stb_free[sq] = ev1
                rs = RSTD[tt % 2]
                evr = rstd_from(psb[4][:, 0:TT], rs[:, 0:TT], 128.0, TT, ev1)
                psfree[4] = evr
                S.wait(DVE, evr)
                S.op(DVE, lambda e, ts=ts, rs=rs: e.tensor_mul(out=O2[:, ts], in0=O2[:, ts], in1=rs[:, 0:TT]))
                S.wait(DVE, S.sig(DVE))
                S.op(DVE, lambda e, ts=ts: e.tensor_mul(out=O2[:, ts], in0=O2[:, ts], in1=GS2[:, ts]))
                ev = S.sig(DVE)
                S.wait(ACT, ev)
                S.op(ACT, lambda e, ts=ts, h=h, l=l: e.activation(out=HT[:, h, ts], in_=O2[:, ts], func=AF.Copy,
                                                                  scale=p_hgog[:, l * H + h:l * H + h + 1]))
            ev = S.sig(ACT)
            c2free = [ev, S.sig(PE)]
        S.barrier()
        if "C" in dbg and l == 0:
            break

        NSB = KT // 128
        KTs = BIGB[:, 0:G * T].rearrange("p (g t) -> p g t", g=G)
        Vs = BIGB[:, G * T:2 * G * T].rearrange("p (n c) -> p n c", c=128)
        Qs = BIGB[:, 2 * G * T:2 * G * T + T]
        f0 = (2 * G * T + T + 1) // 2

        def ftile(k):
            return BIG[:, f0 + k * (KT + 4):f0 + k * (KT + 4) + KT + 4]
        assert f0 + 6 * (KT + 4) <= BIGN, (f0, KT, BIGN)
        E_ = [ftile(0), ftile(1)]; SP_ = [ftile(2), ftile(3)]; C_ = [ftile(4), ftile(5)]
        S.op(DVE, lambda e: e.memset(C_[0][:, 0:1], 0.0))
        S.op(DVE, lambda e: e.memset(C_[1][:, 0:1], 0.0))
        ev_c0 = S.sig(DVE)
        S.wait(DVE, ev_c0)
        hd_free = None
        e_free = [None, None]; sp_free = [None, None]; c_free = [None, None]; a_free = [None, None]; at_free = [None, None]
        z_free = [None, None]; pt_free = [None, None]; po_free = [None, None]; on_free = [None]
        nbk = [0]
        pos = [PS[2][:, 512:640], PS[2][:, 640:768]]
        tiles = [(o, kt) for o in reversed(range(G)) for kt in reversed(range(NKT))]
        ntl = len(tiles)
        for h in range(H):
            S.wait(SP, hd_free, ev_kall, ev_vall)
            S.dma_split(SP, KTs, d_kall[h].rearrange("(g d) t -> d g t", g=G), "at", G)
            S.dma_split(SP, Vs, d_vall[h].rearrange("(n p) c -> p n c", p=128), "at", 8)
            ev_ld = S.dma(SP, Qs, d_qn[h], "at")
            S.wait(PE, ev_ld)
            jobs = [(qb, ti) for qb in range(NQB) for ti in range(ntl)]
            info = {}
            ncar_box = [None]

            def stageA1(jn, h=h):
                qb, ti = jobs[jn]
                o, kt = tiles[ti]
                qs = slice(qb * 128, (qb + 1) * 128)
                m = jn % 2
                pz = PS[m][:, 0:KT]
                S.wait(PE, z_free[m])
                w = min(512, KT)
                for c5 in range(KT // w):
                    S.op(PE, lambda e, c5=c5: e.matmul(pz[:, c5 * w:(c5 + 1) * w], lhsT=Qs[:, qs],
                                                       rhs=KTs[:, o, kt * KT + c5 * w:kt * KT + (c5 + 1) * w], start=True, stop=True))
                ev_z = S.sig(PE)
                Et, SPt = E_[m], SP_[m]
                S.wait(ACT, ev_z, e_free[m])
                S.op(ACT, lambda e: e.activation(out=Et[:, 0:KT], in_=pz, func=AF.Exp))
                ev_e = S.sig(ACT)
                kb0 = kt * NSB
                nbef = min(max(qb - kb0, 0), NSB)
                hasd = (kb0 <= qb < kb0 + NSB)
                naft = NSB - nbef - (1 if hasd else 0)
                rb = (0, nbef * 128)
                rd = (nbef * 128, nbef * 128 + 128) if hasd else None
                ra = ((nbef + (1 if hasd else 0)) * 128, KT)
                ev_em = None
                if rd is not None:
                    S.wait(DVE, ev_e)
                    S.op(DVE, lambda e: e.tensor_mul(out=Et[:, rd[0]:rd[1]], in0=Et[:, rd[0]:rd[1]], in1=MD[:, o, :]))
                    ev_em = S.sig(DVE)
                S.wait(ACT, ev_e, sp_free[m])
                if nbef > 0:
                    S.op(ACT, lambda e: e.activation(out=SPt[:, rb[0]:rb[1]], in_=Et[:, rb[0]:rb[1]], func=AF.Ln,
                                                     scale=c_flags[:, o:o + 1], bias=c_onesf[:, 0:1]))
                if naft > 0:
                    S.op(ACT, lambda e: e.activation(out=SPt[:, ra[0]:ra[1]], in_=Et[:, ra[0]:ra[1]], func=AF.Ln,
                                                     scale=c_flags[:, 4 + o:5 + o], bias=c_onesf[:, 0:1]))
                if rd is not None:
                    S.wait(ACT, ev_em)
                    S.op(ACT, lambda e: e.activation(out=SPt[:, rd[0]:rd[1]], in_=Et[:, rd[0]:rd[1]], func=AF.Ln,
                                                     scale=1.0, bias=c_onesf[:, 0:1]))
                ev_sp = S.sig(ACT)
                info[jn] = dict(m=m, o=o, kt=kt, qb=qb, ti=ti, rb=rb, rd=rd, ra=ra, nbef=nbef, naft=naft, ev_sp=ev_sp, pz=pz)

            def stageA2(jn):
                d = info[jn]
                m, o, pz = d["m"], d["o"], d["pz"]
                SPt, Ct = SP_[m], C_[m]
                S.wait(DVE, d["ev_sp"], c_free[m])
                S.op(DVE, lambda e: e.tensor_tensor_scan(out=Ct[:, 1:KT + 1], data0=c_onesf[:, 0:KT], data1=SPt[:, 0:KT],
                                                         initial=0.0, op0=ALU.mult, op1=ALU.add))
                ev_c = S.sig(DVE)
                S.wait(DVE, ev_c)
                k3 = (nbk[0] % 16) * 3
                nbk[0] += 1
                nbc, nb1, nb2 = NB[:, k3:k3 + 1], NB[:, k3 + 1:k3 + 2], NB[:, k3 + 2:k3 + 3]
                ncar = ncar_box[0] if d["ti"] > 0 else None
                S.op(DVE, lambda e: e.tensor_add(out=SPt[:, 0:KT], in0=pz, in1=Ct[:, 0:KT]))
                ev_u = S.sig(DVE)
                if ncar is None:
                    S.op(DVE, lambda e: e.tensor_scalar(out=nbc, in0=Ct[:, KT:KT + 1], scalar1=-1.0, scalar2=None, op0=ALU.mult))
                else:
                    S.op(DVE, lambda e: e.tensor_sub(out=nbc, in0=ncar, in1=Ct[:, KT:KT + 1]))
                S.wait(DVE, S.sig(DVE))
                S.op(DVE, lambda e: e.tensor_add(out=nb1, in0=nbc, in1=c_flags[:, 16 + o:17 + o]))
                S.op(DVE, lambda e: e.tensor_add(out=nb2, in0=nbc, in1=c_flags[:, 20 + o:21 + o]))
                ev_nb = S.sig(DVE)
                ncar_box[0] = nbc
                z_free[m] = ev_u
                c_free[m] = ev_nb
                d.update(nbc=nbc, nb1=nb1, nb2=nb2, ev_u=[ev_u, ev_nb])

            def stageB(jn, h=h):
                d = info.pop(jn)
                m, o, kt, rb, rd, ra, qb, ti = d["m"], d["o"], d["kt"], d["rb"], d["rd"], d["ra"], d["qb"], d["ti"]
                nbc, nb1, nb2 = d["nbc"], d["nb1"], d["nb2"]
                Et, SPt, At, ATt = E_[m], SP_[m], As[m], ATs[m]
                po = pos[qb % 2]
                S.wait(ACT, d["ev_u"], a_free[m])
                if d["nbef"] > 0:
                    S.op(ACT, lambda e: e.activation(out=At[:, rb[0]:rb[1]], in_=SPt[:, rb[0]:rb[1]], func=AF.Exp, bias=nb1))
                if d["naft"] > 0:
                    S.op(ACT, lambda e: e.activation(out=At[:, ra[0]:ra[1]], in_=SPt[:, ra[0]:ra[1]], func=AF.Exp, bias=nb2))
                if rd is not None:
                    S.op(ACT, lambda e: e.activation(out=Et[:, rd[0]:rd[1]], in_=SPt[:, rd[0]:rd[1]], func=AF.Exp, bias=nbc))
                ev_a = S.sig(ACT)
                evs_a = [ev_a]
                if rd is not None:
                    S.wait(DVE, ev_a)
                    S.op(DVE, lambda e: e.tensor_mul(out=At[:, rd[0]:rd[1]], in0=Et[:, rd[0]:rd[1]], in1=MD[:, o, :]))
                    evs_a.append(S.sig(DVE))
                e_free[m] = [d["ev_sp"]] + evs_a
                sp_free[m] = evs_a
                pT = PST[:, m * 1024:m * 1024 + KT]
                S.wait(PE, evs_a, pt_free[m])
                for sbk in range(NSB):
                    S.op(PE, lambda e, sbk=sbk: e.transpose(out=pT[:, sbk * 128:(sbk + 1) * 128], in_=At[:, sbk * 128:(sbk + 1) * 128],
                                                            identity=c_id[:, :]))
                ev_t = S.sig(PE)
                a_free[m] = ev_t
                S.wait(ACT, ev_t, at_free[m])
                S.op(ACT, lambda e: e.activation(out=ATt[:, 0:KT], in_=pT, func=AF.Copy))
                ev_at = S.sig(ACT)
                pt_free[m] = ev_at
                S.wait(PE, ev_at)
                if ti == 0:
                    S.wait(PE, po_free[qb % 2])
                for sbk in range(NSB):
                    blk = (o * T + kt * KT) // 128 + sbk
                    S.op(PE, lambda e, sbk=sbk, blk=blk, first=(ti == 0 and sbk == 0), last=(ti == ntl - 1 and sbk == NSB - 1):
                         e.matmul(po, lhsT=ATt[:, sbk * 128:(sbk + 1) * 128], rhs=Vs[:, blk, :], start=first, stop=last))
                at_free[m] = S.sig(PE)
                if ti == ntl - 1:
                    epilogue(qb, at_free[m], h)

            def epilogue(qb, ev_po, h, l=l):
                qs = slice(qb * 128, (qb + 1) * 128)
                po = pos[qb % 2]
                k3 = (nbk[0] % 16) * 3
                nbk[0] += 1
                ssq, rsd = NB[:, k3:k3 + 1], NB[:, k3 + 1:k3 + 2]
                S.op(DVE, lambda e: e.memset(ssq, 0.0))
                S.wait(ACT, S.sig(DVE), ev_po, on_free[0])
                S.op(ACT, lambda e: e.activation(out=ONb[:, :], in_=po, func=AF.Square, accum_out=ssq))
                S.wait(ACT, S.sig(ACT))
                S.op(ACT, lambda e: e.activation(out=rsd, in_=ssq, func=AF.Ln, scale=1.0 / 128.0, bias=c_eps[:, 0:1]))
                S.wait(ACT, S.sig(ACT))
                S.op(ACT, lambda e: e.activation(out=rsd, in_=rsd, func=AF.Exp, scale=-0.5))
                S.wait(ACT, S.sig(ACT))
                S.op(ACT, lambda e: e.activation(out=ONb[:, :], in_=po, func=AF.Copy, scale=rsd))
                ev_on = S.sig(ACT)
                po_free[qb % 2] = ev_on
                pT = PST[:, 0:128]
                S.wait(PE, ev_on, pt_free[0])
                S.op(PE, lambda e: e.transpose(out=pT, in_=ONb[:, :], identity=c_id[:, :]))
                ev_tr = S.sig(PE)
                on_free[0] = ev_tr
                S.wait(ACT, ev_tr)
                S.op(ACT, lambda e: e.activation(out=HT[:, H + h, qs], in_=pT, func=AF.Copy,
                                                 scale=p_sbog[:, l * H + h:l * H + h + 1]))
                pt_free[0] = S.sig(ACT)

            nj = len(jobs)
            for step in range(nj + 2):
                if step >= 2:
                    stageB(step - 2)
                if 1 <= step <= nj:
                    stageA2(step - 1)
                if step < nj:
                    stageA1(step)
            hd_free = [S.sig(PE), S.sig(ACT)]
        S.barrier()
        if "D" in dbg and l == 0:
            break

        def evac_res(gcol0, l=l):
            def f(blk, tt, pss, evp):
                ts = slice(tt * TT, (tt + 1) * TT)
                si = get_stg(SP)
                ev = S.dma(SP, STG[si][:, 0:TT], xs[:, blk, ts], f"sg{si}")
                S.wait(DVE, ev, evp)
                S.op(DVE, lambda e: e.scalar_tensor_tensor(out=STG[si][:, 0:TT], in0=pss, scalar=modl[:, l, gcol0 + blk:gcol0 + blk + 1],
                                                           in1=STG[si][:, 0:TT], op0=ALU.mult, op1=ALU.add))
                ev = S.sig(DVE)
                S.wait(SP, ev)
                stg_free[si] = S.dma(SP, xs[:, blk, ts], STG[si][:, 0:TT], f"sg{si}")
                return ev
            return f
        gemm_fm(lambda blk, l=l: wout[l, blk], KD, KD, lambda kc, tt: HT[:, kc, tt * TT:(tt + 1) * TT], NT, TT,
                evac_res(2 * KD), KD * 128)
        S.barrier()
        if "E" in dbg and l == 0:
            break

        evn = norm_to_HT(l, 1, p_n2g)
        S.wait(PE, evn)
        gate_sb = {}

        def evac_ffn_in(blk, tt, pss, evp, l=l):
            j, which = blk // 2, blk % 2
            ts = slice(tt * TT, (tt + 1) * TT)
            if which == 0:
                si = get_stg(ACT)
                S.wait(ACT, evp)
                S.op(ACT, lambda e: e.activation(out=STG[si][:, 0:TT], in_=pss, func=AF.Silu))
                ev = S.sig(ACT)
                gate_sb[(j, tt)] = (si, ev)
                return ev
            si, evg = gate_sb.pop((j, tt))
            so = get_stb(DVE)
            S.wait(DVE, evg, evp)
            S.op(DVE, lambda e: e.tensor_mul(out=STB[so][:, 0:TT], in0=pss, in1=STG[si][:, 0:TT]))
            ev = S.sig(DVE)
            stg_free[si] = ev
            S.wait(SP, ev)
            stb_free[so] = S.dma(SP, d_act[j, :, ts], STB[so][:, 0:TT], f"sb{so}")
            return ev

        gemm_fm(lambda blk, l=l: wfi[l, blk // 2, :, (blk % 2) * KD * 128:(blk % 2 + 1) * KD * 128], 2 * NF, KD,
                lambda kc, tt: HT[:, kc, tt * TT:(tt + 1) * TT], NT, TT, evac_ffn_in, KD * 128)
        S.barrier()
        ACTT = BIGB[:, 0:NF * TT].rearrange("p (k t) -> p k t", k=NF)
        assert NF * TT <= 2 * BIGN
        act_free = None
        for tt in range(NT):
            ts = slice(tt * TT, (tt + 1) * TT)
            S.wait(SP, act_free)
            ev_act = S.dma_split(SP, ACTT, d_act[:, :, ts].rearrange("k p t -> p k t"), "actt", 11)
            S.wait(PE, ev_act)
            er = evac_res(5 * KD)
            for blk in range(KD):
                i, ev = wload(wfo[l, blk], NF * 128)
                S.wait(PE, ev)
                pi = get_ps()
                for kc in range(NF):
                    S.op(PE, lambda e, i=i, kc=kc, pi=pi: e.matmul(psb[pi][:, 0:TT], lhsT=WR[i][:, kc * 128:(kc + 1) * 128], rhs=ACTT[:, kc, :],
                                                                   start=(kc == 0), stop=(kc == NF - 1)))
                evp = S.sig(PE)
                wfree[i] = evp
                psfree[pi] = er(blk, tt, psb[pi][:, 0:TT], evp)
            act_free = S.sig(PE)
        S.barrier()
    scr = dict(d_qa=d_qa, d_kk=d_kk, d_lf=d_lf, d_gs=d_gs, d_vhg=d_vhg, d_qn=d_qn, d_stloc=d_stloc, d_stall=d_stall, d_oloc=d_oloc, d_qh=d_qh, d_act=d_act,
               d_modall=d_modall)
    if "HT" in dbg:
        o = nc.dram_tensor("dbg_HT", [128, KD, T], BF16, kind="ExternalOutput").ap()
        S.dma(SP, o, HT[:, :, :], "dbgout")
    for name in dbg:
        if name in scr:
            src = scr[name]
            o = nc.dram_tensor("dbg_" + name, list(src.shape), src.dtype, kind="ExternalOutput").ap()
            S.dma(SP, o, src, "dbgout")
    ev = S.dma_split(SP, yT[:, :, :], xs[:, :, :], "out", KD)
    S.wait(SP, ev)
    S.barrier()
    with nc.Block() as block:
        S.emit(block)
    stack.close()
    return nc, S


def _bf(a):
    return a


def prep_inputs(cfg, inp):
    D, T, L, KD, H, NF, NMB = cfg.D, cfg.T, cfg.L, cfg.KD, cfg.H, cfg.NF, cfg.NMB
    HW = H * 128
    f = lambda a: np.ascontiguousarray(a, dtype=np.float32)
    x = np.asarray(inp["x"], np.float32)
    c = np.asarray(inp["c"], np.float32)

    def colvec(a, n):
        a = np.asarray(a, np.float32).reshape(L, n, 128)
        return f(a.transpose(2, 0, 1).reshape(128, L * n))

    w_in = np.asarray(inp["w_in"], np.float32)
    segs = [0, 3, 1, 4, 5]
    fm = np.concatenate([w_in[:, :, s * HW:(s + 1) * HW] for s in segs], axis=2)
    fm = fm.reshape(L, KD, 128, 5 * H, 128).transpose(0, 3, 2, 1, 4)
    win_fm = f(fm.reshape(L, 5 * H, 128, KD * 128))
    tm = np.stack([w_in[:, :, 2 * HW:3 * HW], w_in[:, :, 6 * HW:7 * HW]], axis=1)
    tm = tm.reshape(L, 2, KD, 128, HW).transpose(0, 1, 3, 2, 4)
    win_tm = f(tm.reshape(L, 2, 128, KD * HW))
    wo = np.asarray(inp["w_out"], np.float32).reshape(L, KD, 128, KD, 128).transpose(0, 3, 2, 1, 4)
    wout = f(wo.reshape(L, KD, 128, KD * 128))
    wi = np.asarray(inp["w_ffn_in"], np.float32).reshape(L, KD, 128, 2, NF, 128).transpose(0, 4, 2, 3, 1, 5)
    wfi = f(wi.reshape(L, NF, 128, 2 * KD * 128))
    wf = np.asarray(inp["w_ffn_out"], np.float32).reshape(L, NF, 128, KD, 128).transpose(0, 3, 2, 1, 4)
    wfo = f(wf.reshape(L, KD, 128, NF * 128))
    wa = np.asarray(inp["w_ada"], np.float32).reshape(L, KD, 128, 6 * KD, 128).transpose(0, 3, 2, 1, 4)
    ba = np.asarray(inp["b_ada"], np.float32).reshape(L, 6 * KD, 128)
    cTt = f(c.T.reshape(KD, 128, 2).transpose(1, 0, 2))
    common = dict(
        cT=cTt, n1g=colvec(inp["norm1_g"], KD), n2g=colvec(inp["norm2_g"], KD),
        hgog=colvec(inp["hg_out_g"], H), sbog=colvec(inp["sb_out_g"], H),
        sbqg=colvec(inp["sb_q_g"], 1), sbkg=colvec(inp["sb_k_g"], 1), lbl=colvec(inp["hg_lb_logits"], H),
        win_fm=win_fm, win_tm=win_tm, wout=wout, wfi=wfi, wfo=wfo,
    )
    s_idx = np.arange(128)
    mhg = ((s_idx[:, None] <= (s_idx[None, :] % 64)) & (s_idx[:, None] < 64)).astype(np.float32)
    msb = (s_idx[None, :] < s_idx[:, None]).astype(np.float32)
    ident = np.eye(128, dtype=np.float32)
    scan = np.ones((128, min(T, 1024)), np.float32)
    scan[:, ::64] = 0.0
    maps = []
    for r in range(NCORES):
        b, j = r // G, r % G
        xt = x[b, j * T:(j + 1) * T, :].T.reshape(KD, 128, T).transpose(1, 0, 2)
        fl = np.zeros((128, 32), np.float32)
        for o in range(G):
            fl[:, o] = 1.0 if o <= j else 0.0
            fl[:, 4 + o] = 1.0 if o < j else 0.0
            fl[:, 8 + o] = 1.0 if o == j else 0.0
            fl[:, 12 + o] = 1.0 if o == j - 1 else 0.0
            fl[:, 16 + o] = 0.0 if o <= j else BIGNEG
            fl[:, 20 + o] = 0.0 if o < j else BIGNEG
        fl[:, 24] = 1.0 if b == 0 else 0.0
        fl[:, 25] = 1.0 if b == 1 else 0.0
        wad = f(wa[:, j * NMB:(j + 1) * NMB].reshape(L, NMB, 128, KD * 128))
        bad = ba[:, j * NMB:(j + 1) * NMB, :].transpose(2, 0, 1)
        bad = f(np.repeat(bad[:, :, :, None], 2, axis=3).reshape(128, L * NMB * 2))
        m = dict(common)
        m.update(xT=f(xt), wada=wad, bada=bad, flags=fl, cmask_hg=mhg, cmask_sb=msb, cident=ident, cscan=scan)
        maps.append(m)
    return maps


def assemble(cfg, results):
    D, T, KD = cfg.D, cfg.T, cfg.KD
    out = np.zeros((2, cfg.S, D), np.float32)
    for r in range(NCORES):
        b, j = r // G, r % G
        y = np.asarray(results[r]["yT"], np.float32)
        out[b, j * T:(j + 1) * T, :] = y.transpose(1, 0, 2).reshape(D, T).T
    return out


_CACHE = {}


def kernel(**inputs):
    x = inputs["x"]
    B, Sq, D = x.shape
    L = inputs["w_in"].shape[0]
    DFF = inputs["w_ffn_out"].shape[1]
    cfg = Cfg(D, Sq, L, DFF)
    key = (D, Sq, L, DFF)
    if key not in _CACHE:
        _CACHE[key] = build(cfg)[0]
    nc = _CACHE[key]
    maps = prep_inputs(cfg, inputs)
    res = run_bass_kernel_spmd(nc, maps, core_ids=list(range(NCORES)))
    return assemble(cfg, res.results)
```

```python
import contextlib
import numpy as np
import concourse.bass as bass
import concourse.mybir as mybir
from concourse.bass_utils import run_bass_kernel_spmd

F32 = mybir.dt.float32
BF16 = mybir.dt.bfloat16
AF = mybir.ActivationFunctionType
ALU = mybir.AluOpType
AX = mybir.AxisListType

EPS = 1e-6
TINY = 1e-30
BIGNEG = -30000.0
NCORES = 8
G = 4


class Cfg:
    def __init__(self, D, S, L, DFF):
        self.D, self.S, self.L, self.DFF = D, S, L, DFF
        self.T = S // G
        self.KD = D // 128
        self.H = D // 2 // 128
        self.NF = DFF // 128
        self.TT = min(512, self.T)
        self.NT = self.T // self.TT
        self.NMB = 6 * self.KD // G
        self.NCH = self.T // 64
        self.NQB = self.T // 128
        self.KT = min(1024, self.T)
        self.NKT = self.T // self.KT


class Sched:
    ENG = ("pe", "act", "dve", "pool", "sp")

    def __init__(self, nc, stack):
        self.nc = nc
        self.stack = stack
        self.ops = {e: [] for e in self.ENG}
        self.last = {e: None for e in self.ENG}
        self.cnt = {}
        self.sems = {}
        self.seen = {e: {} for e in self.ENG}
        self.pending = {}
        for e in self.ENG:
            self._mksem(e)

    def _mksem(self, key):
        self.sems[key] = self.stack.enter_context(self.nc.semaphore("s_" + key))
        self.cnt[key] = 0

    def op(self, eng, fn):
        rec = {"fn": fn, "sem": None, "val": None, "inc": None}
        self.ops[eng].append(rec)
        self.last[eng] = rec
        return rec

    def sig(self, eng):
        rec = self.last[eng]
        assert rec is not None
        if rec["sem"] is None:
            self.cnt[eng] += 1
            rec["sem"], rec["val"], rec["inc"] = eng, self.cnt[eng], 1
        assert rec["sem"] == eng or rec["inc"] != 1, "sig on dma op"
        return (rec["sem"], rec["val"])

    def wait(self, eng, *evs):
        for ev in evs:
            if ev is None:
                continue
            if isinstance(ev, list):
                self.wait(eng, *ev)
                continue
            key, val = ev
            if self.seen[eng].get(key, 0) >= val:
                continue
            self.seen[eng][key] = val
            self.ops[eng].append({"wait": key, "val": val})

    def dma(self, q, out, in_, chan):
        if chan not in self.sems:
            self._mksem(chan)
        rec = self.op(q, lambda e: e.dma_start(out=out, in_=in_))
        self.cnt[chan] += 16
        rec["sem"], rec["val"], rec["inc"] = chan, self.cnt[chan], 16
        ev = (chan, self.cnt[chan])
        self.pending[chan] = ev
        return ev

    def dma_split(self, q, out, in_, chan, n):
        d = out.shape[1]
        n = min(n, d)
        step = (d + n - 1) // n
        ev = None
        for a in range(0, d, step):
            b = min(d, a + step)
            ev = self.dma(q, out[:, a:b], in_[:, a:b], chan)
        return ev

    def coll(self, kind, groups, in_, out, chan):
        if chan not in self.sems:
            self._mksem(chan)
        rec = self.op("pool", lambda e: e.collective_compute(
            kind, ALU.bypass, replica_groups=groups, ins=[in_], outs=[out]))
        self.cnt[chan] += 1
        rec["sem"], rec["val"], rec["inc"] = chan, self.cnt[chan], None
        ev = (chan, self.cnt[chan])
        self.pending[chan] = ev
        return ev

    def barrier(self):
        evs = []
        for e in ("pe", "act", "dve", "pool"):
            rec = self.last[e]
            if rec is not None and (rec["sem"] is None or rec["sem"] == e):
                evs.append(self.sig(e))
        evs += list(self.pending.values())
        for e in self.ENG:
            self.wait(e, *evs)
        return evs

    def emit(self, block):
        def mk(eng):
            def body(e):
                for rec in self.ops[eng]:
                    if "wait" in rec:
                        e.wait_ge(self.sems[rec["wait"]], rec["val"])
                    else:
                        ins = rec["fn"](e)
                        if rec["sem"] is not None:
                            if rec["inc"] is None:
                                ins.then_inc(self.sems[rec["sem"]])
                            else:
                                ins.then_inc(self.sems[rec["sem"]], rec["inc"])
            return body
        block.tensor(mk("pe"))
        block.scalar(mk("act"))
        block.vector(mk("dve"))
        block.gpsimd(mk("pool"))
        block.sync(mk("sp"))


def build(cfg, dbg=()):
    D, T, L, KD, H, NF, TT, NT, NMB = cfg.D, cfg.T, cfg.L, cfg.KD, cfg.H, cfg.NF, cfg.TT, cfg.NT, cfg.NMB
    NCH, NQB, KT, NKT = cfg.NCH, cfg.NQB, cfg.KT, cfg.NKT
    HW = H * 128
    nc = bass.Bass("TRN2", target_bir_lowering=False)
    stack = contextlib.ExitStack()

    def din(name, shape, dt=F32):
        return nc.dram_tensor(name, list(shape), dt, kind="ExternalInput").ap()

    def dscr(name, shape, dt=F32):
        return nc.dram_tensor(name, list(shape), dt).ap()

    xT = din("xT", [128, KD, T])
    cT = din("cT", [128, KD, 2])
    wada = din("wada", [L, NMB, 128, KD * 128])
    bada = din("bada", [128, L * NMB * 2])
    n1g = din("n1g", [128, L * KD])
    n2g = din("n2g", [128, L * KD])
    hgog = din("hgog", [128, L * H])
    sbog = din("sbog", [128, L * H])
    sbqg = din("sbqg", [128, L])
    sbkg = din("sbkg", [128, L])
    lbl = din("lbl", [128, L * H])
    win_fm = din("win_fm", [L, 5 * H, 128, KD * 128])
    win_tm = din("win_tm", [L, 2, 128, KD * HW])
    wout = din("wout", [L, KD, 128, KD * 128])
    wfi = din("wfi", [L, NF, 128, 2 * KD * 128])
    wfo = din("wfo", [L, KD, 128, NF * 128])
    flags = din("flags", [128, 32])
    cmask_hg = din("cmask_hg", [128, 128])
    cmask_sb = din("cmask_sb", [128, 128])
    cident = din("cident", [128, 128])
    cscan = din("cscan", [128, min(T, 1024)])
    yT = nc.dram_tensor("yT", [128, KD, T], F32, kind="ExternalOutput").ap()
    dbg_out = {}

    xs = dscr("xs", [128, KD, T])
    d_qa = dscr("d_qa", [H, 128, T])
    d_kk = dscr("d_kk", [H, 128, T])
    d_lf = dscr("d_lf", [H, 128, T])
    d_gs = dscr("d_gs", [H, 128, T], BF16)
    d_vhg = dscr("d_vhg", [T, HW], BF16)
    d_qn = dscr("d_qn", [H, 128, T], BF16)
    d_kloc = [dscr(f"d_kloc{h}", [128, T], BF16) for h in range(H)]
    d_kall = [dscr(f"d_kall{h}", [G * 128, T], BF16) for h in range(H)]
    d_vloc = [dscr(f"d_vloc{h}", [T, 128], BF16) for h in range(H)]
    d_vall = [dscr(f"d_vall{h}", [G * T, 128], BF16) for h in range(H)]
    d_stloc = dscr("d_stloc", [H * 128, 132])
    d_stall = dscr("d_stall", [G * H * 128, 132])
    d_oloc = dscr("d_oloc", [H, 128, T])
    d_qh = dscr("d_qh", [H, 128, T], BF16)
    d_act = dscr("d_act", [NF, 128, T], BF16)
    d_modsrc = dscr("d_modsrc", [128, L * NMB * 2])
    d_modall = dscr("d_modall", [G * 128, L * NMB * 2])

    S = Sched(nc, stack)

    def sb(name, shape, dt=F32):
        return stack.enter_context(nc.sbuf_tensor(name, list(shape), dt))

    def ps(name, shape, dt=F32):
        return stack.enter_context(nc.psum_tensor(name, list(shape), dt))

    HT = sb("HT", [128, KD, T], BF16)
    WRN = max(NF * 128, 2 * KD * 128)
    WR = [sb(f"WR{i}", [128, WRN], BF16) for i in range(2)]
    BIGN = 18432
    BIG = sb("BIG", [128, BIGN], F32)
    BIGB = BIG[:, :].bitcast(BF16)
    NSTG = 4
    STG = [sb(f"STG{i}", [128, 512], F32) for i in range(NSTG)]
    NSTB = 4
    STB = [sb(f"STB{i}", [128, 512], BF16) for i in range(NSTB)]
    RSTD = [sb(f"RSTD{i}", [128, 512], F32) for i in range(2)]
    c_flags = sb("c_flags", [128, 32])
    c_mhg = sb("c_mhg", [128, 128])
    c_msb = sb("c_msb", [128, 128])
    c_idf = sb("c_idf", [128, 128])
    c_id = sb("c_id", [128, 128], BF16)
    c_ones = sb("c_ones", [128, 128], BF16)
    c_scan = sb("c_scan", [128, min(T, 1024)])
    p_n1g = sb("p_n1g", [128, L * KD])
    p_n2g = sb("p_n2g", [128, L * KD])
    p_hgog = sb("p_hgog", [128, L * H])
    p_sbog = sb("p_sbog", [128, L * H])
    p_sbqg = sb("p_sbqg", [128, L])
    p_sbkg = sb("p_sbkg", [128, L])
    p_lbl = sb("p_lbl", [128, L * H])
    p_bada = sb("p_bada", [128, L * NMB * 2])
    p_cT = sb("p_cT", [128, KD * 2])
    p_cond = sb("p_cond", [128, KD * 2], BF16)
    modloc = sb("modloc", [128, L * NMB * 2])
    modall = sb("modall", [128, G, L * NMB * 2])
    modl = sb("modl", [128, L, 6 * KD])
    lb_e = sb("lb_e", [128, L * H])
    lb_s = sb("lb_s", [128, H])
    lb_lb = sb("lb_lb", [128, L * H])
    lb_om = sb("lb_om", [128, L * H])
    lb_nom = sb("lb_nom", [128, L * H])
    lb_tm = sb("lb_tm", [128, L * H])
    colA = sb("colA", [128, 2 * KD])
    colq = sb("colq", [128, 2])
    c_eps = sb("c_eps", [128, 1])
    c_onesf = sb("c_onesf", [128, 1024])
    S32 = sb("S32", [128, 128])
    Sb = sb("Sb", [128, 128], BF16)
    STALL = sb("STALL", [128, G, 132])
    ACC = sb("ACC", [128, 128])
    SIN = sb("SIN", [128, 128])
    DTOT = sb("DTOT", [128, 4])
    RC = sb("RC", [128, 32])
    MD = sb("MD", [128, G, 128])
    NB = sb("NB", [128, 64])
    ATs = [sb(f"AT{i}", [128, 1024], BF16) for i in range(2)]
    As = [sb(f"A{i}", [128, 1024], BF16) for i in range(2)]
    ONb = sb("ONb", [128, 128], BF16)

    PS = [ps(f"PS{i}", [128, 1024]) for i in range(3)]
    PST = ps("PST", [128, 2048], BF16)
    psb = [PS[i // 2][:, (i % 2) * 512:(i % 2) * 512 + 512] for i in range(6)]

    SP, PE, ACT, DVE, POOL = "sp", "pe", "act", "dve", "pool"

    evs = []
    for dst, src in ((c_flags, flags), (c_mhg, cmask_hg), (c_msb, cmask_sb), (c_idf, cident), (c_scan, cscan),
                     (p_n1g, n1g), (p_n2g, n2g), (p_hgog, hgog), (p_sbog, sbog), (p_sbqg, sbqg),
                     (p_sbkg, sbkg), (p_lbl, lbl), (p_bada, bada)):
        evs.append(S.dma(SP, dst[:, :], src[:, :], "ld0"))
    evs.append(S.dma(SP, p_cT[:, :], cT.rearrange("p k b -> p (k b)"), "ld0"))
    ev_x0 = S.dma_split(SP, xs[:, :, :], xT[:, :, :], "ldx", KD)
    ld0 = evs[-1]
    for e in (ACT, DVE, POOL, PE):
        S.wait(e, ld0)
    S.op(DVE, lambda e: e.tensor_copy(out=c_id[:, :], in_=c_idf[:, :]))
    S.op(DVE, lambda e: e.memset(c_ones[:, :], 1.0))
    S.op(DVE, lambda e: e.memset(c_eps[:, :], EPS))
    S.op(DVE, lambda e: e.memset(c_onesf[:, :], 1.0))
    S.op(DVE, lambda e: e.memset(DTOT[:, :], 0.0))
    for o in range(G):
        S.op(DVE, lambda e, o=o: e.tensor_scalar(out=MD[:, o, :], in0=c_msb[:, :], scalar1=c_flags[:, 8 + o:9 + o],
                                                 scalar2=c_flags[:, 4 + o:5 + o], op0=ALU.mult, op1=ALU.add))
    ev_const = S.sig(DVE)
    S.op(ACT, lambda e: e.activation(out=p_cond[:, :], in_=p_cT[:, :], func=AF.Silu))
    ev_cond = S.sig(ACT)
    S.op(ACT, lambda e: e.activation(out=lb_e[:, :], in_=p_lbl[:, :], func=AF.Exp))
    ev = S.sig(ACT)
    S.wait(DVE, ev)
    S.op(DVE, lambda e: e.tensor_copy(out=lb_s[:, :], in_=lb_e[:, 0:H]))
    for l in range(1, L):
        S.wait(DVE, S.sig(DVE))
        S.op(DVE, lambda e, l=l: e.tensor_add(out=lb_s[:, :], in0=lb_s[:, :], in1=lb_e[:, l * H:(l + 1) * H]))
    S.wait(DVE, S.sig(DVE))
    S.op(DVE, lambda e: e.reciprocal(out=lb_s[:, :], in_=lb_s[:, :]))
    S.wait(DVE, S.sig(DVE))
    S.op(DVE, lambda e: e.memset(lb_lb[:, 0:H], 0.0))
    for l in range(1, L):
        S.wait(DVE, S.sig(DVE))
        S.op(DVE, lambda e, l=l: e.tensor_mul(out=lb_e[:, l * H:(l + 1) * H], in0=lb_e[:, l * H:(l + 1) * H], in1=lb_s[:, :]))
        S.wait(DVE, S.sig(DVE))
        S.op(DVE, lambda e, l=l: e.tensor_add(out=lb_lb[:, l * H:(l + 1) * H], in0=lb_lb[:, (l - 1) * H:l * H],
                                              in1=lb_e[:, l * H:(l + 1) * H]))
    S.wait(DVE, S.sig(DVE))
    S.op(DVE, lambda e: e.tensor_scalar(out=lb_om[:, :], in0=lb_lb[:, :], scalar1=-1.0, scalar2=1.0, op0=ALU.mult, op1=ALU.add))
    S.op(DVE, lambda e: e.tensor_scalar(out=lb_tm[:, :], in0=lb_lb[:, :], scalar1=-1.0, scalar2=TINY, op0=ALU.mult, op1=ALU.add))
    S.wait(DVE, S.sig(DVE))
    S.op(DVE, lambda e: e.tensor_scalar(out=lb_nom[:, :], in0=lb_om[:, :], scalar1=-1.0, scalar2=None, op0=ALU.mult))
    ev_lb = S.sig(DVE)

    S.wait(PE, ev_cond)
    wfree = [None, None]
    widx = [0]

    def wload(src_ap, ncols):
        i = widx[0] % 2
        widx[0] += 1
        S.wait(POOL, wfree[i])
        ev = S.dma(POOL, WR[i][:, 0:ncols], src_ap, f"w{i}")
        return i, ev

    pmod = psb[0]
    for l in range(L):
        for jb in range(NMB):
            i, ev = wload(wada[l, jb], KD * 128)
            S.wait(PE, ev)
            col = (l * NMB + jb) * 2
            for kc in range(KD):
                S.op(PE, lambda e, i=i, kc=kc, col=col: e.matmul(
                    pmod[:, col:col + 2], lhsT=WR[i][:, kc * 128:(kc + 1) * 128], rhs=p_cond[:, kc * 2:kc * 2 + 2],
                    start=(kc == 0), stop=(kc == KD - 1)))
            wfree[i] = S.sig(PE)
    S.wait(DVE, S.sig(PE))
    NM = L * NMB * 2
    S.op(DVE, lambda e: e.tensor_add(out=modloc[:, :], in0=pmod[:, 0:NM], in1=p_bada[:, :]))
    ev = S.sig(DVE)
    S.wait(SP, ev)
    ev = S.dma(SP, d_modsrc[:, :], modloc[:, :], "mod")
    S.wait(POOL, ev)
    ev = S.coll("AllGather", [[0, 1, 2, 3], [4, 5, 6, 7]], d_modsrc, d_modall, "cc")
    S.wait(SP, ev)
    ev = S.dma(SP, modall[:, :, :], d_modall.rearrange("(r p) f -> p r f", p=128), "mod")
    S.wait(DVE, ev)
    for r in range(G):
        src = modall[:, r, :].rearrange("p (l j b) -> p l j b", l=L, j=NMB, b=2)
        dst = modl[:, :, r * NMB:(r + 1) * NMB]
        S.op(DVE, lambda e, src=src, dst=dst: e.tensor_scalar(out=dst, in0=src[:, :, :, 0], scalar1=c_flags[:, 24:25],
                                                              scalar2=None, op0=ALU.mult))
        S.wait(DVE, S.sig(DVE))
        S.op(DVE, lambda e, src=src, dst=dst: e.scalar_tensor_tensor(out=dst, in0=src[:, :, :, 1], scalar=c_flags[:, 25:26],
                                                                     in1=dst, op0=ALU.mult, op1=ALU.add))
    ev_mod = S.sig(DVE)
    S.wait(SP, ev_x0)
    S.barrier()

    ones_col = c_ones

    def rstd_from(pss, out, n, width, evp=None):
        S.wait(ACT, evp)
        S.op(ACT, lambda e: e.activation(out=out, in_=pss, func=AF.Ln, scale=1.0 / n, bias=c_eps[:, 0:1]))
        S.wait(ACT, S.sig(ACT))
        S.op(ACT, lambda e: e.activation(out=out, in_=out, func=AF.Exp, scale=-0.5))
        return S.sig(ACT)

    def scan_T(out, d0tile, d1):
        PW = min(T, 1024)
        ev = None
        for pc in range(T // PW):
            a = pc * PW
            S.wait(DVE, ev)
            ini = 0.0 if pc == 0 else out[:, a - 1:a]
            S.op(DVE, lambda e, a=a, ini=ini: e.tensor_tensor_scan(out=out[:, a:a + PW], data0=d0tile[:, 0:PW], data1=d1[:, a:a + PW],
                                                                   initial=ini, op0=ALU.mult, op1=ALU.add))
            ev = S.sig(DVE)
        return ev

    stg_free = [None] * NSTG
    stg_i = [0]
    stb_free = [None] * NSTB
    stb_i = [0]

    def get_stg(eng):
        i = stg_i[0] % NSTG
        stg_i[0] += 1
        S.wait(eng, stg_free[i])
        return i

    def get_stb(eng):
        i = stb_i[0] % NSTB
        stb_i[0] += 1
        S.wait(eng, stb_free[i])
        return i

    psfree = [None] * 6
    psidx = [0]

    def get_ps(k=4):
        i = psidx[0] % k
        psidx[0] += 1
        S.wait(PE, psfree[i])
        return i

    def norm_to_HT(l, which, g_sb):
        base = 0 if which == 0 else 3 * KD
        A = colA[:, which * KD:(which + 1) * KD]
        S.op(DVE, lambda e: e.scalar_tensor_tensor(out=A, in0=modl[:, l, base + KD:base + 2 * KD], scalar=1.0,
                                                   in1=g_sb[:, l * KD:(l + 1) * KD], op0=ALU.add, op1=ALU.mult))
        evA = S.sig(DVE)
        XT = BIG[:, 0:KD * TT].rearrange("p (k t) -> p k t", k=KD)
        SQ = BIGB[:, 2 * KD * TT:2 * KD * TT + KD * TT].rearrange("p (k t) -> p k t", k=KD)
        xt_free = None
        for tt in range(NT):
            ts = slice(tt * TT, (tt + 1) * TT)
            S.wait(SP, xt_free)
            ev = S.dma_split(SP, XT, xs[:, :, ts], "xt", 4)
            S.wait(ACT, ev)
            S.wait(DVE, ev)
            S.op(ACT, lambda e: e.activation(out=SQ, in_=XT, func=AF.Square))
            evq = S.sig(ACT)
            pi = get_ps()
            S.wait(PE, evq, ev_const)
            for kc in range(KD):
                S.op(PE, lambda e, kc=kc, pi=pi: e.matmul(psb[pi][:, 0:TT], lhsT=c_ones[:, :], rhs=SQ[:, kc, :],
                                                          start=(kc == 0), stop=(kc == KD - 1)))
            evp = S.sig(PE)
            rs = RSTD[tt % 2]
            evr = rstd_from(psb[pi][:, 0:TT], rs[:, 0:TT], D, TT, evp)
            psfree[pi] = evr
            S.wait(DVE, evr)
            for kc in range(KD):
                S.op(DVE, lambda e, kc=kc, rs=rs: e.tensor_mul(out=XT[:, kc, :], in0=XT[:, kc, :], in1=rs[:, 0:TT]))
            evm = S.sig(DVE)
            S.wait(ACT, evm, evA, ev_mod)
            for kc in range(KD):
                S.op(ACT, lambda e, kc=kc, ts=ts: e.activation(out=HT[:, kc, ts], in_=XT[:, kc, :], func=AF.Identity,
                                                               scale=A[:, kc:kc + 1],
                                                               bias=modl[:, l, base + kc:base + kc + 1]))
            xt_free = S.sig(ACT)
        return xt_free

    def gemm_fm(wsrc_fn, nblk, nk, rhs_fn, ntt, tw, evac_fn, wcols):
        for blk in range(nblk):
            i, ev = wload(wsrc_fn(blk), wcols)
            S.wait(PE, ev)
            for tt in range(ntt):
                pi = get_ps()
                for kc in range(nk):
                    S.op(PE, lambda e, i=i, kc=kc, tt=tt, pi=pi: e.matmul(
                        psb[pi][:, 0:tw], lhsT=WR[i][:, kc * 128:(kc + 1) * 128], rhs=rhs_fn(kc, tt),
                        start=(kc == 0), stop=(kc == nk - 1)))
                evp = S.sig(PE)
                if tt == ntt - 1:
                    wfree[i] = evp
                psfree[pi] = evac_fn(blk, tt, psb[pi][:, 0:tw], evp)

    for l in range(L):
        evn = norm_to_HT(l, 0, p_n1g)
        S.wait(PE, evn)
        S.op(DVE, lambda e, l=l: e.tensor_scalar(out=colq[:, 0:1], in0=p_sbqg[:, l:l + 1], scalar1=128.0 ** -0.5, scalar2=None, op0=ALU.mult))
        S.op(DVE, lambda e, l=l: e.tensor_copy(out=colq[:, 1:2], in_=p_sbkg[:, l:l + 1]))
        ev_colq = S.sig(DVE)

        def evac_in(blk, tt, pss, evp, l=l):
            kind, h = blk // H, blk % H
            ts = slice(tt * TT, (tt + 1) * TT)
            lh = l * H + h
            if kind == 0:
                si = get_stg(ACT)
                S.wait(ACT, evp)
                S.op(ACT, lambda e: e.activation(out=STG[si][:, 0:TT], in_=pss, func=AF.Silu))
                ev = S.sig(ACT)
                S.wait(SP, ev)
                stg_free[si] = S.dma(SP, d_qa[h, :, ts], STG[si][:, 0:TT], f"sg{si}")
                return ev
            if kind == 1:
                si = get_stb(ACT)
                S.wait(ACT, evp)
                S.op(ACT, lambda e: e.activation(out=STB[si][:, 0:TT], in_=pss, func=AF.Silu))
                ev = S.sig(ACT)
                S.wait(SP, ev)
                stb_free[si] = S.dma(SP, d_gs[h, :, ts], STB[si][:, 0:TT], f"sb{si}")
                return ev
            if kind == 2:
                s0 = get_stg(ACT)
                S.wait(ACT, evp, ev_lb)
                S.op(ACT, lambda e: e.activation(out=STG[s0][:, 0:TT], in_=pss, func=AF.Sigmoid))
                ev0 = S.sig(ACT)
                s1 = get_stg(DVE)
                s2 = get_stg(DVE)
                S.wait(DVE, ev0)
                S.op(DVE, lambda e: e.tensor_scalar(out=STG[s1][:, 0:TT], in0=STG[s0][:, 0:TT], scalar1=lb_om[:, lh:lh + 1],
                                                    scalar2=lb_tm[:, lh:lh + 1], op0=ALU.mult, op1=ALU.max))
                ev1 = S.sig(DVE)
                S.op(DVE, lambda e: e.tensor_scalar(out=STG[s2][:, 0:TT], in0=STG[s0][:, 0:TT], scalar1=lb_nom[:, lh:lh + 1],
                                                    scalar2=lb_om[:, lh:lh + 1], op0=ALU.mult, op1=ALU.add))
                ev2 = S.sig(DVE)
                stg_free[s0] = ev2
                S.wait(SP, ev2)
                stg_free[s2] = S.dma(SP, d_kk[h, :, ts], STG[s2][:, 0:TT], f"sg{s2}")
                S.wait(ACT, ev1)
                S.op(ACT, lambda e: e.activation(out=STG[s1][:, 0:TT], in_=STG[s1][:, 0:TT], func=AF.Ln,
                                                 bias=lb_lb[:, lh:lh + 1], scale=1.0))
                ev3 = S.sig(ACT)
                S.wait(SP, ev3)
                stg_free[s1] = S.dma(SP, d_lf[h, :, ts], STG[s1][:, 0:TT], f"sg{s1}")
                return ev0
            sq = get_stb(ACT)
            S.wait(ACT, evp)
            S.op(ACT, lambda e: e.activation(out=STB[sq][:, 0:TT], in_=pss, func=AF.Square))
            ev0 = S.sig(ACT)
            S.wait(PE, ev0)
            S.wait(PE, psfree[4])
            S.op(PE, lambda e: e.matmul(psb[4][:, 0:TT], lhsT=c_ones[:, :], rhs=STB[sq][:, 0:TT], start=True, stop=True))
            ev1 = S.sig(PE)
            stb_free[sq] = ev1
            rs = RSTD[0]
            evr = rstd_from(psb[4][:, 0:TT], rs[:, 0:TT], 128.0, TT, ev1)
            psfree[4] = evr
            si = get_stg(DVE)
            S.wait(DVE, evr, evp)
            S.op(DVE, lambda e: e.tensor_mul(out=STG[si][:, 0:TT], in0=pss, in1=rs[:, 0:TT]))
            ev2 = S.sig(DVE)
            so = get_stb(ACT)
            S.wait(ACT, ev2, ev_colq)
            S.op(ACT, lambda e: e.activation(out=STB[so][:, 0:TT], in_=STG[si][:, 0:TT], func=AF.Copy,
                                             scale=colq[:, kind - 3:kind - 2]))
            ev3 = S.sig(ACT)
            stg_free[si] = ev3
            S.wait(SP, ev3)
            if kind == 3:
                stb_free[so] = S.dma(SP, d_qn[h, :, ts], STB[so][:, 0:TT], f"sb{so}")
            else:
                stb_free[so] = S.dma(SP, d_kloc[h][:, ts], STB[so][:, 0:TT], f"sb{so}")
            return ev2

        gemm_fm(lambda blk, l=l: win_fm[l, blk], 5 * H, KD, lambda kc, tt: HT[:, kc, tt * TT:(tt + 1) * TT], NT, TT,
                evac_in, KD * 128)

        WV = BIGB[:, 0:KD * HW].rearrange("p (k c) -> p k c", k=KD)
        CW = min(512, HW)
        wv_free = S.sig(ACT)
        for which, dst in ((0, d_vhg), (1, None)):
            S.wait(POOL, wv_free, S.sig(PE) if S.last[PE]["sem"] in (None, PE) else None)
            ev = S.dma_split(POOL, WV, win_tm[l, which].rearrange("p (k c) -> p k c", k=KD), "wv", KD)
            S.wait(PE, ev)
            for tb in range(T // 128):
                for cg in range(HW // CW):
                    pi = get_ps()
                    for kc in range(KD):
                        S.op(PE, lambda e, kc=kc, tb=tb, cg=cg, pi=pi: e.matmul(
                            psb[pi][:, 0:CW], lhsT=HT[:, kc, tb * 128:(tb + 1) * 128], rhs=WV[:, kc, cg * CW:(cg + 1) * CW],
                            start=(kc == 0), stop=(kc == KD - 1)))
                    evp = S.sig(PE)
                    si = get_stb(DVE)
                    S.wait(DVE, evp)
                    S.op(DVE, lambda e, si=si, pi=pi: e.tensor_copy(out=STB[si][:, 0:CW], in_=psb[pi][:, 0:CW]))
                    ev = S.sig(DVE)
                    psfree[pi] = ev
                    S.wait(SP, ev)
                    if dst is not None:
                        stb_free[si] = S.dma(SP, dst[tb * 128:(tb + 1) * 128, cg * CW:(cg + 1) * CW], STB[si][:, 0:CW], f"sb{si}")
                    else:
                        for hh in range(CW // 128):
                            stb_free[si] = S.dma(SP, d_vloc[cg * (CW // 128) + hh][tb * 128:(tb + 1) * 128, :],
                                                 STB[si][:, hh * 128:(hh + 1) * 128], f"sb{si}")
            wv_free = S.sig(PE)
        S.barrier()
        if "B" in dbg and l == 0:
            break

        GR = [[0, 1, 2, 3], [4, 5, 6, 7]]
        ev_kall = [S.coll("AllGather", GR, d_kloc[h], d_kall[h], "cc") for h in range(H)][-1]
        ev_vall = [S.coll("AllGather", GR, d_vloc[h], d_vall[h], "cc") for h in range(H)][-1]

        if "C0" in dbg and l == 0:
            S.barrier()
            break
        QA = BIG[:, 0:T]; KK = BIG[:, T:2 * T]; LF = BIG[:, 2 * T:3 * T]; X1 = BIG[:, 3 * T:4 * T]; X2 = BIG[:, 4 * T:5 * T]
        X3 = LF
        OT = X1
        bo = 10 * T
        QTl = BIGB[:, bo:bo + T]; KTl = BIGB[:, bo + T:bo + 2 * T]; KE = BIGB[:, bo + 2 * T:bo + 3 * T]
        KTOK = BIGB[0:64, bo + 3 * T:bo + 5 * T].rearrange("p (n c) -> p n c", c=128)
        VH = BIGB[0:64, bo + 5 * T:bo + 7 * T].rearrange("p (n c) -> p n c", c=128)
        QH = BIGB[:, bo + 7 * T:bo + 8 * T]
        QI = QH
        assert bo + 8 * T <= 2 * BIGN
        hfree = None
        CPR = min(NCH, 16)
        for h in range(H):
            S.wait(SP, hfree)
            S.dma(SP, QA, d_qa[h], "hg")
            S.dma(SP, KK, d_kk[h], "hg")
            S.dma(SP, LF, d_lf[h], "hg")
            ev = S.dma_split(SP, VH, d_vhg.rearrange("(n p) c -> p n c", p=64)[:, :, h * 128:(h + 1) * 128], "hg", 4)
            S.wait(DVE, ev); S.wait(ACT, ev); S.wait(PE, ev)
            ev = scan_T(X1, c_onesf, LF)
            S.wait(ACT, ev)
            S.op(ACT, lambda e: e.activation(out=X1, in_=X1, func=AF.Exp))
            ev = S.sig(ACT)
            S.wait(DVE, ev)
            S.op(DVE, lambda e: e.tensor_mul(out=QH, in0=QA, in1=X1))
            S.op(DVE, lambda e: e.tensor_copy(out=DTOT[:, 0:1], in_=X1[:, T - 1:T]))
            ev = S.sig(DVE)
            S.wait(SP, ev)
            S.dma(SP, d_qh[h], QH, "hgo")
            ev_qh = S.dma(SP, d_stloc[h * 128:(h + 1) * 128, 128:132], DTOT[:, 0:4], "hgo")
            S.wait(DVE, ev)
            ev = scan_T(X1, c_scan, LF)
            S.wait(ACT, ev)
            S.op(ACT, lambda e: e.activation(out=X2, in_=X1, func=AF.Exp))
            ev_x2 = S.sig(ACT)
            S.wait(DVE, ev_x2)
            S.op(DVE, lambda e: e.tensor_copy(out=RC[:, 0:NCH], in_=X1.rearrange("p (n c) -> p n c", c=64)[:, :, 31]))
            S.wait(DVE, S.sig(DVE))
            for c in range(NCH):
                S.op(DVE, lambda e, c=c: e.tensor_scalar(out=X1[:, c * 64:(c + 1) * 64], in0=X1[:, c * 64:(c + 1) * 64],
                                                         scalar1=RC[:, c:c + 1], scalar2=None, op0=ALU.subtract))
            ev = S.sig(DVE)
            S.wait(ACT, ev)
            S.op(ACT, lambda e: e.activation(out=X3, in_=X1, func=AF.Exp))
            S.wait(ACT, S.sig(ACT))
            S.op(ACT, lambda e: e.activation(out=X1, in_=X1, func=AF.Exp, scale=-1.0))
            ev = S.sig(ACT)
            S.wait(DVE, ev, ev_qh)
            S.op(DVE, lambda e: e.tensor_mul(out=QTl, in0=QA, in1=X3))
            S.op(DVE, lambda e: e.tensor_mul(out=KTl, in0=KK, in1=X1))
            S.op(DVE, lambda e: e.tensor_mul(out=QI, in0=QA, in1=X2))
            S.wait(DVE, S.sig(DVE))
            for c in range(NCH):
                S.op(DVE, lambda e, c=c: e.tensor_scalar(out=KE[:, c * 64:(c + 1) * 64], in0=KTl[:, c * 64:(c + 1) * 64],
                                                         scalar1=X3[:, c * 64 + 63:c * 64 + 64], scalar2=None, op0=ALU.mult))
            S.op(DVE, lambda e: e.memset(S32[:, :], 0.0))
            S.op(DVE, lambda e: e.memset(Sb[:, :], 0.0))
            ev_prep = S.sig(DVE)
            S.wait(PE, ev_prep)
            ev_ktok = None
            for rnd in range(NCH // CPR):
                S.wait(PE, ev_ktok)
                for cc in range(CPR):
                    c = rnd * CPR + cc
                    S.op(PE, lambda e, c=c, cc=cc: e.transpose(out=PST[0:64, cc * 128:(cc + 1) * 128], in_=KE[:, c * 64:(c + 1) * 64], identity=c_id[:, :]))
                ev = S.sig(PE)
                S.wait(ACT, ev)
                S.op(ACT, lambda e, rnd=rnd: e.activation(out=KTOK[:, rnd * CPR:(rnd + 1) * CPR, :],
                                                          in_=PST[0:64, 0:CPR * 128].rearrange("p (n c) -> p n c", c=128), func=AF.Copy))
                ev_ktok = S.sig(ACT)
            S.wait(PE, ev_ktok)
            sc_free = [None, None]
            sct_free = [None, None]
            scs = [STB[0], STB[1]]
            u_free = [None, None]
            st_ev = ev_prep
            ev_s = ev_prep
            NBLK = T // 128
            for tb in range(NBLK):
                k2 = tb % 2
                pscore = psb[2][0:64, k2 * 128:(k2 + 1) * 128]
                S.wait(PE, sc_free[k2])
                for half in range(2):
                    b0 = tb * 128 + half * 64
                    pc = pscore[:, half * 64:(half + 1) * 64]
                    S.op(PE, lambda e, b0=b0, pc=pc: e.matmul(pc[0:32, 0:32], lhsT=KTl[:, b0:b0 + 32], rhs=QTl[:, b0:b0 + 32], start=True, stop=True))
                    S.op(PE, lambda e, b0=b0, pc=pc: e.matmul(pc[0:64, 32:64], lhsT=KTl[:, b0:b0 + 64], rhs=QTl[:, b0 + 32:b0 + 64], start=True, stop=True))
                ev = S.sig(PE)
                S.wait(DVE, ev)
                sct = scs[k2][0:64, 0:128]
                S.wait(DVE, sct_free[k2])
                S.op(DVE, lambda e, sct=sct, pscore=pscore: e.tensor_mul(out=sct, in0=pscore, in1=c_mhg[0:64, :]))
                ev = S.sig(DVE)
                S.wait(PE, ev)
                pot = psb[(tb // 4) % 2][:, (tb % 4) * 128:(tb % 4) * 128 + 128]
                if tb % 4 == 0:
                    S.wait(PE, psfree[(tb // 4) % 2])
                for half in range(2):
                    c = tb * 2 + half
                    cs = slice(c * 64, (c + 1) * 64)
                    hs = slice(half * 64, (half + 1) * 64)
                    S.op(PE, lambda e, c=c, sct=sct, pot=pot, hs=hs: e.matmul(pot[:, hs], lhsT=VH[:, c, :], rhs=sct[:, hs], start=True, stop=False))
                    if half == 1:
                        sct_free[k2] = S.sig(PE)
                    S.wait(PE, st_ev)
                    S.op(PE, lambda e, cs=cs, pot=pot, hs=hs: e.matmul(pot[:, hs], lhsT=Sb[:, :], rhs=QI[:, cs], start=False, stop=True))
                    ev_o = S.sig(PE)
                    pu = psb[3][:, (c % 2) * 128:(c % 2) * 128 + 128]
                    S.wait(PE, u_free[c % 2])
                    S.op(PE, lambda e, c=c, pu=pu: e.matmul(pu, lhsT=KTOK[:, c, :], rhs=VH[:, c, :], start=True, stop=True))
                    ev_u = S.sig(PE)
                    S.wait(DVE, ev_u, ev_o)
                    S.op(DVE, lambda e, c=c, pu=pu: e.scalar_tensor_tensor(out=S32[:, :], in0=S32[:, :], scalar=X2[:, c * 64 + 63:c * 64 + 64],
                                                                           in1=pu, op0=ALU.mult, op1=ALU.add))
                    ev_s = S.sig(DVE)
                    u_free[c % 2] = ev_s
                    S.wait(ACT, ev_s)
                    S.op(ACT, lambda e: e.activation(out=Sb[:, :], in_=S32[:, :], func=AF.Copy))
                    st_ev = S.sig(ACT)
                sc_free[k2] = ev_o
                if tb % 4 == 3 or tb == NBLK - 1:
                    nb_ = tb % 4 + 1
                    t0 = (tb - nb_ + 1) * 128
                    pbank = psb[(tb // 4) % 2][:, 0:nb_ * 128]
                    S.wait(ACT, ev_o)
                    S.op(ACT, lambda e, t0=t0, nb_=nb_, pbank=pbank: e.activation(out=OT[:, t0:t0 + nb_ * 128], in_=pbank, func=AF.Copy))
                    psfree[(tb // 4) % 2] = S.sig(ACT)
            ev = S.sig(ACT)
            S.wait(SP, ev, st_ev, ev_s)
            S.dma(SP, d_oloc[h], OT, "hgo")
            ev = S.dma(SP, d_stloc[h * 128:(h + 1) * 128, 0:128], S32[:, :], "hgo")
            hfree = [ev, S.sig(PE)]
        S.wait(POOL, hfree)
        ev_stall = S.coll("AllGather", GR, d_stloc, d_stall, "cc")
        S.barrier()

        if "C1" in dbg and l == 0:
            break
        O2 = BIG[:, 0:T]; QH2 = BIGB[:, 2 * T:3 * T]; GS2 = BIGB[:, 3 * T:4 * T]
        c2free = None
        for h in range(H):
            S.wait(SP, c2free)
            S.dma(SP, O2, d_oloc[h], "c2")
            S.dma(SP, QH2, d_qh[h], "c2")
            S.dma(SP, GS2, d_gs[h], "c2")
            ev = S.dma(SP, STALL[:, :, :], d_stall.rearrange("(g hh p) c -> hh p g c", g=G, hh=H)[h], "c2")
            S.wait(DVE, ev); S.wait(PE, ev); S.wait(ACT, ev)
            S.op(DVE, lambda e: e.memset(ACC[:, :], 0.0))
            S.op(DVE, lambda e: e.memset(SIN[:, :], 0.0))
            for i in range(G):
                S.wait(DVE, S.sig(DVE))
                S.op(DVE, lambda e, i=i: e.scalar_tensor_tensor(out=ACC[:, :], in0=ACC[:, :], scalar=STALL[:, i, 128:129], in1=STALL[:, i, 0:128],
                                                                op0=ALU.mult, op1=ALU.add))
                S.wait(DVE, S.sig(DVE))
                S.op(DVE, lambda e, i=i: e.scalar_tensor_tensor(out=SIN[:, :], in0=ACC[:, :], scalar=c_flags[:, 12 + i:13 + i], in1=SIN[:, :],
                                                                op0=ALU.mult, op1=ALU.add))
            S.wait(DVE, S.sig(DVE))
            S.op(DVE, lambda e: e.tensor_copy(out=Sb[:, :], in_=SIN[:, :]))
            ev = S.sig(DVE)
            S.wait(PE, ev)
            for tt in range(NT):
                ts = slice(tt * TT, (tt + 1) * TT)
                pi = get_ps()
                S.op(PE, lambda e, pi=pi, ts=ts: e.matmul(psb[pi][:, 0:TT], lhsT=Sb[:, :], rhs=QH2[:, ts], start=True, stop=True))
                evp = S.sig(PE)
                S.wait(DVE, evp)
                S.op(DVE, lambda e, pi=pi, ts=ts: e.tensor_add(out=O2[:, ts], in0=psb[pi][:, 0:TT], in1=O2[:, ts]))
                ev = S.sig(DVE)
                psfree[pi] = ev
                sq = get_stb(ACT)
                S.wait(ACT, ev)
                S.op(ACT, lambda e, sq=sq, ts=ts: e.activation(out=STB[sq][:, 0:TT], in_=O2[:, ts], func=AF.Square))
                ev0 = S.sig(ACT)
                S.wait(PE, ev0, psfree[4])
                S.op(PE, lambda e, sq=sq: e.matmul(psb[4][:, 0:TT], lhsT=c_ones[:, :], rhs=STB[sq][:, 0:TT], start=True, stop=True))
                ev1 = S.sig(PE)
                stb_free[sq] = ev1
                rs = RSTD[tt % 2]
                evr = rstd_from(psb[4][:, 0:TT], rs[:, 0:TT], 128.0, TT, ev1)
                psfree[4] = evr
                S.wait(DVE, evr)
                S.op(DVE, lambda e, ts=ts, rs=rs: e.tensor_mul(out=O2[:, ts], in0=O2[:, ts], in1=rs[:, 0:TT]))
                S.wait(DVE, S.sig(DVE))
                S.op(DVE, lambda e, ts=ts: e.tensor_mul(out=O2[:, ts], in0=O2[:, ts], in1=GS2[:, ts]))
                ev = S.sig(DVE)
                S.wait(ACT, ev)
                S.op(ACT, lambda e, ts=ts, h=h, l=l: e.activation(out=HT[:, h, ts], in_=O2[:, ts], func=AF.Copy,
                                                                  scale=p_hgog[:, l * H + h:l * H + h + 1]))
            ev = S.sig(ACT)
            c2free = [ev, S.sig(PE)]
        S.barrier()
        if "C" in dbg and l == 0:
            break

        NSB = KT // 128
        KTs = BIGB[:, 0:G * T].rearrange("p (g t) -> p g t", g=G)
        Vs = BIGB[:, G * T:2 * G * T].rearrange("p (n c) -> p n c", c=128)
        Qs = BIGB[:, 2 * G * T:2 * G * T + T]
        f0 = (2 * G * T + T + 1) // 2

        def ftile(k):
            return BIG[:, f0 + k * (KT + 4):f0 + k * (KT + 4) + KT + 4]
        assert f0 + 6 * (KT + 4) <= BIGN, (f0, KT, BIGN)
        E_ = [ftile(0), ftile(1)]; SP_ = [ftile(2), ftile(3)]; C_ = [ftile(4), ftile(5)]
        S.op(DVE, lambda e: e.memset(C_[0][:, 0:1], 0.0))
        S.op(DVE, lambda e: e.memset(C_[1][:, 0:1], 0.0))
        ev_c0 = S.sig(DVE)
        S.wait(DVE, ev_c0)
        hd_free = None
        e_free = [None, None]; sp_free = [None, None]; c_free = [None, None]; a_free = [None, None]; at_free = [None, None]
        z_free = [None, None]; pt_free = [None, None]; po_free = [None, None]; on_free = [None]
        nbk = [0]
        pos = [PS[2][:, 512:640], PS[2][:, 640:768]]
        tiles = [(o, kt) for o in reversed(range(G)) for kt in reversed(range(NKT))]
        ntl = len(tiles)
        for h in range(H):
            S.wait(SP, hd_free, ev_kall, ev_vall)
            S.dma_split(SP, KTs, d_kall[h].rearrange("(g d) t -> d g t", g=G), "at", G)
            S.dma_split(SP, Vs, d_vall[h].rearrange("(n p) c -> p n c", p=128), "at", 8)
            ev_ld = S.dma(SP, Qs, d_qn[h], "at")
            S.wait(PE, ev_ld)
            jobs = [(qb, ti) for qb in range(NQB) for ti in range(ntl)]
            info = {}
            ncar_box = [None]

            def stageA1pe(jn, h=h):
                qb, ti = jobs[jn]
                o, kt = tiles[ti]
                qs = slice(qb * 128, (qb + 1) * 128)
                m = jn % 2
                pz = PS[m][:, 0:KT]
                S.wait(PE, z_free[m])
                w = min(512, KT)
                for c5 in range(KT // w):
                    S.op(PE, lambda e, c5=c5: e.matmul(pz[:, c5 * w:(c5 + 1) * w], lhsT=Qs[:, qs],
                                                       rhs=KTs[:, o, kt * KT + c5 * w:kt * KT + (c5 + 1) * w], start=True, stop=True))
                info[jn] = dict(m=m, o=o, kt=kt, qb=qb, ti=ti, pz=pz, ev_z=S.sig(PE))

            def stageA1(jn, h=h):
                d = info[jn]
                m, o, kt, qb, ti, pz, ev_z = d["m"], d["o"], d["kt"], d["qb"], d["ti"], d["pz"], d["ev_z"]
                Et, SPt = E_[m], SP_[m]
                S.wait(ACT, ev_z, e_free[m])
                S.op(ACT, lambda e: e.activation(out=Et[:, 0:KT], in_=pz, func=AF.Exp))
                ev_e = S.sig(ACT)
                kb0 = kt * NSB
                nbef = min(max(qb - kb0, 0), NSB)
                hasd = (kb0 <= qb < kb0 + NSB)
                naft = NSB - nbef - (1 if hasd else 0)
                rb = (0, nbef * 128)
                rd = (nbef * 128, nbef * 128 + 128) if hasd else None
                ra = ((nbef + (1 if hasd else 0)) * 128, KT)
                ev_em = None
                if rd is not None:
                    S.wait(DVE, ev_e)
                    S.op(DVE, lambda e: e.tensor_mul(out=Et[:, rd[0]:rd[1]], in0=Et[:, rd[0]:rd[1]], in1=MD[:, o, :]))
                    ev_em = S.sig(DVE)
                S.wait(ACT, ev_e, sp_free[m])
                if nbef > 0:
                    S.op(ACT, lambda e: e.activation(out=SPt[:, rb[0]:rb[1]], in_=Et[:, rb[0]:rb[1]], func=AF.Ln,
                                                     scale=c_flags[:, o:o + 1], bias=c_onesf[:, 0:1]))
                if naft > 0:
                    S.op(ACT, lambda e: e.activation(out=SPt[:, ra[0]:ra[1]], in_=Et[:, ra[0]:ra[1]], func=AF.Ln,
                                                     scale=c_flags[:, 4 + o:5 + o], bias=c_onesf[:, 0:1]))
                if rd is not None:
                    S.wait(ACT, ev_em)
                    S.op(ACT, lambda e: e.activation(out=SPt[:, rd[0]:rd[1]], in_=Et[:, rd[0]:rd[1]], func=AF.Ln,
                                                     scale=1.0, bias=c_onesf[:, 0:1]))
                ev_sp = S.sig(ACT)
                d.update(rb=rb, rd=rd, ra=ra, nbef=nbef, naft=naft, ev_sp=ev_sp)

            def stageA2(jn):
                d = info[jn]
                m, o, pz = d["m"], d["o"], d["pz"]
                SPt, Ct = SP_[m], C_[m]
                S.wait(DVE, d["ev_sp"], c_free[m])
                S.op(DVE, lambda e: e.tensor_tensor_scan(out=Ct[:, 1:KT + 1], data0=c_onesf[:, 0:KT], data1=SPt[:, 0:KT],
                                                         initial=0.0, op0=ALU.mult, op1=ALU.add))
                ev_c = S.sig(DVE)
                S.wait(DVE, ev_c)
                k3 = (nbk[0] % 16) * 3
                nbk[0] += 1
                nbc, nb1, nb2 = NB[:, k3:k3 + 1], NB[:, k3 + 1:k3 + 2], NB[:, k3 + 2:k3 + 3]
                ncar = ncar_box[0] if d["ti"] > 0 else None
                S.op(DVE, lambda e: e.tensor_add(out=SPt[:, 0:KT], in0=pz, in1=Ct[:, 0:KT]))
                ev_u = S.sig(DVE)
                if ncar is None:
                    S.op(DVE, lambda e: e.tensor_scalar(out=nbc, in0=Ct[:, KT:KT + 1], scalar1=-1.0, scalar2=None, op0=ALU.mult))
                else:
                    S.op(DVE, lambda e: e.tensor_sub(out=nbc, in0=ncar, in1=Ct[:, KT:KT + 1]))
                S.wait(DVE, S.sig(DVE))
                S.op(DVE, lambda e: e.tensor_add(out=nb1, in0=nbc, in1=c_flags[:, 16 + o:17 + o]))
                S.op(DVE, lambda e: e.tensor_add(out=nb2, in0=nbc, in1=c_flags[:, 20 + o:21 + o]))
                ev_nb = S.sig(DVE)
                ncar_box[0] = nbc
                z_free[m] = ev_u
                c_free[m] = ev_nb
                d.update(nbc=nbc, nb1=nb1, nb2=nb2, ev_u=[ev_u, ev_nb])

            def stageB1(jn, h=h):
                d = info[jn]
                m, o, kt, rb, rd, ra, qb, ti = d["m"], d["o"], d["kt"], d["rb"], d["rd"], d["ra"], d["qb"], d["ti"]
                nbc, nb1, nb2 = d["nbc"], d["nb1"], d["nb2"]
                Et, SPt, At = E_[m], SP_[m], As[m]
                S.wait(ACT, d["ev_u"], a_free[m])
                if d["nbef"] > 0:
                    S.op(ACT, lambda e: e.activation(out=At[:, rb[0]:rb[1]], in_=SPt[:, rb[0]:rb[1]], func=AF.Exp, bias=nb1))
                if d["naft"] > 0:
                    S.op(ACT, lambda e: e.activation(out=At[:, ra[0]:ra[1]], in_=SPt[:, ra[0]:ra[1]], func=AF.Exp, bias=nb2))
                if rd is not None:
                    S.op(ACT, lambda e: e.activation(out=Et[:, rd[0]:rd[1]], in_=SPt[:, rd[0]:rd[1]], func=AF.Exp, bias=nbc))
                ev_a = S.sig(ACT)
                evs_a = [ev_a]
                if rd is not None:
                    S.wait(DVE, ev_a)
                    S.op(DVE, lambda e: e.tensor_mul(out=At[:, rd[0]:rd[1]], in0=Et[:, rd[0]:rd[1]], in1=MD[:, o, :]))
                    evs_a.append(S.sig(DVE))
                e_free[m] = [d["ev_sp"]] + evs_a
                sp_free[m] = evs_a
                pT = PST[:, m * 1024:m * 1024 + KT]
                S.wait(PE, evs_a, pt_free[m])
                for sbk in range(NSB):
                    S.op(PE, lambda e, sbk=sbk: e.transpose(out=pT[:, sbk * 128:(sbk + 1) * 128], in_=At[:, sbk * 128:(sbk + 1) * 128],
                                                            identity=c_id[:, :]))
                ev_t = S.sig(PE)
                a_free[m] = ev_t
                d.update(ev_t=ev_t, pT=pT)

            def stageB2(jn, h=h):
                d = info.pop(jn)
                m, o, kt, qb, ti, pT, ev_t = d["m"], d["o"], d["kt"], d["qb"], d["ti"], d["pT"], d["ev_t"]
                ATt = ATs[m]
                po = pos[qb % 2]
                S.wait(ACT, ev_t, at_free[m])
                S.op(ACT, lambda e: e.activation(out=ATt[:, 0:KT], in_=pT, func=AF.Copy))
                ev_at = S.sig(ACT)
                pt_free[m] = ev_at
                S.wait(PE, ev_at)
                if ti == 0:
                    S.wait(PE, po_free[qb % 2])
                for sbk in range(NSB):
                    blk = (o * T + kt * KT) // 128 + sbk
                    S.op(PE, lambda e, sbk=sbk, blk=blk, first=(ti == 0 and sbk == 0), last=(ti == ntl - 1 and sbk == NSB - 1):
                         e.matmul(po, lhsT=ATt[:, sbk * 128:(sbk + 1) * 128], rhs=Vs[:, blk, :], start=first, stop=last))
                at_free[m] = S.sig(PE)
                if ti == ntl - 1:
                    epilogue(qb, at_free[m], h)

            def epilogue(qb, ev_po, h, l=l):
                qs = slice(qb * 128, (qb + 1) * 128)
                po = pos[qb % 2]
                k3 = (nbk[0] % 16) * 3
                nbk[0] += 1
                ssq, rsd = NB[:, k3:k3 + 1], NB[:, k3 + 1:k3 + 2]
                S.op(DVE, lambda e: e.memset(ssq, 0.0))
                S.wait(ACT, S.sig(DVE), ev_po, on_free[0])
                S.op(ACT, lambda e: e.activation(out=ONb[:, :], in_=po, func=AF.Square, accum_out=ssq))
                S.wait(ACT, S.sig(ACT))
                S.op(ACT, lambda e: e.activation(out=rsd, in_=ssq, func=AF.Ln, scale=1.0 / 128.0, bias=c_eps[:, 0:1]))
                S.wait(ACT, S.sig(ACT))
                S.op(ACT, lambda e: e.activation(out=rsd, in_=rsd, func=AF.Exp, scale=-0.5))
                S.wait(ACT, S.sig(ACT))
                S.op(ACT, lambda e: e.activation(out=ONb[:, :], in_=po, func=AF.Copy, scale=rsd))
                ev_on = S.sig(ACT)
                po_free[qb % 2] = ev_on
                pT = PST[:, 0:128]
                S.wait(PE, ev_on, pt_free[0])
                S.op(PE, lambda e: e.transpose(out=pT, in_=ONb[:, :], identity=c_id[:, :]))
                ev_tr = S.sig(PE)
                on_free[0] = ev_tr
                S.wait(ACT, ev_tr)
                S.op(ACT, lambda e: e.activation(out=HT[:, H + h, qs], in_=pT, func=AF.Copy,
                                                 scale=p_sbog[:, l * H + h:l * H + h + 1]))
                pt_free[0] = S.sig(ACT)

            nj = len(jobs)
            for step in range(nj + 2):
                if step < nj:
                    stageA1pe(step)
                if step >= 2:
                    stageB1(step - 2)
                if step < nj:
                    stageA1(step)
                if 1 <= step <= nj:
                    stageA2(step - 1)
                if step >= 2:
                    stageB2(step - 2)
            hd_free = [S.sig(PE), S.sig(ACT)]
        S.barrier()
        if "D" in dbg and l == 0:
            break

        def evac_res(gcol0, l=l):
            def f(blk, tt, pss, evp):
                ts = slice(tt * TT, (tt + 1) * TT)
                si = get_stg(SP)
                ev = S.dma(SP, STG[si][:, 0:TT], xs[:, blk, ts], f"sg{si}")
                S.wait(DVE, ev, evp)
                S.op(DVE, lambda e: e.scalar_tensor_tensor(out=STG[si][:, 0:TT], in0=pss, scalar=modl[:, l, gcol0 + blk:gcol0 + blk + 1],
                                                           in1=STG[si][:, 0:TT], op0=ALU.mult, op1=ALU.add))
                ev = S.sig(DVE)
                S.wait(SP, ev)
                stg_free[si] = S.dma(SP, xs[:, blk, ts], STG[si][:, 0:TT], f"sg{si}")
                return ev
            return f
        gemm_fm(lambda blk, l=l: wout[l, blk], KD, KD, lambda kc, tt: HT[:, kc, tt * TT:(tt + 1) * TT], NT, TT,
                evac_res(2 * KD), KD * 128)
        S.barrier()
        if "E" in dbg and l == 0:
            break

        evn = norm_to_HT(l, 1, p_n2g)
        S.wait(PE, evn)
        gate_sb = {}

        def evac_ffn_in(blk, tt, pss, evp, l=l):
            j, which = blk // 2, blk % 2
            ts = slice(tt * TT, (tt + 1) * TT)
            if which == 0:
                si = get_stg(ACT)
                S.wait(ACT, evp)
                S.op(ACT, lambda e: e.activation(out=STG[si][:, 0:TT], in_=pss, func=AF.Silu))
                ev = S.sig(ACT)
                gate_sb[(j, tt)] = (si, ev)
                return ev
            si, evg = gate_sb.pop((j, tt))
            so = get_stb(DVE)
            S.wait(DVE, evg, evp)
            S.op(DVE, lambda e: e.tensor_mul(out=STB[so][:, 0:TT], in0=pss, in1=STG[si][:, 0:TT]))
            ev = S.sig(DVE)
            stg_free[si] = ev
            S.wait(SP, ev)
            stb_free[so] = S.dma(SP, d_act[j, :, ts], STB[so][:, 0:TT], f"sb{so}")
            return ev

        gemm_fm(lambda blk, l=l: wfi[l, blk // 2, :, (blk % 2) * KD * 128:(blk % 2 + 1) * KD * 128], 2 * NF, KD,
                lambda kc, tt: HT[:, kc, tt * TT:(tt + 1) * TT], NT, TT, evac_ffn_in, KD * 128)
        S.barrier()
        ACTT = BIGB[:, 0:NF * TT].rearrange("p (k t) -> p k t", k=NF)
        assert NF * TT <= 2 * BIGN
        act_free = None
        for tt in range(NT):
            ts = slice(tt * TT, (tt + 1) * TT)
            S.wait(SP, act_free)
            ev_act = S.dma_split(SP, ACTT, d_act[:, :, ts].rearrange("k p t -> p k t"), "actt", 11)
            S.wait(PE, ev_act)
            er = evac_res(5 * KD)
            for blk in range(KD):
                i, ev = wload(wfo[l, blk], NF * 128)
                S.wait(PE, ev)
                pi = get_ps()
                for kc in range(NF):
                    S.op(PE, lambda e, i=i, kc=kc, pi=pi: e.matmul(psb[pi][:, 0:TT], lhsT=WR[i][:, kc * 128:(kc + 1) * 128], rhs=ACTT[:, kc, :],
                                                                   start=(kc == 0), stop=(kc == NF - 1)))
                evp = S.sig(PE)
                wfree[i] = evp
                psfree[pi] = er(blk, tt, psb[pi][:, 0:TT], evp)
            act_free = S.sig(PE)
        S.barrier()
    scr = dict(d_qa=d_qa, d_kk=d_kk, d_lf=d_lf, d_gs=d_gs, d_vhg=d_vhg, d_qn=d_qn, d_stloc=d_stloc, d_stall=d_stall, d_oloc=d_oloc, d_qh=d_qh, d_act=d_act,
               d_modall=d_modall)
    if "HT" in dbg:
        o = nc.dram_tensor("dbg_HT", [128, KD, T], BF16, kind="ExternalOutput").ap()
        S.dma(SP, o, HT[:, :, :], "dbgout")
    for name in dbg:
        if name in scr:
            src = scr[name]
            o = nc.dram_tensor("dbg_" + name, list(src.shape), src.dtype, kind="ExternalOutput").ap()
            S.dma(SP, o, src, "dbgout")
    ev = S.dma_split(SP, yT[:, :, :], xs[:, :, :], "out", KD)
    S.wait(SP, ev)
    S.barrier()
    with nc.Block() as block:
        S.emit(block)
    stack.close()
    return nc, S


def _bf(a):
    return a


def prep_inputs(cfg, inp):
    D, T, L, KD, H, NF, NMB = cfg.D, cfg.T, cfg.L, cfg.KD, cfg.H, cfg.NF, cfg.NMB
    HW = H * 128
    f = lambda a: np.ascontiguousarray(a, dtype=np.float32)
    x = np.asarray(inp["x"], np.float32)
    c = np.asarray(inp["c"], np.float32)

    def colvec(a, n):
        a = np.asarray(a, np.float32).reshape(L, n, 128)
        return f(a.transpose(2, 0, 1).reshape(128, L * n))

    w_in = np.asarray(inp["w_in"], np.float32)
    segs = [0, 3, 1, 4, 5]
    fm = np.concatenate([w_in[:, :, s * HW:(s + 1) * HW] for s in segs], axis=2)
    fm = fm.reshape(L, KD, 128, 5 * H, 128).transpose(0, 3, 2, 1, 4)
    win_fm = f(fm.reshape(L, 5 * H, 128, KD * 128))
    tm = np.stack([w_in[:, :, 2 * HW:3 * HW], w_in[:, :, 6 * HW:7 * HW]], axis=1)
    tm = tm.reshape(L, 2, KD, 128, HW).transpose(0, 1, 3, 2, 4)
    win_tm = f(tm.reshape(L, 2, 128, KD * HW))
    wo = np.asarray(inp["w_out"], np.float32).reshape(L, KD, 128, KD, 128).transpose(0, 3, 2, 1, 4)
    wout = f(wo.reshape(L, KD, 128, KD * 128))
    wi = np.asarray(inp["w_ffn_in"], np.float32).reshape(L, KD, 128, 2, NF, 128).transpose(0, 4, 2, 3, 1, 5)
    wfi = f(wi.reshape(L, NF, 128, 2 * KD * 128))
    wf = np.asarray(inp["w_ffn_out"], np.float32).reshape(L, NF, 128, KD, 128).transpose(0, 3, 2, 1, 4)
    wfo = f(wf.reshape(L, KD, 128, NF * 128))
    wa = np.asarray(inp["w_ada"], np.float32).reshape(L, KD, 128, 6 * KD, 128).transpose(0, 3, 2, 1, 4)
    ba = np.asarray(inp["b_ada"], np.float32).reshape(L, 6 * KD, 128)
    cTt = f(c.T.reshape(KD, 128, 2).transpose(1, 0, 2))
    common = dict(
        cT=cTt, n1g=colvec(inp["norm1_g"], KD), n2g=colvec(inp["norm2_g"], KD),
        hgog=colvec(inp["hg_out_g"], H), sbog=colvec(inp["sb_out_g"], H),
        sbqg=colvec(inp["sb_q_g"], 1), sbkg=colvec(inp["sb_k_g"], 1), lbl=colvec(inp["hg_lb_logits"], H),
        win_fm=win_fm, win_tm=win_tm, wout=wout, wfi=wfi, wfo=wfo,
    )
    s_idx = np.arange(128)
    mhg = ((s_idx[:, None] <= (s_idx[None, :] % 64)) & (s_idx[:, None] < 64)).astype(np.float32)
    msb = (s_idx[None, :] < s_idx[:, None]).astype(np.float32)
    ident = np.eye(128, dtype=np.float32)
    scan = np.ones((128, min(T, 1024)), np.float32)
    scan[:, ::64] = 0.0
    maps = []
    for r in range(NCORES):
        b, j = r // G, r % G
        xt = x[b, j * T:(j + 1) * T, :].T.reshape(KD, 128, T).transpose(1, 0, 2)
        fl = np.zeros((128, 32), np.float32)
        for o in range(G):
            fl[:, o] = 1.0 if o <= j else 0.0
            fl[:, 4 + o] = 1.0 if o < j else 0.0
            fl[:, 8 + o] = 1.0 if o == j else 0.0
            fl[:, 12 + o] = 1.0 if o == j - 1 else 0.0
            fl[:, 16 + o] = 0.0 if o <= j else BIGNEG
            fl[:, 20 + o] = 0.0 if o < j else BIGNEG
        fl[:, 24] = 1.0 if b == 0 else 0.0
        fl[:, 25] = 1.0 if b == 1 else 0.0
        wad = f(wa[:, j * NMB:(j + 1) * NMB].reshape(L, NMB, 128, KD * 128))
        bad = ba[:, j * NMB:(j + 1) * NMB, :].transpose(2, 0, 1)
        bad = f(np.repeat(bad[:, :, :, None], 2, axis=3).reshape(128, L * NMB * 2))
        m = dict(common)
        m.update(xT=f(xt), wada=wad, bada=bad, flags=fl, cmask_hg=mhg, cmask_sb=msb, cident=ident, cscan=scan)
        maps.append(m)
    return maps


def assemble(cfg, results):
    D, T, KD = cfg.D, cfg.T, cfg.KD
    out = np.zeros((2, cfg.S, D), np.float32)
    for r in range(NCORES):
        b, j = r // G, r % G
        y = np.asarray(results[r]["yT"], np.float32)
        out[b, j * T:(j + 1) * T, :] = y.transpose(1, 0, 2).reshape(D, T).T
    return out


_CACHE = {}


def kernel(**inputs):
    x = inputs["x"]
    B, Sq, D = x.shape
    L = inputs["w_in"].shape[0]
    DFF = inputs["w_ffn_out"].shape[1]
    cfg = Cfg(D, Sq, L, DFF)
    key = (D, Sq, L, DFF)
    if key not in _CACHE:
        _CACHE[key] = build(cfg)[0]
    nc = _CACHE[key]
    maps = prep_inputs(cfg, inputs)
    res = run_bass_kernel_spmd(nc, maps, core_ids=list(range(NCORES)))
    return assemble(cfg, res.results)
```

```python
import contextlib
import numpy as np
import concourse.bass as bass
import concourse.mybir as mybir
from concourse.bass_utils import run_bass_kernel_spmd

F32 = mybir.dt.float32
BF16 = mybir.dt.bfloat16
AF = mybir.ActivationFunctionType
ALU = mybir.AluOpType
AX = mybir.AxisListType

EPS = 1e-6
TINY = 1e-30
BIGNEG = -30000.0
NCORES = 8
G = 4


class Cfg:
    def __init__(self, D, S, L, DFF):
        self.D, self.S, self.L, self.DFF = D, S, L, DFF
        self.T = S // G
        self.KD = D // 128
        self.H = D // 2 // 128
        self.NF = DFF // 128
        self.TT = min(512, self.T)
        self.NT = self.T // self.TT
        self.NMB = 6 * self.KD // G
        self.NCH = self.T // 64
        self.NQB = self.T // 128
        self.KT = min(1024, self.T)
        self.NKT = self.T // self.KT


class Sched:
    ENG = ("pe", "act", "dve", "pool", "sp")

    def __init__(self, nc, stack):
        self.nc = nc
        self.stack = stack
        self.ops = {e: [] for e in self.ENG}
        self.last = {e: None for e in self.ENG}
        self.cnt = {}
        self.sems = {}
        self.seen = {e: {} for e in self.ENG}
        self.pending = {}
        for e in self.ENG:
            self._mksem(e)

    def _mksem(self, key):
        self.sems[key] = self.stack.enter_context(self.nc.semaphore("s_" + key))
        self.cnt[key] = 0

    def op(self, eng, fn):
        rec = {"fn": fn, "sem": None, "val": None, "inc": None}
        self.ops[eng].append(rec)
        self.last[eng] = rec
        return rec

    def sig(self, eng):
        rec = self.last[eng]
        assert rec is not None
        if rec["sem"] is None:
            self.cnt[eng] += 1
            rec["sem"], rec["val"], rec["inc"] = eng, self.cnt[eng], 1
        assert rec["sem"] == eng or rec["inc"] != 1, "sig on dma op"
        return (rec["sem"], rec["val"])

    def wait(self, eng, *evs):
        for ev in evs:
            if ev is None:
                continue
            if isinstance(ev, list):
                self.wait(eng, *ev)
                continue
            key, val = ev
            if self.seen[eng].get(key, 0) >= val:
                continue
            self.seen[eng][key] = val
            self.ops[eng].append({"wait": key, "val": val})

    def dma(self, q, out, in_, chan):
        if chan not in self.sems:
            self._mksem(chan)
        rec = self.op(q, lambda e: e.dma_start(out=out, in_=in_))
        self.cnt[chan] += 16
        rec["sem"], rec["val"], rec["inc"] = chan, self.cnt[chan], 16
        ev = (chan, self.cnt[chan])
        self.pending[chan] = ev
        return ev

    def dma_split(self, q, out, in_, chan, n):
        d = out.shape[1]
        n = min(n, d)
        step = (d + n - 1) // n
        ev = None
        for a in range(0, d, step):
            b = min(d, a + step)
            ev = self.dma(q, out[:, a:b], in_[:, a:b], chan)
        return ev

    def coll(self, kind, groups, in_, out, chan):
        if chan not in self.sems:
            self._mksem(chan)
        rec = self.op("pool", lambda e: e.collective_compute(
            kind, ALU.bypass, replica_groups=groups, ins=[in_], outs=[out]))
        self.cnt[chan] += 1
        rec["sem"], rec["val"], rec["inc"] = chan, self.cnt[chan], None
        ev = (chan, self.cnt[chan])
        self.pending[chan] = ev
        return ev

    def barrier(self):
        evs = []
        for e in ("pe", "act", "dve", "pool"):
            rec = self.last[e]
            if rec is not None and (rec["sem"] is None or rec["sem"] == e):
                evs.append(self.sig(e))
        evs += list(self.pending.values())
        for e in self.ENG:
            self.wait(e, *evs)
        return evs

    def emit(self, block):
        def mk(eng):
            def body(e):
                for rec in self.ops[eng]:
                    if "wait" in rec:
                        e.wait_ge(self.sems[rec["wait"]], rec["val"])
                    else:
                        ins = rec["fn"](e)
                        if rec["sem"] is not None:
                            if rec["inc"] is None:
                                ins.then_inc(self.sems[rec["sem"]])
                            else:
                                ins.then_inc(self.sems[rec["sem"]], rec["inc"])
            return body
        block.tensor(mk("pe"))
        block.scalar(mk("act"))
        block.vector(mk("dve"))
        block.gpsimd(mk("pool"))
        block.sync(mk("sp"))


def build(cfg, dbg=()):
    D, T, L, KD, H, NF, TT, NT, NMB = cfg.D, cfg.T, cfg.L, cfg.KD, cfg.H, cfg.NF, cfg.TT, cfg.NT, cfg.NMB
    NCH, NQB, KT, NKT = cfg.NCH, cfg.NQB, cfg.KT, cfg.NKT
    HW = H * 128
    nc = bass.Bass("TRN2", target_bir_lowering=False)
    stack = contextlib.ExitStack()

    def din(name, shape, dt=F32):
        return nc.dram_tensor(name, list(shape), dt, kind="ExternalInput").ap()

    def dscr(name, shape, dt=F32):
        return nc.dram_tensor(name, list(shape), dt).ap()

    xT = din("xT", [128, KD, T])
    cT = din("cT", [128, KD, 2])
    wada = din("wada", [L, NMB, 128, KD * 128])
    bada = din("bada", [128, L * NMB * 2])
    n1g = din("n1g", [128, L * KD])
    n2g = din("n2g", [128, L * KD])
    hgog = din("hgog", [128, L * H])
    sbog = din("sbog", [128, L * H])
    sbqg = din("sbqg", [128, L])
    sbkg = din("sbkg", [128, L])
    lbl = din("lbl", [128, L * H])
    win_fm = din("win_fm", [L, 5 * H, 128, KD * 128])
    win_tm = din("win_tm", [L, 2, 128, KD * HW])
    wout = din("wout", [L, KD, 128, KD * 128])
    wfi = din("wfi", [L, NF, 128, 2 * KD * 128])
    wfo = din("wfo", [L, KD, 128, NF * 128])
    flags = din("flags", [128, 32])
    cmask_hg = din("cmask_hg", [128, 128])
    cmask_sb = din("cmask_sb", [128, 128])
    cident = din("cident", [128, 128])
    cscan = din("cscan", [128, min(T, 1024)])
    yT = nc.dram_tensor("yT", [128, KD, T], F32, kind="ExternalOutput").ap()
    dbg_out = {}

    xs = dscr("xs", [128, KD, T])
    d_qa = dscr("d_qa", [H, 128, T])
    d_kk = dscr("d_kk", [H, 128, T])
    d_lf = dscr("d_lf", [H, 128, T])
    d_gs = dscr("d_gs", [H, 128, T], BF16)
    d_vhg = dscr("d_vhg", [T, HW], BF16)
    d_qn = dscr("d_qn", [H, 128, T], BF16)
    d_kloc = [dscr(f"d_kloc{h}", [128, T], BF16) for h in range(H)]
    d_kall = [dscr(f"d_kall{h}", [G * 128, T], BF16) for h in range(H)]
    d_vloc = [dscr(f"d_vloc{h}", [T, 128], BF16) for h in range(H)]
    d_vall = [dscr(f"d_vall{h}", [G * T, 128], BF16) for h in range(H)]
    d_stloc = dscr("d_stloc", [H * 128, 132])
    d_stall = dscr("d_stall", [G * H * 128, 132])
    d_oloc = dscr("d_oloc", [H, 128, T])
    d_qh = dscr("d_qh", [H, 128, T], BF16)
    d_act = dscr("d_act", [NF, 128, T], BF16)
    d_modsrc = dscr("d_modsrc", [128, L * NMB * 2])
    d_modall = dscr("d_modall", [G * 128, L * NMB * 2])

    S = Sched(nc, stack)

    def sb(name, shape, dt=F32):
        return stack.enter_context(nc.sbuf_tensor(name, list(shape), dt))

    def ps(name, shape, dt=F32):
        return stack.enter_context(nc.psum_tensor(name, list(shape), dt))

    HT = sb("HT", [128, KD, T], BF16)
    WRN = max(NF * 128, 2 * KD * 128)
    WR = [sb(f"WR{i}", [128, WRN], BF16) for i in range(2)]
    BIGN = 18432
    BIG = sb("BIG", [128, BIGN], F32)
    BIGB = BIG[:, :].bitcast(BF16)
    NSTG = 4
    STG = [sb(f"STG{i}", [128, 512], F32) for i in range(NSTG)]
    NSTB = 4
    STB = [sb(f"STB{i}", [128, 512], BF16) for i in range(NSTB)]
    RSTD = [sb(f"RSTD{i}", [128, 512], F32) for i in range(2)]
    c_flags = sb("c_flags", [128, 32])
    c_mhg = sb("c_mhg", [128, 128])
    c_msb = sb("c_msb", [128, 128])
    c_idf = sb("c_idf", [128, 128])
    c_id = sb("c_id", [128, 128], BF16)
    c_ones = sb("c_ones", [128, 128], BF16)
    c_scan = sb("c_scan", [128, min(T, 1024)])
    p_n1g = sb("p_n1g", [128, L * KD])
    p_n2g = sb("p_n2g", [128, L * KD])
    p_hgog = sb("p_hgog", [128, L * H])
    p_sbog = sb("p_sbog", [128, L * H])
    p_sbqg = sb("p_sbqg", [128, L])
    p_sbkg = sb("p_sbkg", [128, L])
    p_lbl = sb("p_lbl", [128, L * H])
    p_bada = sb("p_bada", [128, L * NMB * 2])
    p_cT = sb("p_cT", [128, KD * 2])
    p_cond = sb("p_cond", [128, KD * 2], BF16)
    modloc = sb("modloc", [128, L * NMB * 2])
    modall = sb("modall", [128, G, L * NMB * 2])
    modl = sb("modl", [128, L, 6 * KD])
    lb_e = sb("lb_e", [128, L * H])
    lb_s = sb("lb_s", [128, H])
    lb_lb = sb("lb_lb", [128, L * H])
    lb_om = sb("lb_om", [128, L * H])
    lb_nom = sb("lb_nom", [128, L * H])
    lb_tm = sb("lb_tm", [128, L * H])
    colA = sb("colA", [128, 2 * KD])
    colq = sb("colq", [128, 2])
    c_eps = sb("c_eps", [128, 1])
    c_onesf = sb("c_onesf", [128, 1024])
    S32 = sb("S32", [128, 128])
    Sb = sb("Sb", [128, 128], BF16)
    STALL = sb("STALL", [128, G, 132])
    ACC = sb("ACC", [128, 128])
    SIN = sb("SIN", [128, 128])
    DTOT = sb("DTOT", [128, 4])
    RC = sb("RC", [128, 32])
    MD = sb("MD", [128, G, 128])
    NB = sb("NB", [128, 64])
    ATs = [sb(f"AT{i}", [128, 1024], BF16) for i in range(2)]
    As = [sb(f"A{i}", [128, 1024], BF16) for i in range(2)]
    ONb = sb("ONb", [128, 128], BF16)

    PS = [ps(f"PS{i}", [128, 1024]) for i in range(3)]
    PST = ps("PST", [128, 2048], BF16)
    psb = [PS[i // 2][:, (i % 2) * 512:(i % 2) * 512 + 512] for i in range(6)]

    SP, PE, ACT, DVE, POOL = "sp", "pe", "act", "dve", "pool"

    evs = []
    for dst, src in ((c_flags, flags), (c_mhg, cmask_hg), (c_msb, cmask_sb), (c_idf, cident), (c_scan, cscan),
                     (p_n1g, n1g), (p_n2g, n2g), (p_hgog, hgog), (p_sbog, sbog), (p_sbqg, sbqg),
                     (p_sbkg, sbkg), (p_lbl, lbl), (p_bada, bada)):
        evs.append(S.dma(SP, dst[:, :], src[:, :], "ld0"))
    evs.append(S.dma(SP, p_cT[:, :], cT.rearrange("p k b -> p (k b)"), "ld0"))
    ev_x0 = S.dma_split(SP, xs[:, :, :], xT[:, :, :], "ldx", KD)
    ld0 = evs[-1]
    for e in (ACT, DVE, POOL, PE):
        S.wait(e, ld0)
    S.op(DVE, lambda e: e.tensor_copy(out=c_id[:, :], in_=c_idf[:, :]))
    S.op(DVE, lambda e: e.memset(c_ones[:, :], 1.0))
    S.op(DVE, lambda e: e.memset(c_eps[:, :], EPS))
    S.op(DVE, lambda e: e.memset(c_onesf[:, :], 1.0))
    S.op(DVE, lambda e: e.memset(DTOT[:, :], 0.0))
    for o in range(G):
        S.op(DVE, lambda e, o=o: e.tensor_scalar(out=MD[:, o, :], in0=c_msb[:, :], scalar1=c_flags[:, 8 + o:9 + o],
                                                 scalar2=c_flags[:, 4 + o:5 + o], op0=ALU.mult, op1=ALU.add))
    ev_const = S.sig(DVE)
    S.op(ACT, lambda e: e.activation(out=p_cond[:, :], in_=p_cT[:, :], func=AF.Silu))
    ev_cond = S.sig(ACT)
    S.op(ACT, lambda e: e.activation(out=lb_e[:, :], in_=p_lbl[:, :], func=AF.Exp))
    ev = S.sig(ACT)
    S.wait(DVE, ev)
    S.op(DVE, lambda e: e.tensor_copy(out=lb_s[:, :], in_=lb_e[:, 0:H]))
    for l in range(1, L):
        S.wait(DVE, S.sig(DVE))
        S.op(DVE, lambda e, l=l: e.tensor_add(out=lb_s[:, :], in0=lb_s[:, :], in1=lb_e[:, l * H:(l + 1) * H]))
    S.wait(DVE, S.sig(DVE))
    S.op(DVE, lambda e: e.reciprocal(out=lb_s[:, :], in_=lb_s[:, :]))
    S.wait(DVE, S.sig(DVE))
    S.op(DVE, lambda e: e.memset(lb_lb[:, 0:H], 0.0))
    for l in range(1, L):
        S.wait(DVE, S.sig(DVE))
        S.op(DVE, lambda e, l=l: e.tensor_mul(out=lb_e[:, l * H:(l + 1) * H], in0=lb_e[:, l * H:(l + 1) * H], in1=lb_s[:, :]))
        S.wait(DVE, S.sig(DVE))
        S.op(DVE, lambda e, l=l: e.tensor_add(out=lb_lb[:, l * H:(l + 1) * H], in0=lb_lb[:, (l - 1) * H:l * H],
                                              in1=lb_e[:, l * H:(l + 1) * H]))
    S.wait(DVE, S.sig(DVE))
    S.op(DVE, lambda e: e.tensor_scalar(out=lb_om[:, :], in0=lb_lb[:, :], scalar1=-1.0, scalar2=1.0, op0=ALU.mult, op1=ALU.add))
    S.op(DVE, lambda e: e.tensor_scalar(out=lb_tm[:, :], in0=lb_lb[:, :], scalar1=-1.0, scalar2=TINY, op0=ALU.mult, op1=ALU.add))
    S.wait(DVE, S.sig(DVE))
    S.op(DVE, lambda e: e.tensor_scalar(out=lb_nom[:, :], in0=lb_om[:, :], scalar1=-1.0, scalar2=None, op0=ALU.mult))
    ev_lb = S.sig(DVE)

    S.wait(PE, ev_cond)
    wfree = [None, None]
    widx = [0]

    def wload(src_ap, ncols):
        i = widx[0] % 2
        widx[0] += 1
        S.wait(POOL, wfree[i])
        ev = S.dma(POOL, WR[i][:, 0:ncols], src_ap, f"w{i}")
        return i, ev

    pmod = psb[0]
    for l in range(L):
        for jb in range(NMB):
            i, ev = wload(wada[l, jb], KD * 128)
            S.wait(PE, ev)
            col = (l * NMB + jb) * 2
            for kc in range(KD):
                S.op(PE, lambda e, i=i, kc=kc, col=col: e.matmul(
                    pmod[:, col:col + 2], lhsT=WR[i][:, kc * 128:(kc + 1) * 128], rhs=p_cond[:, kc * 2:kc * 2 + 2],
                    start=(kc == 0), stop=(kc == KD - 1)))
            wfree[i] = S.sig(PE)
    S.wait(DVE, S.sig(PE))
    NM = L * NMB * 2
    S.op(DVE, lambda e: e.tensor_add(out=modloc[:, :], in0=pmod[:, 0:NM], in1=p_bada[:, :]))
    ev = S.sig(DVE)
    S.wait(SP, ev)
    ev = S.dma(SP, d_modsrc[:, :], modloc[:, :], "mod")
    S.wait(POOL, ev)
    ev = S.coll("AllGather", [[0, 1, 2, 3], [4, 5, 6, 7]], d_modsrc, d_modall, "cc")
    S.wait(SP, ev)
    ev = S.dma(SP, modall[:, :, :], d_modall.rearrange("(r p) f -> p r f", p=128), "mod")
    S.wait(DVE, ev)
    for r in range(G):
        src = modall[:, r, :].rearrange("p (l j b) -> p l j b", l=L, j=NMB, b=2)
        dst = modl[:, :, r * NMB:(r + 1) * NMB]
        S.op(DVE, lambda e, src=src, dst=dst: e.tensor_scalar(out=dst, in0=src[:, :, :, 0], scalar1=c_flags[:, 24:25],
                                                              scalar2=None, op0=ALU.mult))
        S.wait(DVE, S.sig(DVE))
        S.op(DVE, lambda e, src=src, dst=dst: e.scalar_tensor_tensor(out=dst, in0=src[:, :, :, 1], scalar=c_flags[:, 25:26],
                                                                     in1=dst, op0=ALU.mult, op1=ALU.add))
    ev_mod = S.sig(DVE)
    S.wait(SP, ev_x0)
    S.barrier()

    ones_col = c_ones

    def rstd_from(pss, out, n, width, evp=None):
        S.wait(ACT, evp)
        S.op(ACT, lambda e: e.activation(out=out, in_=pss, func=AF.Ln, scale=1.0 / n, bias=c_eps[:, 0:1]))
        S.wait(ACT, S.sig(ACT))
        S.op(ACT, lambda e: e.activation(out=out, in_=out, func=AF.Exp, scale=-0.5))
        return S.sig(ACT)

    def scan_T(out, d0tile, d1):
        PW = min(T, 1024)
        ev = None
        for pc in range(T // PW):
            a = pc * PW
            S.wait(DVE, ev)
            ini = 0.0 if pc == 0 else out[:, a - 1:a]
            S.op(DVE, lambda e, a=a, ini=ini: e.tensor_tensor_scan(out=out[:, a:a + PW], data0=d0tile[:, 0:PW], data1=d1[:, a:a + PW],
                                                                   initial=ini, op0=ALU.mult, op1=ALU.add))
            ev = S.sig(DVE)
        return ev

    stg_free = [None] * NSTG
    stg_i = [0]
    stb_free = [None] * NSTB
    stb_i = [0]

    def get_stg(eng):
        i = stg_i[0] % NSTG
        stg_i[0] += 1
        S.wait(eng, stg_free[i])
        return i

    def get_stb(eng):
        i = stb_i[0] % NSTB
        stb_i[0] += 1
        S.wait(eng, stb_free[i])
        return i

    psfree = [None] * 6
    psidx = [0]

    def get_ps(k=4):
        i = psidx[0] % k
        psidx[0] += 1
        S.wait(PE, psfree[i])
        return i

    def norm_to_HT(l, which, g_sb):
        base = 0 if which == 0 else 3 * KD
        A = colA[:, which * KD:(which + 1) * KD]
        S.op(DVE, lambda e: e.scalar_tensor_tensor(out=A, in0=modl[:, l, base + KD:base + 2 * KD], scalar=1.0,
                                                   in1=g_sb[:, l * KD:(l + 1) * KD], op0=ALU.add, op1=ALU.mult))
        evA = S.sig(DVE)
        XT = BIG[:, 0:KD * TT].rearrange("p (k t) -> p k t", k=KD)
        SQ = BIGB[:, 2 * KD * TT:2 * KD * TT + KD * TT].rearrange("p (k t) -> p k t", k=KD)
        xt_free = None
        for tt in range(NT):
            ts = slice(tt * TT, (tt + 1) * TT)
            S.wait(SP, xt_free)
            ev = S.dma_split(SP, XT, xs[:, :, ts], "xt", 4)
            S.wait(ACT, ev)
            S.wait(DVE, ev)
            S.op(ACT, lambda e: e.activation(out=SQ, in_=XT, func=AF.Square))
            evq = S.sig(ACT)
            pi = get_ps()
            S.wait(PE, evq, ev_const)
            for kc in range(KD):
                S.op(PE, lambda e, kc=kc, pi=pi: e.matmul(psb[pi][:, 0:TT], lhsT=c_ones[:, :], rhs=SQ[:, kc, :],
                                                          start=(kc == 0), stop=(kc == KD - 1)))
            evp = S.sig(PE)
            rs = RSTD[tt % 2]
            evr = rstd_from(psb[pi][:, 0:TT], rs[:, 0:TT], D, TT, evp)
            psfree[pi] = evr
            S.wait(DVE, evr)
            for kc in range(KD):
                S.op(DVE, lambda e, kc=kc, rs=rs: e.tensor_mul(out=XT[:, kc, :], in0=XT[:, kc, :], in1=rs[:, 0:TT]))
            evm = S.sig(DVE)
            S.wait(ACT, evm, evA, ev_mod)
            for kc in range(KD):
                S.op(ACT, lambda e, kc=kc, ts=ts: e.activation(out=HT[:, kc, ts], in_=XT[:, kc, :], func=AF.Identity,
                                                               scale=A[:, kc:kc + 1],
                                                               bias=modl[:, l, base + kc:base + kc + 1]))
            xt_free = S.sig(ACT)
        return xt_free

    def gemm_fm(wsrc_fn, nblk, nk, rhs_fn, ntt, tw, evac_fn, wcols):
        for blk in range(nblk):
            i, ev = wload(wsrc_fn(blk), wcols)
            S.wait(PE, ev)
            for tt in range(ntt):
                pi = get_ps()
                for kc in range(nk):
                    S.op(PE, lambda e, i=i, kc=kc, tt=tt, pi=pi: e.matmul(
                        psb[pi][:, 0:tw], lhsT=WR[i][:, kc * 128:(kc + 1) * 128], rhs=rhs_fn(kc, tt),
                        start=(kc == 0), stop=(kc == nk - 1)))
                evp = S.sig(PE)
                if tt == ntt - 1:
                    wfree[i] = evp
                psfree[pi] = evac_fn(blk, tt, psb[pi][:, 0:tw], evp)

    for l in range(L):
        evn = norm_to_HT(l, 0, p_n1g)
        S.wait(PE, evn)
        S.op(DVE, lambda e, l=l: e.tensor_scalar(out=colq[:, 0:1], in0=p_sbqg[:, l:l + 1], scalar1=128.0 ** -0.5, scalar2=None, op0=ALU.mult))
        S.op(DVE, lambda e, l=l: e.tensor_copy(out=colq[:, 1:2], in_=p_sbkg[:, l:l + 1]))
        ev_colq = S.sig(DVE)

        def evac_in(blk, tt, pss, evp, l=l):
            kind, h = blk // H, blk % H
            ts = slice(tt * TT, (tt + 1) * TT)
            lh = l * H + h
            if kind == 0:
                si = get_stg(ACT)
                S.wait(ACT, evp)
                S.op(ACT, lambda e: e.activation(out=STG[si][:, 0:TT], in_=pss, func=AF.Silu))
                ev = S.sig(ACT)
                S.wait(SP, ev)
                stg_free[si] = S.dma(SP, d_qa[h, :, ts], STG[si][:, 0:TT], f"sg{si}")
                return ev
            if kind == 1:
                si = get_stb(ACT)
                S.wait(ACT, evp)
                S.op(ACT, lambda e: e.activation(out=STB[si][:, 0:TT], in_=pss, func=AF.Silu))
                ev = S.sig(ACT)
                S.wait(SP, ev)
                stb_free[si] = S.dma(SP, d_gs[h, :, ts], STB[si][:, 0:TT], f"sb{si}")
                return ev
            if kind == 2:
                s0 = get_stg(ACT)
                S.wait(ACT, evp, ev_lb)
                S.op(ACT, lambda e: e.activation(out=STG[s0][:, 0:TT], in_=pss, func=AF.Sigmoid))
                ev0 = S.sig(ACT)
                s1 = get_stg(DVE)
                s2 = get_stg(DVE)
                S.wait(DVE, ev0)
                S.op(DVE, lambda e: e.tensor_scalar(out=STG[s1][:, 0:TT], in0=STG[s0][:, 0:TT], scalar1=lb_om[:, lh:lh + 1],
                                                    scalar2=lb_tm[:, lh:lh + 1], op0=ALU.mult, op1=ALU.max))
                ev1 = S.sig(DVE)
                S.op(DVE, lambda e: e.tensor_scalar(out=STG[s2][:, 0:TT], in0=STG[s0][:, 0:TT], scalar1=lb_nom[:, lh:lh + 1],
                                                    scalar2=lb_om[:, lh:lh + 1], op0=ALU.mult, op1=ALU.add))
                ev2 = S.sig(DVE)
                stg_free[s0] = ev2
                S.wait(SP, ev2)
                stg_free[s2] = S.dma(SP, d_kk[h, :, ts], STG[s2][:, 0:TT], f"sg{s2}")
                S.wait(ACT, ev1)
                S.op(ACT, lambda e: e.activation(out=STG[s1][:, 0:TT], in_=STG[s1][:, 0:TT], func=AF.Ln,
                                                 bias=lb_lb[:, lh:lh + 1], scale=1.0))
                ev3 = S.sig(ACT)
                S.wait(SP, ev3)
                stg_free[s1] = S.dma(SP, d_lf[h, :, ts], STG[s1][:, 0:TT], f"sg{s1}")
                return ev0
            sq = get_stb(ACT)
            S.wait(ACT, evp)
            S.op(ACT, lambda e: e.activation(out=STB[sq][:, 0:TT], in_=pss, func=AF.Square))
            ev0 = S.sig(ACT)
            S.wait(PE, ev0)
            S.wait(PE, psfree[4])
            S.op(PE, lambda e: e.matmul(psb[4][:, 0:TT], lhsT=c_ones[:, :], rhs=STB[sq][:, 0:TT], start=True, stop=True))
            ev1 = S.sig(PE)
            stb_free[sq] = ev1
            rs = RSTD[0]
            evr = rstd_from(psb[4][:, 0:TT], rs[:, 0:TT], 128.0, TT, ev1)
            psfree[4] = evr
            si = get_stg(DVE)
            S.wait(DVE, evr, evp)
            S.op(DVE, lambda e: e.tensor_mul(out=STG[si][:, 0:TT], in0=pss, in1=rs[:, 0:TT]))
            ev2 = S.sig(DVE)
            so = get_stb(ACT)
            S.wait(ACT, ev2, ev_colq)
            S.op(ACT, lambda e: e.activation(out=STB[so][:, 0:TT], in_=STG[si][:, 0:TT], func=AF.Copy,
                                             scale=colq[:, kind - 3:kind - 2]))
            ev3 = S.sig(ACT)
            stg_free[si] = ev3
            S.wait(SP, ev3)
            if kind == 3:
                stb_free[so] = S.dma(SP, d_qn[h, :, ts], STB[so][:, 0:TT], f"sb{so}")
            else:
                stb_free[so] = S.dma(SP, d_kloc[h][:, ts], STB[so][:, 0:TT], f"sb{so}")
            return ev2

        gemm_fm(lambda blk, l=l: win_fm[l, blk], 5 * H, KD, lambda kc, tt: HT[:, kc, tt * TT:(tt + 1) * TT], NT, TT,
                evac_in, KD * 128)

        WV = BIGB[:, 0:KD * HW].rearrange("p (k c) -> p k c", k=KD)
        CW = min(512, HW)
        wv_free = S.sig(ACT)
        for which, dst in ((0, d_vhg), (1, None)):
            S.wait(POOL, wv_free, S.sig(PE) if S.last[PE]["sem"] in (None, PE) else None)
            ev = S.dma_split(POOL, WV, win_tm[l, which].rearrange("p (k c) -> p k c", k=KD), "wv", KD)
            S.wait(PE, ev)
            for tb in range(T // 128):
                for cg in range(HW // CW):
                    pi = get_ps()
                    for kc in range(KD):
                        S.op(PE, lambda e, kc=kc, tb=tb, cg=cg, pi=pi: e.matmul(
                            psb[pi][:, 0:CW], lhsT=HT[:, kc, tb * 128:(tb + 1) * 128], rhs=WV[:, kc, cg * CW:(cg + 1) * CW],
                            start=(kc == 0), stop=(kc == KD - 1)))
                    evp = S.sig(PE)
                    si = get_stb(DVE)
                    S.wait(DVE, evp)
                    S.op(DVE, lambda e, si=si, pi=pi: e.tensor_copy(out=STB[si][:, 0:CW], in_=psb[pi][:, 0:CW]))
                    ev = S.sig(DVE)
                    psfree[pi] = ev
                    S.wait(SP, ev)
                    if dst is not None:
                        stb_free[si] = S.dma(SP, dst[tb * 128:(tb + 1) * 128, cg * CW:(cg + 1) * CW], STB[si][:, 0:CW], f"sb{si}")
                    else:
                        for hh in range(CW // 128):
                            stb_free[si] = S.dma(SP, d_vloc[cg * (CW // 128) + hh][tb * 128:(tb + 1) * 128, :],
                                                 STB[si][:, hh * 128:(hh + 1) * 128], f"sb{si}")
            wv_free = S.sig(PE)
        S.barrier()
        if "B" in dbg and l == 0:
            break

        GR = [[0, 1, 2, 3], [4, 5, 6, 7]]
        ev_kall = [S.coll("AllGather", GR, d_kloc[h], d_kall[h], "cc") for h in range(H)][-1]
        ev_vall = [S.coll("AllGather", GR, d_vloc[h], d_vall[h], "cc") for h in range(H)][-1]

        if "C0" in dbg and l == 0:
            S.barrier()
            break
        QA = BIG[:, 0:T]; KK = BIG[:, T:2 * T]; LF = BIG[:, 2 * T:3 * T]; X1 = BIG[:, 3 * T:4 * T]; X2 = BIG[:, 4 * T:5 * T]
        X3 = LF
        OT = X1
        bo = 10 * T
        QTl = BIGB[:, bo:bo + T]; KTl = BIGB[:, bo + T:bo + 2 * T]; KE = BIGB[:, bo + 2 * T:bo + 3 * T]
        KTOK = BIGB[0:64, bo + 3 * T:bo + 5 * T].rearrange("p (n c) -> p n c", c=128)
        VH = BIGB[0:64, bo + 5 * T:bo + 7 * T].rearrange("p (n c) -> p n c", c=128)
        QH = BIGB[:, bo + 7 * T:bo + 8 * T]
        QI = QH
        assert bo + 8 * T <= 2 * BIGN
        hfree = None
        CPR = min(NCH, 16)
        for h in range(H):
            S.wait(SP, hfree)
            S.dma(SP, QA, d_qa[h], "hg")
            S.dma(SP, KK, d_kk[h], "hg")
            S.dma(SP, LF, d_lf[h], "hg")
            ev = S.dma_split(SP, VH, d_vhg.rearrange("(n p) c -> p n c", p=64)[:, :, h * 128:(h + 1) * 128], "hg", 4)
            S.wait(DVE, ev); S.wait(ACT, ev); S.wait(PE, ev)
            ev = scan_T(X1, c_onesf, LF)
            S.wait(ACT, ev)
            S.op(ACT, lambda e: e.activation(out=X1, in_=X1, func=AF.Exp))
            ev = S.sig(ACT)
            S.wait(DVE, ev)
            S.op(DVE, lambda e: e.tensor_mul(out=QH, in0=QA, in1=X1))
            S.op(DVE, lambda e: e.tensor_copy(out=DTOT[:, 0:1], in_=X1[:, T - 1:T]))
            ev = S.sig(DVE)
            S.wait(SP, ev)
            S.dma(SP, d_qh[h], QH, "hgo")
            ev_qh = S.dma(SP, d_stloc[h * 128:(h + 1) * 128, 128:132], DTOT[:, 0:4], "hgo")
            S.wait(DVE, ev)
            ev = scan_T(X1, c_scan, LF)
            S.wait(ACT, ev)
            S.op(ACT, lambda e: e.activation(out=X2, in_=X1, func=AF.Exp))
            ev_x2 = S.sig(ACT)
            S.wait(DVE, ev_x2)
            S.op(DVE, lambda e: e.tensor_copy(out=RC[:, 0:NCH], in_=X1.rearrange("p (n c) -> p n c", c=64)[:, :, 31]))
            S.wait(DVE, S.sig(DVE))
            for c in range(NCH):
                S.op(DVE, lambda e, c=c: e.tensor_scalar(out=X1[:, c * 64:(c + 1) * 64], in0=X1[:, c * 64:(c + 1) * 64],
                                                         scalar1=RC[:, c:c + 1], scalar2=None, op0=ALU.subtract))
            ev = S.sig(DVE)
            S.wait(ACT, ev)
            S.op(ACT, lambda e: e.activation(out=X3, in_=X1, func=AF.Exp))
            S.wait(ACT, S.sig(ACT))
            S.op(ACT, lambda e: e.activation(out=X1, in_=X1, func=AF.Exp, scale=-1.0))
            ev = S.sig(ACT)
            S.wait(DVE, ev, ev_qh)
            S.op(DVE, lambda e: e.tensor_mul(out=QTl, in0=QA, in1=X3))
            S.op(DVE, lambda e: e.tensor_mul(out=KTl, in0=KK, in1=X1))
            S.op(DVE, lambda e: e.tensor_mul(out=QI, in0=QA, in1=X2))
            S.wait(DVE, S.sig(DVE))
            for c in range(NCH):
                S.op(DVE, lambda e, c=c: e.tensor_scalar(out=KE[:, c * 64:(c + 1) * 64], in0=KTl[:, c * 64:(c + 1) * 64],
                                                         scalar1=X3[:, c * 64 + 63:c * 64 + 64], scalar2=None, op0=ALU.mult))
            S.op(DVE, lambda e: e.memset(S32[:, :], 0.0))
            S.op(DVE, lambda e: e.memset(Sb[:, :], 0.0))
            ev_prep = S.sig(DVE)
            S.wait(PE, ev_prep)
            ev_ktok = None
            for rnd in range(NCH // CPR):
                S.wait(PE, ev_ktok)
                for cc in range(CPR):
                    c = rnd * CPR + cc
                    S.op(PE, lambda e, c=c, cc=cc: e.transpose(out=PST[0:64, cc * 128:(cc + 1) * 128], in_=KE[:, c * 64:(c + 1) * 64], identity=c_id[:, :]))
                ev = S.sig(PE)
                S.wait(ACT, ev)
                S.op(ACT, lambda e, rnd=rnd: e.activation(out=KTOK[:, rnd * CPR:(rnd + 1) * CPR, :],
                                                          in_=PST[0:64, 0:CPR * 128].rearrange("p (n c) -> p n c", c=128), func=AF.Copy))
                ev_ktok = S.sig(ACT)
            S.wait(PE, ev_ktok)
            sc_free = [None, None]
            sct_free = [None, None]
            scs = [STB[0], STB[1]]
            u_free = [None, None]
            st_ev = ev_prep
            ev_s = ev_prep
            NBLK = T // 128
            for tb in range(NBLK):
                k2 = tb % 2
                pscore = psb[2][0:64, k2 * 128:(k2 + 1) * 128]
                S.wait(PE, sc_free[k2])
                for half in range(2):
                    b0 = tb * 128 + half * 64
                    pc = pscore[:, half * 64:(half + 1) * 64]
                    S.op(PE, lambda e, b0=b0, pc=pc: e.matmul(pc[0:32, 0:32], lhsT=KTl[:, b0:b0 + 32], rhs=QTl[:, b0:b0 + 32], start=True, stop=True))
                    S.op(PE, lambda e, b0=b0, pc=pc: e.matmul(pc[0:64, 32:64], lhsT=KTl[:, b0:b0 + 64], rhs=QTl[:, b0 + 32:b0 + 64], start=True, stop=True))
                ev = S.sig(PE)
                S.wait(DVE, ev)
                sct = scs[k2][0:64, 0:128]
                S.wait(DVE, sct_free[k2])
                S.op(DVE, lambda e, sct=sct, pscore=pscore: e.tensor_mul(out=sct, in0=pscore, in1=c_mhg[0:64, :]))
                ev = S.sig(DVE)
                S.wait(PE, ev)
                pot = psb[(tb // 4) % 2][:, (tb % 4) * 128:(tb % 4) * 128 + 128]
                if tb % 4 == 0:
                    S.wait(PE, psfree[(tb // 4) % 2])
                for half in range(2):
                    c = tb * 2 + half
                    cs = slice(c * 64, (c + 1) * 64)
                    hs = slice(half * 64, (half + 1) * 64)
                    S.op(PE, lambda e, c=c, sct=sct, pot=pot, hs=hs: e.matmul(pot[:, hs], lhsT=VH[:, c, :], rhs=sct[:, hs], start=True, stop=False))
                    if half == 1:
                        sct_free[k2] = S.sig(PE)
                    S.wait(PE, st_ev)
                    S.op(PE, lambda e, cs=cs, pot=pot, hs=hs: e.matmul(pot[:, hs], lhsT=Sb[:, :], rhs=QI[:, cs], start=False, stop=True))
                    ev_o = S.sig(PE)
                    pu = psb[3][:, (c % 2) * 128:(c % 2) * 128 + 128]
                    S.wait(PE, u_free[c % 2])
                    S.op(PE, lambda e, c=c, pu=pu: e.matmul(pu, lhsT=KTOK[:, c, :], rhs=VH[:, c, :], start=True, stop=True))
                    ev_u = S.sig(PE)
                    S.wait(DVE, ev_u, ev_o)
                    S.op(DVE, lambda e, c=c, pu=pu: e.scalar_tensor_tensor(out=S32[:, :], in0=S32[:, :], scalar=X2[:, c * 64 + 63:c * 64 + 64],
                                                                           in1=pu, op0=ALU.mult, op1=ALU.add))
                    ev_s = S.sig(DVE)
                    u_free[c % 2] = ev_s
                    S.wait(ACT, ev_s)
                    S.op(ACT, lambda e: e.activation(out=Sb[:, :], in_=S32[:, :], func=AF.Copy))
                    st_ev = S.sig(ACT)
                sc_free[k2] = ev_o
                if tb % 4 == 3 or tb == NBLK - 1:
                    nb_ = tb % 4 + 1
                    t0 = (tb - nb_ + 1) * 128
                    pbank = psb[(tb // 4) % 2][:, 0:nb_ * 128]
                    S.wait(ACT, ev_o)
                    S.op(ACT, lambda e, t0=t0, nb_=nb_, pbank=pbank: e.activation(out=OT[:, t0:t0 + nb_ * 128], in_=pbank, func=AF.Copy))
                    psfree[(tb // 4) % 2] = S.sig(ACT)
            ev = S.sig(ACT)
            S.wait(SP, ev, st_ev, ev_s)
            S.dma(SP, d_oloc[h], OT, "hgo")
            ev = S.dma(SP, d_stloc[h * 128:(h + 1) * 128, 0:128], S32[:, :], "hgo")
            hfree = [ev, S.sig(PE)]
        S.wait(POOL, hfree)
        ev_stall = S.coll("AllGather", GR, d_stloc, d_stall, "cc")
        S.barrier()

        if "C1" in dbg and l == 0:
            break
        O2 = BIG[:, 0:T]; QH2 = BIGB[:, 2 * T:3 * T]; GS2 = BIGB[:, 3 * T:4 * T]
        c2free = None
        for h in range(H):
            S.wait(SP, c2free)
            S.dma(SP, O2, d_oloc[h], "c2")
            S.dma(SP, QH2, d_qh[h], "c2")
            S.dma(SP, GS2, d_gs[h], "c2")
            ev = S.dma(SP, STALL[:, :, :], d_stall.rearrange("(g hh p) c -> hh p g c", g=G, hh=H)[h], "c2")
            S.wait(DVE, ev); S.wait(PE, ev); S.wait(ACT, ev)
            S.op(DVE, lambda e: e.memset(ACC[:, :], 0.0))
            S.op(DVE, lambda e: e.memset(SIN[:, :], 0.0))
            for i in range(G):
                S.wait(DVE, S.sig(DVE))
                S.op(DVE, lambda e, i=i: e.scalar_tensor_tensor(out=ACC[:, :], in0=ACC[:, :], scalar=STALL[:, i, 128:129], in1=STALL[:, i, 0:128],
                                                                op0=ALU.mult, op1=ALU.add))
                S.wait(DVE, S.sig(DVE))
                S.op(DVE, lambda e, i=i: e.scalar_tensor_tensor(out=SIN[:, :], in0=ACC[:, :], scalar=c_flags[:, 12 + i:13 + i], in1=SIN[:, :],
                                                                op0=ALU.mult, op1=ALU.add))
            S.wait(DVE, S.sig(DVE))
            S.op(DVE, lambda e: e.tensor_copy(out=Sb[:, :], in_=SIN[:, :]))
            ev = S.sig(DVE)
            S.wait(PE, ev)
            for tt in range(NT):
                ts = slice(tt * TT, (tt + 1) * TT)
                pi = get_ps()
                S.op(PE, lambda e, pi=pi, ts=ts: e.matmul(psb[pi][:, 0:TT], lhsT=Sb[:, :], rhs=QH2[:, ts], start=True, stop=True))
                evp = S.sig(PE)
                S.wait(DVE, evp)
                S.op(DVE, lambda e, pi=pi, ts=ts: e.tensor_add(out=O2[:, ts], in0=psb[pi][:, 0:TT], in1=O2[:, ts]))
                ev = S.sig(DVE)
                psfree[pi] = ev
                sq = get_stb(ACT)
                S.wait(ACT, ev)
                S.op(ACT, lambda e, sq=sq, ts=ts: e.activation(out=STB[sq][:, 0:TT], in_=O2[:, ts], func=AF.Square))
                ev0 = S.sig(ACT)
                S.wait(PE, ev0, psfree[4])
                S.op(PE, lambda e, sq=sq: e.matmul(psb[4][:, 0:TT], lhsT=c_ones[:, :], rhs=STB[sq][:, 0:TT], start=True, stop=True))
                ev1 = S.sig(PE)
                stb_free[sq] = ev1
                rs = RSTD[tt % 2]
                evr = rstd_from(psb[4][:, 0:TT], rs[:, 0:TT], 128.0, TT, ev1)
                psfree[4] = evr
                S.wait(DVE, evr)
                S.op(DVE, lambda e, ts=ts, rs=rs: e.tensor_mul(out=O2[:, ts], in0=O2[:, ts], in1=rs[:, 0:TT]))
                S.wait(DVE, S.sig(DVE))
                S.op(DVE, lambda e, ts=ts: e.tensor_mul(out=O2[:, ts], in0=O2[:, ts], in1=GS2[:, ts]))
                ev = S.sig(DVE)
                S.wait(ACT, ev)
                S.op(ACT, lambda e, ts=ts, h=h, l=l: e.activation(out=HT[:, h, ts], in_=O2[:, ts], func=AF.Copy,
                                                                  scale=p_hgog[:, l * H + h:l * H + h + 1]))
            ev = S.sig(ACT)
            c2free = [ev, S.sig(PE)]
        S.barrier()
        if "C" in dbg and l == 0:
            break

        NSB = KT // 128
        KTs = BIGB[:, 0:G * T].rearrange("p (g t) -> p g t", g=G)
        Vs = BIGB[:, G * T:2 * G * T].rearrange("p (n c) -> p n c", c=128)
        Qs = BIGB[:, 2 * G * T:2 * G * T + T]
        f0 = (2 * G * T + T + 1) // 2

        def ftile(k):
            return BIG[:, f0 + k * (KT + 4):f0 + k * (KT + 4) + KT + 4]
        assert f0 + 6 * (KT + 4) <= BIGN, (f0, KT, BIGN)
        E_ = [ftile(0), ftile(1)]; SP_ = [ftile(2), ftile(3)]; C_ = [ftile(4), ftile(5)]
        S.op(DVE, lambda e: e.memset(C_[0][:, 0:1], 0.0))
        S.op(DVE, lambda e: e.memset(C_[1][:, 0:1], 0.0))
        ev_c0 = S.sig(DVE)
        S.wait(DVE, ev_c0)
        hd_free = None
        e_free = [None, None]; sp_free = [None, None]; c_free = [None, None]; a_free = [None, None]; at_free = [None, None]
        z_free = [None, None]; pt_free = [None, None]; po_free = [None, None]; on_free = [None]
        nbk = [0]
        pos = [PS[2][:, 512:640], PS[2][:, 640:768]]
        tiles = [(o, kt) for o in reversed(range(G)) for kt in reversed(range(NKT))]
        ntl = len(tiles)
        for h in range(H):
            S.wait(SP, hd_free, ev_kall, ev_vall)
            S.dma_split(SP, KTs, d_kall[h].rearrange("(g d) t -> d g t", g=G), "at", G)
            S.dma_split(SP, Vs, d_vall[h].rearrange("(n p) c -> p n c", p=128), "at", 8)
            ev_ld = S.dma(SP, Qs, d_qn[h], "at")
            S.wait(PE, ev_ld)
            jobs = [(qb, ti) for qb in range(NQB) for ti in range(ntl)]
            info = {}
            ncar_box = [None]

            def stageA1pe(jn, h=h):
                qb, ti = jobs[jn]
                o, kt = tiles[ti]
                qs = slice(qb * 128, (qb + 1) * 128)
                m = jn % 2
                pz = PS[m][:, 0:KT]
                S.wait(PE, z_free[m])
                w = min(512, KT)
                for c5 in range(KT // w):
                    S.op(PE, lambda e, c5=c5: e.matmul(pz[:, c5 * w:(c5 + 1) * w], lhsT=Qs[:, qs],
                                                       rhs=KTs[:, o, kt * KT + c5 * w:kt * KT + (c5 + 1) * w], start=True, stop=True))
                info[jn] = dict(m=m, o=o, kt=kt, qb=qb, ti=ti, pz=pz, ev_z=S.sig(PE))

            def stageA1(jn, h=h):
                d = info[jn]
                m, o, kt, qb, ti, pz, ev_z = d["m"], d["o"], d["kt"], d["qb"], d["ti"], d["pz"], d["ev_z"]
                Et, SPt = E_[m], SP_[m]
                S.wait(ACT, ev_z, e_free[m])
                S.op(ACT, lambda e: e.activation(out=Et[:, 0:KT], in_=pz, func=AF.Exp))
                ev_e = S.sig(ACT)
                kb0 = kt * NSB
                nbef = min(max(qb - kb0, 0), NSB)
                hasd = (kb0 <= qb < kb0 + NSB)
                naft = NSB - nbef - (1 if hasd else 0)
                rb = (0, nbef * 128)
                rd = (nbef * 128, nbef * 128 + 128) if hasd else None
                ra = ((nbef + (1 if hasd else 0)) * 128, KT)
                ev_em = None
                if rd is not None:
                    S.wait(DVE, ev_e)
                    S.op(DVE, lambda e: e.tensor_mul(out=Et[:, rd[0]:rd[1]], in0=Et[:, rd[0]:rd[1]], in1=MD[:, o, :]))
                    ev_em = S.sig(DVE)
                S.wait(ACT, ev_e, sp_free[m])
                if nbef > 0:
                    S.op(ACT, lambda e: e.activation(out=SPt[:, rb[0]:rb[1]], in_=Et[:, rb[0]:rb[1]], func=AF.Ln,
                                                     scale=c_flags[:, o:o + 1], bias=c_onesf[:, 0:1]))
                if naft > 0:
                    S.op(ACT, lambda e: e.activation(out=SPt[:, ra[0]:ra[1]], in_=Et[:, ra[0]:ra[1]], func=AF.Ln,
                                                     scale=c_flags[:, 4 + o:5 + o], bias=c_onesf[:, 0:1]))
                if rd is not None:
                    S.wait(ACT, ev_em)
                    S.op(ACT, lambda e: e.activation(out=SPt[:, rd[0]:rd[1]], in_=Et[:, rd[0]:rd[1]], func=AF.Ln,
                                                     scale=1.0, bias=c_onesf[:, 0:1]))
                ev_sp = S.sig(ACT)
                d.update(rb=rb, rd=rd, ra=ra, nbef=nbef, naft=naft, ev_sp=ev_sp)

            def stageA2(jn):
                d = info[jn]
                m, o, pz = d["m"], d["o"], d["pz"]
                SPt, Ct = SP_[m], C_[m]
                S.wait(DVE, d["ev_sp"], c_free[m])
                S.op(DVE, lambda e: e.tensor_tensor_scan(out=Ct[:, 1:KT + 1], data0=c_onesf[:, 0:KT], data1=SPt[:, 0:KT],
                                                         initial=0.0, op0=ALU.mult, op1=ALU.add))
                ev_c = S.sig(DVE)
                S.wait(DVE, ev_c)
                k3 = (nbk[0] % 16) * 3
                nbk[0] += 1
                nbc, nb1, nb2 = NB[:, k3:k3 + 1], NB[:, k3 + 1:k3 + 2], NB[:, k3 + 2:k3 + 3]
                ncar = ncar_box[0] if d["ti"] > 0 else None
                S.op(DVE, lambda e: e.tensor_add(out=SPt[:, 0:KT], in0=pz, in1=Ct[:, 0:KT]))
                ev_u = S.sig(DVE)
                if ncar is None:
                    S.op(DVE, lambda e: e.tensor_scalar(out=nbc, in0=Ct[:, KT:KT + 1], scalar1=-1.0, scalar2=None, op0=ALU.mult))
                else:
                    S.op(DVE, lambda e: e.tensor_sub(out=nbc, in0=ncar, in1=Ct[:, KT:KT + 1]))
                S.wait(DVE, S.sig(DVE))
                S.op(DVE, lambda e: e.tensor_add(out=nb1, in0=nbc, in1=c_flags[:, 16 + o:17 + o]))
                S.op(DVE, lambda e: e.tensor_add(out=nb2, in0=nbc, in1=c_flags[:, 20 + o:21 + o]))
                ev_nb = S.sig(DVE)
                ncar_box[0] = nbc
                z_free[m] = ev_u
                c_free[m] = ev_nb
                d.update(nbc=nbc, nb1=nb1, nb2=nb2, ev_u=[ev_u, ev_nb])

            def stageB1(jn, h=h):
                d = info[jn]
                m, o, kt, rb, rd, ra, qb, ti = d["m"], d["o"], d["kt"], d["rb"], d["rd"], d["ra"], d["qb"], d["ti"]
                nbc, nb1, nb2 = d["nbc"], d["nb1"], d["nb2"]
                Et, SPt, At = E_[m], SP_[m], As[m]
                S.wait(ACT, d["ev_u"], a_free[m])
                if d["nbef"] > 0:
                    S.op(ACT, lambda e: e.activation(out=At[:, rb[0]:rb[1]], in_=SPt[:, rb[0]:rb[1]], func=AF.Exp, bias=nb1))
                if d["naft"] > 0:
                    S.op(ACT, lambda e: e.activation(out=At[:, ra[0]:ra[1]], in_=SPt[:, ra[0]:ra[1]], func=AF.Exp, bias=nb2))
                if rd is not None:
                    S.op(ACT, lambda e: e.activation(out=Et[:, rd[0]:rd[1]], in_=SPt[:, rd[0]:rd[1]], func=AF.Exp, bias=nbc))
                ev_a = S.sig(ACT)
                evs_a = [ev_a]
                if rd is not None:
                    S.wait(DVE, ev_a)
                    S.op(DVE, lambda e: e.tensor_mul(out=At[:, rd[0]:rd[1]], in0=Et[:, rd[0]:rd[1]], in1=MD[:, o, :]))
                    evs_a.append(S.sig(DVE))
                e_free[m] = [d["ev_sp"]] + evs_a
                sp_free[m] = evs_a
                pT = PST[:, m * 1024:m * 1024 + KT]
                S.wait(PE, evs_a, pt_free[m])
                for sbk in range(NSB):
                    S.op(PE, lambda e, sbk=sbk: e.transpose(out=pT[:, sbk * 128:(sbk + 1) * 128], in_=At[:, sbk * 128:(sbk + 1) * 128],
                                                            identity=c_id[:, :]))
                ev_t = S.sig(PE)
                a_free[m] = ev_t
                d.update(ev_t=ev_t, pT=pT)

            def stageB2(jn, h=h):
                d = info.pop(jn)
                m, o, kt, qb, ti, pT, ev_t = d["m"], d["o"], d["kt"], d["qb"], d["ti"], d["pT"], d["ev_t"]
                ATt = ATs[m]
                po = pos[qb % 2]
                S.wait(ACT, ev_t, at_free[m])
                S.op(ACT, lambda e: e.activation(out=ATt[:, 0:KT], in_=pT, func=AF.Copy))
                ev_at = S.sig(ACT)
                pt_free[m] = ev_at
                S.wait(PE, ev_at)
                if ti == 0:
                    S.wait(PE, po_free[qb % 2])
                for sbk in range(NSB):
                    blk = (o * T + kt * KT) // 128 + sbk
                    S.op(PE, lambda e, sbk=sbk, blk=blk, first=(ti == 0 and sbk == 0), last=(ti == ntl - 1 and sbk == NSB - 1):
                         e.matmul(po, lhsT=ATt[:, sbk * 128:(sbk + 1) * 128], rhs=Vs[:, blk, :], start=first, stop=last))
                at_free[m] = S.sig(PE)
                if ti == ntl - 1:
                    epilogue(qb, at_free[m], h)

            def epilogue(qb, ev_po, h, l=l):
                qs = slice(qb * 128, (qb + 1) * 128)
                po = pos[qb % 2]
                k3 = (nbk[0] % 16) * 3
                nbk[0] += 1
                ssq, rsd = NB[:, k3:k3 + 1], NB[:, k3 + 1:k3 + 2]
                S.op(DVE, lambda e: e.memset(ssq, 0.0))
                S.wait(ACT, S.sig(DVE), ev_po, on_free[0])
                S.op(ACT, lambda e: e.activation(out=ONb[:, :], in_=po, func=AF.Square, accum_out=ssq))
                S.wait(ACT, S.sig(ACT))
                S.op(ACT, lambda e: e.activation(out=rsd, in_=ssq, func=AF.Ln, scale=1.0 / 128.0, bias=c_eps[:, 0:1]))
                S.wait(ACT, S.sig(ACT))
                S.op(ACT, lambda e: e.activation(out=rsd, in_=rsd, func=AF.Exp, scale=-0.5))
                S.wait(ACT, S.sig(ACT))
                S.op(ACT, lambda e: e.activation(out=ONb[:, :], in_=po, func=AF.Copy, scale=rsd))
                ev_on = S.sig(ACT)
                po_free[qb % 2] = ev_on
                pT = PST[:, 0:128]
                S.wait(PE, ev_on, pt_free[0])
                S.op(PE, lambda e: e.transpose(out=pT, in_=ONb[:, :], identity=c_id[:, :]))
                ev_tr = S.sig(PE)
                on_free[0] = ev_tr
                S.wait(ACT, ev_tr)
                S.op(ACT, lambda e: e.activation(out=HT[:, H + h, qs], in_=pT, func=AF.Copy,
                                                 scale=p_sbog[:, l * H + h:l * H + h + 1]))
                pt_free[0] = S.sig(ACT)

            nj = len(jobs)
            for step in range(-1, nj + 2):
                if 0 <= step - 2 < nj:
                    stageB1(step - 2)
                if 0 <= step - 1 < nj:
                    stageA2(step - 1)
                if 0 <= step + 1 < nj:
                    stageA1pe(step + 1)
                if 0 <= step < nj:
                    stageA1(step)
                if 0 <= step - 2 < nj:
                    stageB2(step - 2)
            hd_free = [S.sig(PE), S.sig(ACT)]
        S.barrier()
        if "D" in dbg and l == 0:
            break

        def evac_res(gcol0, l=l):
            def f(blk, tt, pss, evp):
                ts = slice(tt * TT, (tt + 1) * TT)
                si = get_stg(SP)
                ev = S.dma(SP, STG[si][:, 0:TT], xs[:, blk, ts], f"sg{si}")
                S.wait(DVE, ev, evp)
                S.op(DVE, lambda e: e.scalar_tensor_tensor(out=STG[si][:, 0:TT], in0=pss, scalar=modl[:, l, gcol0 + blk:gcol0 + blk + 1],
                                                           in1=STG[si][:, 0:TT], op0=ALU.mult, op1=ALU.add))
                ev = S.sig(DVE)
                S.wait(SP, ev)
                stg_free[si] = S.dma(SP, xs[:, blk, ts], STG[si][:, 0:TT], f"sg{si}")
                return ev
            return f
        gemm_fm(lambda blk, l=l: wout[l, blk], KD, KD, lambda kc, tt: HT[:, kc, tt * TT:(tt + 1) * TT], NT, TT,
                evac_res(2 * KD), KD * 128)
        S.barrier()
        if "E" in dbg and l == 0:
            break

        evn = norm_to_HT(l, 1, p_n2g)
        S.wait(PE, evn)
        gate_sb = {}

        def evac_ffn_in(blk, tt, pss, evp, l=l):
            j, which = blk // 2, blk % 2
            ts = slice(tt * TT, (tt + 1) * TT)
            if which == 0:
                si = get_stg(ACT)
                S.wait(ACT, evp)
                S.op(ACT, lambda e: e.activation(out=STG[si][:, 0:TT], in_=pss, func=AF.Silu))
                ev = S.sig(ACT)
                gate_sb[(j, tt)] = (si, ev)
                return ev
            si, evg = gate_sb.pop((j, tt))
            so = get_stb(DVE)
            S.wait(DVE, evg, evp)
            S.op(DVE, lambda e: e.tensor_mul(out=STB[so][:, 0:TT], in0=pss, in1=STG[si][:, 0:TT]))
            ev = S.sig(DVE)
            stg_free[si] = ev
            S.wait(SP, ev)
            stb_free[so] = S.dma(SP, d_act[j, :, ts], STB[so][:, 0:TT], f"sb{so}")
            return ev

        gemm_fm(lambda blk, l=l: wfi[l, blk // 2, :, (blk % 2) * KD * 128:(blk % 2 + 1) * KD * 128], 2 * NF, KD,
                lambda kc, tt: HT[:, kc, tt * TT:(tt + 1) * TT], NT, TT, evac_ffn_in, KD * 128)
        S.barrier()
        ACTT = BIGB[:, 0:NF * TT].rearrange("p (k t) -> p k t", k=NF)
        assert NF * TT <= 2 * BIGN
        act_free = None
        for tt in range(NT):
            ts = slice(tt * TT, (tt + 1) * TT)
            S.wait(SP, act_free)
            ev_act = S.dma_split(SP, ACTT, d_act[:, :, ts].rearrange("k p t -> p k t"), "actt", 11)
            S.wait(PE, ev_act)
            er = evac_res(5 * KD)
            for blk in range(KD):
                i, ev = wload(wfo[l, blk], NF * 128)
                S.wait(PE, ev)
                pi = get_ps()
                for kc in range(NF):
                    S.op(PE, lambda e, i=i, kc=kc, pi=pi: e.matmul(psb[pi][:, 0:TT], lhsT=WR[i][:, kc * 128:(kc + 1) * 128], rhs=ACTT[:, kc, :],
                                                                   start=(kc == 0), stop=(kc == NF - 1)))
                evp = S.sig(PE)
                wfree[i] = evp
                psfree[pi] = er(blk, tt, psb[pi][:, 0:TT], evp)
            act_free = S.sig(PE)
        S.barrier()
    scr = dict(d_qa=d_qa, d_kk=d_kk, d_lf=d_lf, d_gs=d_gs, d_vhg=d_vhg, d_qn=d_qn, d_stloc=d_stloc, d_stall=d_stall, d_oloc=d_oloc, d_qh=d_qh, d_act=d_act,
               d_modall=d_modall)
    if "HT" in dbg:
        o = nc.dram_tensor("dbg_HT", [128, KD, T], BF16, kind="ExternalOutput").ap()
        S.dma(SP, o, HT[:, :, :], "dbgout")
    for name in dbg:
        if name in scr:
            src = scr[name]
            o = nc.dram_tensor("dbg_" + name, list(src.shape), src.dtype, kind="ExternalOutput").ap()
            S.dma(SP, o, src, "dbgout")
    ev = S.dma_split(SP, yT[:, :, :], xs[:, :, :], "out", KD)
    S.wait(SP, ev)
    S.barrier()
    with nc.Block() as block:
        S.emit(block)
    stack.close()
    return nc, S


def _bf(a):
    return a


def prep_inputs(cfg, inp):
    D, T, L, KD, H, NF, NMB = cfg.D, cfg.T, cfg.L, cfg.KD, cfg.H, cfg.NF, cfg.NMB
    HW = H * 128
    f = lambda a: np.ascontiguousarray(a, dtype=np.float32)
    x = np.asarray(inp["x"], np.float32)
    c = np.asarray(inp["c"], np.float32)

    def colvec(a, n):
        a = np.asarray(a, np.float32).reshape(L, n, 128)
        return f(a.transpose(2, 0, 1).reshape(128, L * n))

    w_in = np.asarray(inp["w_in"], np.float32)
    segs = [0, 3, 1, 4, 5]
    fm = np.concatenate([w_in[:, :, s * HW:(s + 1) * HW] for s in segs], axis=2)
    fm = fm.reshape(L, KD, 128, 5 * H, 128).transpose(0, 3, 2, 1, 4)
    win_fm = f(fm.reshape(L, 5 * H, 128, KD * 128))
    tm = np.stack([w_in[:, :, 2 * HW:3 * HW], w_in[:, :, 6 * HW:7 * HW]], axis=1)
    tm = tm.reshape(L, 2, KD, 128, HW).transpose(0, 1, 3, 2, 4)
    win_tm = f(tm.reshape(L, 2, 128, KD * HW))
    wo = np.asarray(inp["w_out"], np.float32).reshape(L, KD, 128, KD, 128).transpose(0, 3, 2, 1, 4)
    wout = f(wo.reshape(L, KD, 128, KD * 128))
    wi = np.asarray(inp["w_ffn_in"], np.float32).reshape(L, KD, 128, 2, NF, 128).transpose(0, 4, 2, 3, 1, 5)
    wfi = f(wi.reshape(L, NF, 128, 2 * KD * 128))
    wf = np.asarray(inp["w_ffn_out"], np.float32).reshape(L, NF, 128, KD, 128).transpose(0, 3, 2, 1, 4)
    wfo = f(wf.reshape(L, KD, 128, NF * 128))
    wa = np.asarray(inp["w_ada"], np.float32).reshape(L, KD, 128, 6 * KD, 128).transpose(0, 3, 2, 1, 4)
    ba = np.asarray(inp["b_ada"], np.float32).reshape(L, 6 * KD, 128)
    cTt = f(c.T.reshape(KD, 128, 2).transpose(1, 0, 2))
    common = dict(
        cT=cTt, n1g=colvec(inp["norm1_g"], KD), n2g=colvec(inp["norm2_g"], KD),
        hgog=colvec(inp["hg_out_g"], H), sbog=colvec(inp["sb_out_g"], H),
        sbqg=colvec(inp["sb_q_g"], 1), sbkg=colvec(inp["sb_k_g"], 1), lbl=colvec(inp["hg_lb_logits"], H),
        win_fm=win_fm, win_tm=win_tm, wout=wout, wfi=wfi, wfo=wfo,
    )
    s_idx = np.arange(128)
    mhg = ((s_idx[:, None] <= (s_idx[None, :] % 64)) & (s_idx[:, None] < 64)).astype(np.float32)
    msb = (s_idx[None, :] < s_idx[:, None]).astype(np.float32)
    ident = np.eye(128, dtype=np.float32)
    scan = np.ones((128, min(T, 1024)), np.float32)
    scan[:, ::64] = 0.0
    maps = []
    for r in range(NCORES):
        b, j = r // G, r % G
        xt = x[b, j * T:(j + 1) * T, :].T.reshape(KD, 128, T).transpose(1, 0, 2)
        fl = np.zeros((128, 32), np.float32)
        for o in range(G):
            fl[:, o] = 1.0 if o <= j else 0.0
            fl[:, 4 + o] = 1.0 if o < j else 0.0
            fl[:, 8 + o] = 1.0 if o == j else 0.0
            fl[:, 12 + o] = 1.0 if o == j - 1 else 0.0
            fl[:, 16 + o] = 0.0 if o <= j else BIGNEG
            fl[:, 20 + o] = 0.0 if o < j else BIGNEG
        fl[:, 24] = 1.0 if b == 0 else 0.0
        fl[:, 25] = 1.0 if b == 1 else 0.0
        wad = f(wa[:, j * NMB:(j + 1) * NMB].reshape(L, NMB, 128, KD * 128))
        bad = ba[:, j * NMB:(j + 1) * NMB, :].transpose(2, 0, 1)
        bad = f(np.repeat(bad[:, :, :, None], 2, axis=3).reshape(128, L * NMB * 2))
        m = dict(common)
        m.update(xT=f(xt), wada=wad, bada=bad, flags=fl, cmask_hg=mhg, cmask_sb=msb, cident=ident, cscan=scan)
        maps.append(m)
    return maps


def assemble(cfg, results):
    D, T, KD = cfg.D, cfg.T, cfg.KD
    out = np.zeros((2, cfg.S, D), np.float32)
    for r in range(NCORES):
        b, j = r // G, r % G
        y = np.asarray(results[r]["yT"], np.float32)
        out[b, j * T:(j + 1) * T, :] = y.transpose(1, 0, 2).reshape(D, T).T
    return out


_CACHE = {}


def kernel(**inputs):
    x = inputs["x"]
    B, Sq, D = x.shape
    L = inputs["w_in"].shape[0]
    DFF = inputs["w_ffn_out"].shape[1]
    cfg = Cfg(D, Sq, L, DFF)
    key = (D, Sq, L, DFF)
    if key not in _CACHE:
        _CACHE[key] = build(cfg)[0]
    nc = _CACHE[key]
    maps = prep_inputs(cfg, inputs)
    res = run_bass_kernel_spmd(nc, maps, core_ids=list(range(NCORES)))
    return assemble(cfg, res.results)
```

```python
import contextlib
import numpy as np
import concourse.bass as bass
import concourse.mybir as mybir
from concourse.bass_utils import run_bass_kernel_spmd

F32 = mybir.dt.float32
BF16 = mybir.dt.bfloat16
AF = mybir.ActivationFunctionType
ALU = mybir.AluOpType
AX = mybir.AxisListType

EPS = 1e-6
TINY = 1e-30
BIGNEG = -30000.0
NCORES = 8
G = 4


class Cfg:
    def __init__(self, D, S, L, DFF):
        self.D, self.S, self.L, self.DFF = D, S, L, DFF
        self.T = S // G
        self.KD = D // 128
        self.H = D // 2 // 128
        self.NF = DFF // 128
        self.TT = min(512, self.T)
        self.NT = self.T // self.TT
        self.NMB = 6 * self.KD // G
        self.NCH = self.T // 64
        self.NQB = self.T // 128
        self.KT = min(1024, self.T)
        self.NKT = self.T // self.KT


class Sched:
    ENG = ("pe", "act", "dve", "pool", "sp")

    def __init__(self, nc, stack):
        self.nc = nc
        self.stack = stack
        self.ops = {e: [] for e in self.ENG}
        self.last = {e: None for e in self.ENG}
        self.cnt = {}
        self.sems = {}
        self.seen = {e: {} for e in self.ENG}
        self.pending = {}
        for e in self.ENG:
            self._mksem(e)

    def _mksem(self, key):
        self.sems[key] = self.stack.enter_context(self.nc.semaphore("s_" + key))
        self.cnt[key] = 0

    def op(self, eng, fn):
        rec = {"fn": fn, "sem": None, "val": None, "inc": None}
        self.ops[eng].append(rec)
        self.last[eng] = rec
        return rec

    def sig(self, eng):
        rec = self.last[eng]
        assert rec is not None
        if rec["sem"] is None:
            self.cnt[eng] += 1
            rec["sem"], rec["val"], rec["inc"] = eng, self.cnt[eng], 1
        assert rec["sem"] == eng or rec["inc"] != 1, "sig on dma op"
        return (rec["sem"], rec["val"])

    def wait(self, eng, *evs):
        for ev in evs:
            if ev is None:
                continue
            if isinstance(ev, list):
                self.wait(eng, *ev)
                continue
            key, val = ev
            if self.seen[eng].get(key, 0) >= val:
                continue
            self.seen[eng][key] = val
            self.ops[eng].append({"wait": key, "val": val})

    def dma(self, q, out, in_, chan):
        if chan not in self.sems:
            self._mksem(chan)
        rec = self.op(q, lambda e: e.dma_start(out=out, in_=in_))
        self.cnt[chan] += 16
        rec["sem"], rec["val"], rec["inc"] = chan, self.cnt[chan], 16
        ev = (chan, self.cnt[chan])
        self.pending[chan] = ev
        return ev

    def dma_split(self, q, out, in_, chan, n):
        d = out.shape[1]
        n = min(n, d)
        step = (d + n - 1) // n
        ev = None
        for a in range(0, d, step):
            b = min(d, a + step)
            ev = self.dma(q, out[:, a:b], in_[:, a:b], chan)
        return ev

    def coll(self, kind, groups, in_, out, chan):
        if chan not in self.sems:
            self._mksem(chan)
        rec = self.op("pool", lambda e: e.collective_compute(
            kind, ALU.bypass, replica_groups=groups, ins=[in_], outs=[out]))
        self.cnt[chan] += 1
        rec["sem"], rec["val"], rec["inc"] = chan, self.cnt[chan], None
        ev = (chan, self.cnt[chan])
        self.pending[chan] = ev
        return ev

    def barrier(self):
        evs = []
        for e in ("pe", "act", "dve", "pool"):
            rec = self.last[e]
            if rec is not None and (rec["sem"] is None or rec["sem"] == e):
                evs.append(self.sig(e))
        evs += list(self.pending.values())
        for e in self.ENG:
            self.wait(e, *evs)
        return evs

    def emit(self, block):
        def mk(eng):
            def body(e):
                for rec in self.ops[eng]:
                    if "wait" in rec:
                        e.wait_ge(self.sems[rec["wait"]], rec["val"])
                    else:
                        ins = rec["fn"](e)
                        if rec["sem"] is not None:
                            if rec["inc"] is None:
                                ins.then_inc(self.sems[rec["sem"]])
                            else:
                                ins.then_inc(self.sems[rec["sem"]], rec["inc"])
            return body
        block.tensor(mk("pe"))
        block.scalar(mk("act"))
        block.vector(mk("dve"))
        block.gpsimd(mk("pool"))
        block.sync(mk("sp"))


def build(cfg, dbg=()):
    D, T, L, KD, H, NF, TT, NT, NMB = cfg.D, cfg.T, cfg.L, cfg.KD, cfg.H, cfg.NF, cfg.TT, cfg.NT, cfg.NMB
    NCH, NQB, KT, NKT = cfg.NCH, cfg.NQB, cfg.KT, cfg.NKT
    HW = H * 128
    nc = bass.Bass("TRN2", target_bir_lowering=False)
    stack = contextlib.ExitStack()

    def din(name, shape, dt=F32):
        return nc.dram_tensor(name, list(shape), dt, kind="ExternalInput").ap()

    def dscr(name, shape, dt=F32):
        return nc.dram_tensor(name, list(shape), dt).ap()

    xT = din("xT", [128, KD, T])
    cT = din("cT", [128, KD, 2])
    wada = din("wada", [L, NMB, 128, KD * 128])
    bada = din("bada", [128, L * NMB * 2])
    n1g = din("n1g", [128, L * KD])
    n2g = din("n2g", [128, L * KD])
    hgog = din("hgog", [128, L * H])
    sbog = din("sbog", [128, L * H])
    sbqg = din("sbqg", [128, L])
    sbkg = din("sbkg", [128, L])
    lbl = din("lbl", [128, L * H])
    win_fm = din("win_fm", [L, 5 * H, 128, KD * 128])
    win_tm = din("win_tm", [L, 2, 128, KD * HW])
    wout = din("wout", [L, KD, 128, KD * 128])
    wfi = din("wfi", [L, NF, 128, 2 * KD * 128])
    wfo = din("wfo", [L, KD, 128, NF * 128])
    flags = din("flags", [128, 32])
    cmask_hg = din("cmask_hg", [128, 128])
    cmask_sb = din("cmask_sb", [128, 128])
    cident = din("cident", [128, 128])
    cscan = din("cscan", [128, min(T, 1024)])
    yT = nc.dram_tensor("yT", [128, KD, T], F32, kind="ExternalOutput").ap()
    dbg_out = {}

    xs = dscr("xs", [128, KD, T])
    d_qa = dscr("d_qa", [H, 128, T])
    d_kk = dscr("d_kk", [H, 128, T])
    d_lf = dscr("d_lf", [H, 128, T])
    d_gs = dscr("d_gs", [H, 128, T], BF16)
    d_vhg = dscr("d_vhg", [T, HW], BF16)
    d_qn = dscr("d_qn", [H, 128, T], BF16)
    d_kloc = [dscr(f"d_kloc{h}", [128, T], BF16) for h in range(H)]
    d_kall = [dscr(f"d_kall{h}", [G * 128, T], BF16) for h in range(H)]
    d_vloc = [dscr(f"d_vloc{h}", [T, 128], BF16) for h in range(H)]
    d_vall = [dscr(f"d_vall{h}", [G * T, 128], BF16) for h in range(H)]
    d_stloc = dscr("d_stloc", [H * 128, 132])
    d_stall = dscr("d_stall", [G * H * 128, 132])
    d_oloc = dscr("d_oloc", [H, 128, T])
    d_qh = dscr("d_qh", [H, 128, T], BF16)
    d_act = dscr("d_act", [NF, 128, T], BF16)
    d_modsrc = dscr("d_modsrc", [128, L * NMB * 2])
    d_modall = dscr("d_modall", [G * 128, L * NMB * 2])

    S = Sched(nc, stack)

    def sb(name, shape, dt=F32):
        return stack.enter_context(nc.sbuf_tensor(name, list(shape), dt))

    def ps(name, shape, dt=F32):
        return stack.enter_context(nc.psum_tensor(name, list(shape), dt))

    HT = sb("HT", [128, KD, T], BF16)
    WRN = max(NF * 128, 2 * KD * 128)
    WR = [sb(f"WR{i}", [128, WRN], BF16) for i in range(2)]
    BIGN = 18432
    BIG = sb("BIG", [128, BIGN], F32)
    BIGB = BIG[:, :].bitcast(BF16)
    NSTG = 4
    STG = [sb(f"STG{i}", [128, 512], F32) for i in range(NSTG)]
    NSTB = 4
    STB = [sb(f"STB{i}", [128, 512], BF16) for i in range(NSTB)]
    RSTD = [sb(f"RSTD{i}", [128, 512], F32) for i in range(2)]
    c_flags = sb("c_flags", [128, 32])
    c_mhg = sb("c_mhg", [128, 128])
    c_msb = sb("c_msb", [128, 128])
    c_idf = sb("c_idf", [128, 128])
    c_id = sb("c_id", [128, 128], BF16)
    c_ones = sb("c_ones", [128, 128], BF16)
    c_scan = sb("c_scan", [128, min(T, 1024)])
    p_n1g = sb("p_n1g", [128, L * KD])
    p_n2g = sb("p_n2g", [128, L * KD])
    p_hgog = sb("p_hgog", [128, L * H])
    p_sbog = sb("p_sbog", [128, L * H])
    p_sbqg = sb("p_sbqg", [128, L])
    p_sbkg = sb("p_sbkg", [128, L])
    p_lbl = sb("p_lbl", [128, L * H])
    p_bada = sb("p_bada", [128, L * NMB * 2])
    p_cT = sb("p_cT", [128, KD * 2])
    p_cond = sb("p_cond", [128, KD * 2], BF16)
    modloc = sb("modloc", [128, L * NMB * 2])
    MODA = sb("MODA", [128, max(G * L * NMB * 2, 768)])
    modall = MODA[:, 0:G * L * NMB * 2].rearrange("p (g f) -> p g f", g=G)
    MODB = MODA[:, 0:768].bitcast(BF16)
    FB1 = MODB[0:1, 0:512].rearrange("p (g c) -> p g c", g=G)
    FB2 = MODB[0:1, 512:1024].rearrange("p (g c) -> p g c", g=G)
    ONESROW = MODB[0:1, 1024:1536]
    modl = sb("modl", [128, L, 6 * KD])
    lb_e = sb("lb_e", [128, L * H])
    lb_s = sb("lb_s", [128, H])
    lb_lb = sb("lb_lb", [128, L * H])
    lb_om = sb("lb_om", [128, L * H])
    lb_nom = sb("lb_nom", [128, L * H])
    lb_tm = sb("lb_tm", [128, L * H])
    colA = sb("colA", [128, 2 * KD])
    colq = sb("colq", [128, 2])
    c_eps = sb("c_eps", [128, 1])
    c_onesf = sb("c_onesf", [128, 1024])
    S32 = sb("S32", [128, 128])
    Sb = sb("Sb", [128, 128], BF16)
    STALL = sb("STALL", [128, G, 132])
    ACC = sb("ACC", [128, 128])
    SIN = sb("SIN", [128, 128])
    DTOT = sb("DTOT", [128, 4])
    RC = sb("RC", [128, 32])
    MD = sb("MD", [128, G, 128])
    NB = sb("NB", [128, 64])
    ATs = [sb(f"AT{i}", [128, 1024], BF16) for i in range(2)]
    As = [sb(f"A{i}", [128, 1024], BF16) for i in range(2)]
    ONb = sb("ONb", [128, 128], BF16)
    NEGM = sb("NEGM", [128, G, 128], BF16)

    PS = [ps(f"PS{i}", [128, 1024]) for i in range(3)]
    PST = ps("PST", [128, 2048], BF16)
    psb = [PS[i // 2][:, (i % 2) * 512:(i % 2) * 512 + 512] for i in range(6)]

    SP, PE, ACT, DVE, POOL = "sp", "pe", "act", "dve", "pool"

    evs = []
    for dst, src in ((c_flags, flags), (c_mhg, cmask_hg), (c_msb, cmask_sb), (c_idf, cident), (c_scan, cscan),
                     (p_n1g, n1g), (p_n2g, n2g), (p_hgog, hgog), (p_sbog, sbog), (p_sbqg, sbqg),
                     (p_sbkg, sbkg), (p_lbl, lbl), (p_bada, bada)):
        evs.append(S.dma(SP, dst[:, :], src[:, :], "ld0"))
    evs.append(S.dma(SP, p_cT[:, :], cT.rearrange("p k b -> p (k b)"), "ld0"))
    ev_x0 = S.dma_split(SP, xs[:, :, :], xT[:, :, :], "ldx", KD)
    ld0 = evs[-1]
    for e in (ACT, DVE, POOL, PE):
        S.wait(e, ld0)
    S.op(DVE, lambda e: e.tensor_copy(out=c_id[:, :], in_=c_idf[:, :]))
    S.op(DVE, lambda e: e.memset(c_ones[:, :], 1.0))
    S.op(DVE, lambda e: e.memset(c_eps[:, :], EPS))
    S.op(DVE, lambda e: e.memset(c_onesf[:, :], 1.0))
    S.op(DVE, lambda e: e.memset(DTOT[:, :], 0.0))
    for o in range(G):
        S.op(DVE, lambda e, o=o: e.tensor_scalar(out=MD[:, o, :], in0=c_msb[:, :], scalar1=c_flags[:, 8 + o:9 + o],
                                                 scalar2=c_flags[:, 4 + o:5 + o], op0=ALU.mult, op1=ALU.add))
    S.wait(DVE, S.sig(DVE))
    for o in range(G):
        S.op(DVE, lambda e, o=o: e.tensor_scalar(out=NEGM[:, o, :], in0=MD[:, o, :], scalar1=-BIGNEG, scalar2=BIGNEG, op0=ALU.mult, op1=ALU.add))
    ev_const = S.sig(DVE)
    S.op(ACT, lambda e: e.activation(out=p_cond[:, :], in_=p_cT[:, :], func=AF.Silu))
    ev_cond = S.sig(ACT)
    S.op(ACT, lambda e: e.activation(out=lb_e[:, :], in_=p_lbl[:, :], func=AF.Exp))
    ev = S.sig(ACT)
    S.wait(DVE, ev)
    S.op(DVE, lambda e: e.tensor_copy(out=lb_s[:, :], in_=lb_e[:, 0:H]))
    for l in range(1, L):
        S.wait(DVE, S.sig(DVE))
        S.op(DVE, lambda e, l=l: e.tensor_add(out=lb_s[:, :], in0=lb_s[:, :], in1=lb_e[:, l * H:(l + 1) * H]))
    S.wait(DVE, S.sig(DVE))
    S.op(DVE, lambda e: e.reciprocal(out=lb_s[:, :], in_=lb_s[:, :]))
    S.wait(DVE, S.sig(DVE))
    S.op(DVE, lambda e: e.memset(lb_lb[:, 0:H], 0.0))
    for l in range(1, L):
        S.wait(DVE, S.sig(DVE))
        S.op(DVE, lambda e, l=l: e.tensor_mul(out=lb_e[:, l * H:(l + 1) * H], in0=lb_e[:, l * H:(l + 1) * H], in1=lb_s[:, :]))
        S.wait(DVE, S.sig(DVE))
        S.op(DVE, lambda e, l=l: e.tensor_add(out=lb_lb[:, l * H:(l + 1) * H], in0=lb_lb[:, (l - 1) * H:l * H],
                                              in1=lb_e[:, l * H:(l + 1) * H]))
    S.wait(DVE, S.sig(DVE))
    S.op(DVE, lambda e: e.tensor_scalar(out=lb_om[:, :], in0=lb_lb[:, :], scalar1=-1.0, scalar2=1.0, op0=ALU.mult, op1=ALU.add))
    S.op(DVE, lambda e: e.tensor_scalar(out=lb_tm[:, :], in0=lb_lb[:, :], scalar1=-1.0, scalar2=TINY, op0=ALU.mult, op1=ALU.add))
    S.wait(DVE, S.sig(DVE))
    S.op(DVE, lambda e: e.tensor_scalar(out=lb_nom[:, :], in0=lb_om[:, :], scalar1=-1.0, scalar2=None, op0=ALU.mult))
    ev_lb = S.sig(DVE)

    S.wait(PE, ev_cond)
    wfree = [None, None]
    widx = [0]

    def wload(src_ap, ncols):
        i = widx[0] % 2
        widx[0] += 1
        S.wait(POOL, wfree[i])
        ev = S.dma(POOL, WR[i][:, 0:ncols], src_ap, f"w{i}")
        return i, ev

    pmod = psb[0]
    for l in range(L):
        for jb in range(NMB):
            i, ev = wload(wada[l, jb], KD * 128)
            S.wait(PE, ev)
            col = (l * NMB + jb) * 2
            for kc in range(KD):
                S.op(PE, lambda e, i=i, kc=kc, col=col: e.matmul(
                    pmod[:, col:col + 2], lhsT=WR[i][:, kc * 128:(kc + 1) * 128], rhs=p_cond[:, kc * 2:kc * 2 + 2],
                    start=(kc == 0), stop=(kc == KD - 1)))
            wfree[i] = S.sig(PE)
    S.wait(DVE, S.sig(PE))
    NM = L * NMB * 2
    S.op(DVE, lambda e: e.tensor_add(out=modloc[:, :], in0=pmod[:, 0:NM], in1=p_bada[:, :]))
    ev = S.sig(DVE)
    S.wait(SP, ev)
    ev = S.dma(SP, d_modsrc[:, :], modloc[:, :], "mod")
    S.wait(POOL, ev)
    ev = S.coll("AllGather", [[0, 1, 2, 3], [4, 5, 6, 7]], d_modsrc, d_modall, "cc")
    S.wait(SP, ev)
    ev = S.dma(SP, modall[:, :, :], d_modall.rearrange("(r p) f -> p r f", p=128), "mod")
    S.wait(DVE, ev)
    for r in range(G):
        src = modall[:, r, :].rearrange("p (l j b) -> p l j b", l=L, j=NMB, b=2)
        dst = modl[:, :, r * NMB:(r + 1) * NMB]
        S.op(DVE, lambda e, src=src, dst=dst: e.tensor_scalar(out=dst, in0=src[:, :, :, 0], scalar1=c_flags[:, 24:25],
                                                              scalar2=None, op0=ALU.mult))
        S.wait(DVE, S.sig(DVE))
        S.op(DVE, lambda e, src=src, dst=dst: e.scalar_tensor_tensor(out=dst, in0=src[:, :, :, 1], scalar=c_flags[:, 25:26],
                                                                     in1=dst, op0=ALU.mult, op1=ALU.add))
    ev_mod = S.sig(DVE)
    S.wait(DVE, ev_mod)
    for o in range(G):
        S.op(DVE, lambda e, o=o: e.tensor_scalar(out=FB1[0:1, o, :], in0=c_onesf[0:1, 0:128], scalar1=c_flags[0:1, 16 + o:17 + o],
                                                 scalar2=None, op0=ALU.mult))
        S.op(DVE, lambda e, o=o: e.tensor_scalar(out=FB2[0:1, o, :], in0=c_onesf[0:1, 0:128], scalar1=c_flags[0:1, 20 + o:21 + o],
                                                 scalar2=None, op0=ALU.mult))
    S.op(DVE, lambda e: e.memset(ONESROW, 1.0))
    S.wait(SP, ev_x0)
    S.barrier()

    ones_col = c_ones

    def rstd_from(pss, out, n, width, evp=None):
        S.wait(ACT, evp)
        S.op(ACT, lambda e: e.activation(out=out, in_=pss, func=AF.Ln, scale=1.0 / n, bias=c_eps[:, 0:1]))
        S.wait(ACT, S.sig(ACT))
        S.op(ACT, lambda e: e.activation(out=out, in_=out, func=AF.Exp, scale=-0.5))
        return S.sig(ACT)

    def scan_T(out, d0tile, d1):
        PW = min(T, 1024)
        ev = None
        for pc in range(T // PW):
            a = pc * PW
            S.wait(DVE, ev)
            ini = 0.0 if pc == 0 else out[:, a - 1:a]
            S.op(DVE, lambda e, a=a, ini=ini: e.tensor_tensor_scan(out=out[:, a:a + PW], data0=d0tile[:, 0:PW], data1=d1[:, a:a + PW],
                                                                   initial=ini, op0=ALU.mult, op1=ALU.add))
            ev = S.sig(DVE)
        return ev

    stg_free = [None] * NSTG
    stg_i = [0]
    stb_free = [None] * NSTB
    stb_i = [0]

    def get_stg(eng):
        i = stg_i[0] % NSTG
        stg_i[0] += 1
        S.wait(eng, stg_free[i])
        return i

    def get_stb(eng):
        i = stb_i[0] % NSTB
        stb_i[0] += 1
        S.wait(eng, stb_free[i])
        return i

    psfree = [None] * 6
    psidx = [0]

    def get_ps(k=4):
        i = psidx[0] % k
        psidx[0] += 1
        S.wait(PE, psfree[i])
        return i

    def norm_to_HT(l, which, g_sb):
        base = 0 if which == 0 else 3 * KD
        A = colA[:, which * KD:(which + 1) * KD]
        S.op(DVE, lambda e: e.scalar_tensor_tensor(out=A, in0=modl[:, l, base + KD:base + 2 * KD], scalar=1.0,
                                                   in1=g_sb[:, l * KD:(l + 1) * KD], op0=ALU.add, op1=ALU.mult))
        evA = S.sig(DVE)
        XT = BIG[:, 0:KD * TT].rearrange("p (k t) -> p k t", k=KD)
        SQ = BIGB[:, 2 * KD * TT:2 * KD * TT + KD * TT].rearrange("p (k t) -> p k t", k=KD)
        xt_free = None
        for tt in range(NT):
            ts = slice(tt * TT, (tt + 1) * TT)
            S.wait(SP, xt_free)
            ev = S.dma_split(SP, XT, xs[:, :, ts], "xt", 4)
            S.wait(ACT, ev)
            S.wait(DVE, ev)
            S.op(ACT, lambda e: e.activation(out=SQ, in_=XT, func=AF.Square))
            evq = S.sig(ACT)
            pi = get_ps()
            S.wait(PE, evq, ev_const)
            for kc in range(KD):
                S.op(PE, lambda e, kc=kc, pi=pi: e.matmul(psb[pi][:, 0:TT], lhsT=c_ones[:, :], rhs=SQ[:, kc, :],
                                                          start=(kc == 0), stop=(kc == KD - 1)))
            evp = S.sig(PE)
            rs = RSTD[tt % 2]
            evr = rstd_from(psb[pi][:, 0:TT], rs[:, 0:TT], D, TT, evp)
            psfree[pi] = evr
            S.wait(DVE, evr)
            for kc in range(KD):
                S.op(DVE, lambda e, kc=kc, rs=rs: e.tensor_mul(out=XT[:, kc, :], in0=XT[:, kc, :], in1=rs[:, 0:TT]))
            evm = S.sig(DVE)
            S.wait(ACT, evm, evA, ev_mod)
            for kc in range(KD):
                S.op(ACT, lambda e, kc=kc, ts=ts: e.activation(out=HT[:, kc, ts], in_=XT[:, kc, :], func=AF.Identity,
                                                               scale=A[:, kc:kc + 1],
                                                               bias=modl[:, l, base + kc:base + kc + 1]))
            xt_free = S.sig(ACT)
        return xt_free

    def gemm_fm(wsrc_fn, nblk, nk, rhs_fn, ntt, tw, evac_fn, wcols):
        for blk in range(nblk):
            i, ev = wload(wsrc_fn(blk), wcols)
            S.wait(PE, ev)
            for tt in range(ntt):
                pi = get_ps()
                for kc in range(nk):
                    S.op(PE, lambda e, i=i, kc=kc, tt=tt, pi=pi: e.matmul(
                        psb[pi][:, 0:tw], lhsT=WR[i][:, kc * 128:(kc + 1) * 128], rhs=rhs_fn(kc, tt),
                        start=(kc == 0), stop=(kc == nk - 1)))
                evp = S.sig(PE)
                if tt == ntt - 1:
                    wfree[i] = evp
                psfree[pi] = evac_fn(blk, tt, psb[pi][:, 0:tw], evp)

    for l in range(L):
        evn = norm_to_HT(l, 0, p_n1g)
        S.wait(PE, evn)
        S.op(DVE, lambda e, l=l: e.tensor_scalar(out=colq[:, 0:1], in0=p_sbqg[:, l:l + 1], scalar1=128.0 ** -0.5, scalar2=None, op0=ALU.mult))
        S.op(DVE, lambda e, l=l: e.tensor_copy(out=colq[:, 1:2], in_=p_sbkg[:, l:l + 1]))
        ev_colq = S.sig(DVE)

        def evac_in(blk, tt, pss, evp, l=l):
            kind, h = blk // H, blk % H
            ts = slice(tt * TT, (tt + 1) * TT)
            lh = l * H + h
            if kind == 0:
                si = get_stg(ACT)
                S.wait(ACT, evp)
                S.op(ACT, lambda e: e.activation(out=STG[si][:, 0:TT], in_=pss, func=AF.Silu))
                ev = S.sig(ACT)
                S.wait(SP, ev)
                stg_free[si] = S.dma(SP, d_qa[h, :, ts], STG[si][:, 0:TT], f"sg{si}")
                return ev
            if kind == 1:
                si = get_stb(ACT)
                S.wait(ACT, evp)
                S.op(ACT, lambda e: e.activation(out=STB[si][:, 0:TT], in_=pss, func=AF.Silu))
                ev = S.sig(ACT)
                S.wait(SP, ev)
                stb_free[si] = S.dma(SP, d_gs[h, :, ts], STB[si][:, 0:TT], f"sb{si}")
                return ev
            if kind == 2:
                s0 = get_stg(ACT)
                S.wait(ACT, evp, ev_lb)
                S.op(ACT, lambda e: e.activation(out=STG[s0][:, 0:TT], in_=pss, func=AF.Sigmoid))
                ev0 = S.sig(ACT)
                s1 = get_stg(DVE)
                s2 = get_stg(DVE)
                S.wait(DVE, ev0)
                S.op(DVE, lambda e: e.tensor_scalar(out=STG[s1][:, 0:TT], in0=STG[s0][:, 0:TT], scalar1=lb_om[:, lh:lh + 1],
                                                    scalar2=lb_tm[:, lh:lh + 1], op0=ALU.mult, op1=ALU.max))
                ev1 = S.sig(DVE)
                S.op(DVE, lambda e: e.tensor_scalar(out=STG[s2][:, 0:TT], in0=STG[s0][:, 0:TT], scalar1=lb_nom[:, lh:lh + 1],
                                                    scalar2=lb_om[:, lh:lh + 1], op0=ALU.mult, op1=ALU.add))
                ev2 = S.sig(DVE)
                stg_free[s0] = ev2
                S.wait(SP, ev2)
                stg_free[s2] = S.dma(SP, d_kk[h, :, ts], STG[s2][:, 0:TT], f"sg{s2}")
                S.wait(ACT, ev1)
                S.op(ACT, lambda e: e.activation(out=STG[s1][:, 0:TT], in_=STG[s1][:, 0:TT], func=AF.Ln,
                                                 bias=lb_lb[:, lh:lh + 1], scale=1.0))
                ev3 = S.sig(ACT)
                S.wait(SP, ev3)
                stg_free[s1] = S.dma(SP, d_lf[h, :, ts], STG[s1][:, 0:TT], f"sg{s1}")
                return ev0
            sq = get_stb(ACT)
            S.wait(ACT, evp)
            S.op(ACT, lambda e: e.activation(out=STB[sq][:, 0:TT], in_=pss, func=AF.Square))
            ev0 = S.sig(ACT)
            S.wait(PE, ev0)
            S.wait(PE, psfree[4])
            S.op(PE, lambda e: e.matmul(psb[4][:, 0:TT], lhsT=c_ones[:, :], rhs=STB[sq][:, 0:TT], start=True, stop=True))
            ev1 = S.sig(PE)
            stb_free[sq] = ev1
            rs = RSTD[0]
            evr = rstd_from(psb[4][:, 0:TT], rs[:, 0:TT], 128.0, TT, ev1)
            psfree[4] = evr
            si = get_stg(DVE)
            S.wait(DVE, evr, evp)
            S.op(DVE, lambda e: e.tensor_mul(out=STG[si][:, 0:TT], in0=pss, in1=rs[:, 0:TT]))
            ev2 = S.sig(DVE)
            so = get_stb(ACT)
            S.wait(ACT, ev2, ev_colq)
            S.op(ACT, lambda e: e.activation(out=STB[so][:, 0:TT], in_=STG[si][:, 0:TT], func=AF.Copy,
                                             scale=colq[:, kind - 3:kind - 2]))
            ev3 = S.sig(ACT)
            stg_free[si] = ev3
            S.wait(SP, ev3)
            if kind == 3:
                stb_free[so] = S.dma(SP, d_qn[h, :, ts], STB[so][:, 0:TT], f"sb{so}")
            else:
                stb_free[so] = S.dma(SP, d_kloc[h][:, ts], STB[so][:, 0:TT], f"sb{so}")
            return ev2

        gemm_fm(lambda blk, l=l: win_fm[l, blk], 5 * H, KD, lambda kc, tt: HT[:, kc, tt * TT:(tt + 1) * TT], NT, TT,
                evac_in, KD * 128)

        WV = BIGB[:, 0:KD * HW].rearrange("p (k c) -> p k c", k=KD)
        CW = min(512, HW)
        wv_free = S.sig(ACT)
        for which, dst in ((0, d_vhg), (1, None)):
            S.wait(POOL, wv_free, S.sig(PE) if S.last[PE]["sem"] in (None, PE) else None)
            ev = S.dma_split(POOL, WV, win_tm[l, which].rearrange("p (k c) -> p k c", k=KD), "wv", KD)
            S.wait(PE, ev)
            for tb in range(T // 128):
                for cg in range(HW // CW):
                    pi = get_ps()
                    for kc in range(KD):
                        S.op(PE, lambda e, kc=kc, tb=tb, cg=cg, pi=pi: e.matmul(
                            psb[pi][:, 0:CW], lhsT=HT[:, kc, tb * 128:(tb + 1) * 128], rhs=WV[:, kc, cg * CW:(cg + 1) * CW],
                            start=(kc == 0), stop=(kc == KD - 1)))
                    evp = S.sig(PE)
                    si = get_stb(DVE)
                    S.wait(DVE, evp)
                    S.op(DVE, lambda e, si=si, pi=pi: e.tensor_copy(out=STB[si][:, 0:CW], in_=psb[pi][:, 0:CW]))
                    ev = S.sig(DVE)
                    psfree[pi] = ev
                    S.wait(SP, ev)
                    if dst is not None:
                        stb_free[si] = S.dma(SP, dst[tb * 128:(tb + 1) * 128, cg * CW:(cg + 1) * CW], STB[si][:, 0:CW], f"sb{si}")
                    else:
                        for hh in range(CW // 128):
                            stb_free[si] = S.dma(SP, d_vloc[cg * (CW // 128) + hh][tb * 128:(tb + 1) * 128, :],
                                                 STB[si][:, hh * 128:(hh + 1) * 128], f"sb{si}")
            wv_free = S.sig(PE)
        S.barrier()
        if "B" in dbg and l == 0:
            break

        GR = [[0, 1, 2, 3], [4, 5, 6, 7]]
        ev_kall = [S.coll("AllGather", GR, d_kloc[h], d_kall[h], "cc") for h in range(H)][-1]
        ev_vall = [S.coll("AllGather", GR, d_vloc[h], d_vall[h], "cc") for h in range(H)][-1]

        if "C0" in dbg and l == 0:
            S.barrier()
            break
        QA = BIG[:, 0:T]; KK = BIG[:, T:2 * T]; LF = BIG[:, 2 * T:3 * T]; X1 = BIG[:, 3 * T:4 * T]; X2 = BIG[:, 4 * T:5 * T]
        X3 = LF
        OT = X1
        bo = 10 * T
        QTl = BIGB[:, bo:bo + T]; KTl = BIGB[:, bo + T:bo + 2 * T]; KE = BIGB[:, bo + 2 * T:bo + 3 * T]
        KTOK = BIGB[0:64, bo + 3 * T:bo + 5 * T].rearrange("p (n c) -> p n c", c=128)
        VH = BIGB[0:64, bo + 5 * T:bo + 7 * T].rearrange("p (n c) -> p n c", c=128)
        QH = BIGB[:, bo + 7 * T:bo + 8 * T]
        QI = QH
        assert bo + 8 * T <= 2 * BIGN
        hfree = None
        CPR = min(NCH, 16)
        for h in range(H):
            S.wait(SP, hfree)
            S.dma(SP, QA, d_qa[h], "hg")
            S.dma(SP, KK, d_kk[h], "hg")
            S.dma(SP, LF, d_lf[h], "hg")
            ev = S.dma_split(SP, VH, d_vhg.rearrange("(n p) c -> p n c", p=64)[:, :, h * 128:(h + 1) * 128], "hg", 4)
            S.wait(DVE, ev); S.wait(ACT, ev); S.wait(PE, ev)
            ev = scan_T(X1, c_onesf, LF)
            S.wait(ACT, ev)
            S.op(ACT, lambda e: e.activation(out=X1, in_=X1, func=AF.Exp))
            ev = S.sig(ACT)
            S.wait(DVE, ev)
            S.op(DVE, lambda e: e.tensor_mul(out=QH, in0=QA, in1=X1))
            S.op(DVE, lambda e: e.tensor_copy(out=DTOT[:, 0:1], in_=X1[:, T - 1:T]))
            ev = S.sig(DVE)
            S.wait(SP, ev)
            S.dma(SP, d_qh[h], QH, "hgo")
            ev_qh = S.dma(SP, d_stloc[h * 128:(h + 1) * 128, 128:132], DTOT[:, 0:4], "hgo")
            S.wait(DVE, ev)
            ev = scan_T(X1, c_scan, LF)
            S.wait(ACT, ev)
            S.op(ACT, lambda e: e.activation(out=X2, in_=X1, func=AF.Exp))
            ev_x2 = S.sig(ACT)
            S.wait(DVE, ev_x2)
            S.op(DVE, lambda e: e.tensor_copy(out=RC[:, 0:NCH], in_=X1.rearrange("p (n c) -> p n c", c=64)[:, :, 31]))
            S.wait(DVE, S.sig(DVE))
            for c in range(NCH):
                S.op(DVE, lambda e, c=c: e.tensor_scalar(out=X1[:, c * 64:(c + 1) * 64], in0=X1[:, c * 64:(c + 1) * 64],
                                                         scalar1=RC[:, c:c + 1], scalar2=None, op0=ALU.subtract))
            ev = S.sig(DVE)
            S.wait(ACT, ev)
            S.op(ACT, lambda e: e.activation(out=X3, in_=X1, func=AF.Exp))
            S.wait(ACT, S.sig(ACT))
            S.op(ACT, lambda e: e.activation(out=X1, in_=X1, func=AF.Exp, scale=-1.0))
            ev = S.sig(ACT)
            S.wait(DVE, ev, ev_qh)
            S.op(DVE, lambda e: e.tensor_mul(out=QTl, in0=QA, in1=X3))
            S.op(DVE, lambda e: e.tensor_mul(out=KTl, in0=KK, in1=X1))
            S.op(DVE, lambda e: e.tensor_mul(out=QI, in0=QA, in1=X2))
            S.wait(DVE, S.sig(DVE))
            for c in range(NCH):
                S.op(DVE, lambda e, c=c: e.tensor_scalar(out=KE[:, c * 64:(c + 1) * 64], in0=KTl[:, c * 64:(c + 1) * 64],
                                                         scalar1=X3[:, c * 64 + 63:c * 64 + 64], scalar2=None, op0=ALU.mult))
            S.op(DVE, lambda e: e.memset(S32[:, :], 0.0))
            S.op(DVE, lambda e: e.memset(Sb[:, :], 0.0))
            ev_prep = S.sig(DVE)
            S.wait(PE, ev_prep)
            ev_ktok = None
            for rnd in range(NCH // CPR):
                S.wait(PE, ev_ktok)
                for cc in range(CPR):
                    c = rnd * CPR + cc
                    S.op(PE, lambda e, c=c, cc=cc: e.transpose(out=PST[0:64, cc * 128:(cc + 1) * 128], in_=KE[:, c * 64:(c + 1) * 64], identity=c_id[:, :]))
                ev = S.sig(PE)
                S.wait(ACT, ev)
                S.op(ACT, lambda e, rnd=rnd: e.activation(out=KTOK[:, rnd * CPR:(rnd + 1) * CPR, :],
                                                          in_=PST[0:64, 0:CPR * 128].rearrange("p (n c) -> p n c", c=128), func=AF.Copy))
                ev_ktok = S.sig(ACT)
            S.wait(PE, ev_ktok)
            sc_free = [None, None]
            sct_free = [None, None]
            scs = [STB[0], STB[1]]
            u_free = [None, None]
            st_ev = ev_prep
            ev_s = ev_prep
            NBLK = T // 128
            for tb in range(NBLK):
                k2 = tb % 2
                pscore = psb[2][0:64, k2 * 128:(k2 + 1) * 128]
                S.wait(PE, sc_free[k2])
                for half in range(2):
                    b0 = tb * 128 + half * 64
                    pc = pscore[:, half * 64:(half + 1) * 64]
                    S.op(PE, lambda e, b0=b0, pc=pc: e.matmul(pc[0:32, 0:32], lhsT=KTl[:, b0:b0 + 32], rhs=QTl[:, b0:b0 + 32], start=True, stop=True))
                    S.op(PE, lambda e, b0=b0, pc=pc: e.matmul(pc[0:64, 32:64], lhsT=KTl[:, b0:b0 + 64], rhs=QTl[:, b0 + 32:b0 + 64], start=True, stop=True))
                ev = S.sig(PE)
                S.wait(DVE, ev)
                sct = scs[k2][0:64, 0:128]
                S.wait(DVE, sct_free[k2])
                S.op(DVE, lambda e, sct=sct, pscore=pscore: e.tensor_mul(out=sct, in0=pscore, in1=c_mhg[0:64, :]))
                ev = S.sig(DVE)
                S.wait(PE, ev)
                pot = psb[(tb // 4) % 2][:, (tb % 4) * 128:(tb % 4) * 128 + 128]
                if tb % 4 == 0:
                    S.wait(PE, psfree[(tb // 4) % 2])
                for half in range(2):
                    c = tb * 2 + half
                    cs = slice(c * 64, (c + 1) * 64)
                    hs = slice(half * 64, (half + 1) * 64)
                    S.op(PE, lambda e, c=c, sct=sct, pot=pot, hs=hs: e.matmul(pot[:, hs], lhsT=VH[:, c, :], rhs=sct[:, hs], start=True, stop=False))
                    if half == 1:
                        sct_free[k2] = S.sig(PE)
                    S.wait(PE, st_ev)
                    S.op(PE, lambda e, cs=cs, pot=pot, hs=hs: e.matmul(pot[:, hs], lhsT=Sb[:, :], rhs=QI[:, cs], start=False, stop=True))
                    ev_o = S.sig(PE)
                    pu = psb[3][:, (c % 2) * 128:(c % 2) * 128 + 128]
                    S.wait(PE, u_free[c % 2])
                    S.op(PE, lambda e, c=c, pu=pu: e.matmul(pu, lhsT=KTOK[:, c, :], rhs=VH[:, c, :], start=True, stop=True))
                    ev_u = S.sig(PE)
                    S.wait(DVE, ev_u, ev_o)
                    S.op(DVE, lambda e, c=c, pu=pu: e.scalar_tensor_tensor(out=S32[:, :], in0=S32[:, :], scalar=X2[:, c * 64 + 63:c * 64 + 64],
                                                                           in1=pu, op0=ALU.mult, op1=ALU.add))
                    ev_s = S.sig(DVE)
                    u_free[c % 2] = ev_s
                    S.wait(ACT, ev_s)
                    S.op(ACT, lambda e: e.activation(out=Sb[:, :], in_=S32[:, :], func=AF.Copy))
                    st_ev = S.sig(ACT)
                sc_free[k2] = ev_o
                if tb % 4 == 3 or tb == NBLK - 1:
                    nb_ = tb % 4 + 1
                    t0 = (tb - nb_ + 1) * 128
                    pbank = psb[(tb // 4) % 2][:, 0:nb_ * 128]
                    S.wait(ACT, ev_o)
                    S.op(ACT, lambda e, t0=t0, nb_=nb_, pbank=pbank: e.activation(out=OT[:, t0:t0 + nb_ * 128], in_=pbank, func=AF.Copy))
                    psfree[(tb // 4) % 2] = S.sig(ACT)
            ev = S.sig(ACT)
            S.wait(SP, ev, st_ev, ev_s)
            S.dma(SP, d_oloc[h], OT, "hgo")
            ev = S.dma(SP, d_stloc[h * 128:(h + 1) * 128, 0:128], S32[:, :], "hgo")
            hfree = [ev, S.sig(PE)]
        S.wait(POOL, hfree)
        ev_stall = S.coll("AllGather", GR, d_stloc, d_stall, "cc")
        S.barrier()

        if "C1" in dbg and l == 0:
            break
        O2 = BIG[:, 0:T]; QH2 = BIGB[:, 2 * T:3 * T]; GS2 = BIGB[:, 3 * T:4 * T]
        c2free = None
        for h in range(H):
            S.wait(SP, c2free)
            S.dma(SP, O2, d_oloc[h], "c2")
            S.dma(SP, QH2, d_qh[h], "c2")
            S.dma(SP, GS2, d_gs[h], "c2")
            ev = S.dma(SP, STALL[:, :, :], d_stall.rearrange("(g hh p) c -> hh p g c", g=G, hh=H)[h], "c2")
            S.wait(DVE, ev); S.wait(PE, ev); S.wait(ACT, ev)
            S.op(DVE, lambda e: e.memset(ACC[:, :], 0.0))
            S.op(DVE, lambda e: e.memset(SIN[:, :], 0.0))
            for i in range(G):
                S.wait(DVE, S.sig(DVE))
                S.op(DVE, lambda e, i=i: e.scalar_tensor_tensor(out=ACC[:, :], in0=ACC[:, :], scalar=STALL[:, i, 128:129], in1=STALL[:, i, 0:128],
                                                                op0=ALU.mult, op1=ALU.add))
                S.wait(DVE, S.sig(DVE))
                S.op(DVE, lambda e, i=i: e.scalar_tensor_tensor(out=SIN[:, :], in0=ACC[:, :], scalar=c_flags[:, 12 + i:13 + i], in1=SIN[:, :],
                                                                op0=ALU.mult, op1=ALU.add))
            S.wait(DVE, S.sig(DVE))
            S.op(DVE, lambda e: e.tensor_copy(out=Sb[:, :], in_=SIN[:, :]))
            ev = S.sig(DVE)
            S.wait(PE, ev)
            for tt in range(NT):
                ts = slice(tt * TT, (tt + 1) * TT)
                pi = get_ps()
                S.op(PE, lambda e, pi=pi, ts=ts: e.matmul(psb[pi][:, 0:TT], lhsT=Sb[:, :], rhs=QH2[:, ts], start=True, stop=True))
                evp = S.sig(PE)
                S.wait(DVE, evp)
                S.op(DVE, lambda e, pi=pi, ts=ts: e.tensor_add(out=O2[:, ts], in0=psb[pi][:, 0:TT], in1=O2[:, ts]))
                ev = S.sig(DVE)
                psfree[pi] = ev
                sq = get_stb(ACT)
                S.wait(ACT, ev)
                S.op(ACT, lambda e, sq=sq, ts=ts: e.activation(out=STB[sq][:, 0:TT], in_=O2[:, ts], func=AF.Square))
                ev0 = S.sig(ACT)
                S.wait(PE, ev0, psfree[4])
                S.op(PE, lambda e, sq=sq: e.matmul(psb[4][:, 0:TT], lhsT=c_ones[:, :], rhs=STB[sq][:, 0:TT], start=True, stop=True))
                ev1 = S.sig(PE)
                stb_free[sq] = ev1
                rs = RSTD[tt % 2]
                evr = rstd_from(psb[4][:, 0:TT], rs[:, 0:TT], 128.0, TT, ev1)
                psfree[4] = evr
                S.wait(DVE, evr)
                S.op(DVE, lambda e, ts=ts, rs=rs: e.tensor_mul(out=O2[:, ts], in0=O2[:, ts], in1=rs[:, 0:TT]))
                S.wait(DVE, S.sig(DVE))
                S.op(DVE, lambda e, ts=ts: e.tensor_mul(out=O2[:, ts], in0=O2[:, ts], in1=GS2[:, ts]))
                ev = S.sig(DVE)
                S.wait(ACT, ev)
                S.op(ACT, lambda e, ts=ts, h=h, l=l: e.activation(out=HT[:, h, ts], in_=O2[:, ts], func=AF.Copy,
                                                                  scale=p_hgog[:, l * H + h:l * H + h + 1]))
            ev = S.sig(ACT)
            c2free = [ev, S.sig(PE)]
        S.barrier()
        if "C" in dbg and l == 0:
            break

        NSB = KT // 128
        KTs = BIGB[:, 0:G * T].rearrange("p (g t) -> p g t", g=G)
        Vs = BIGB[:, G * T:2 * G * T].rearrange("p (n c) -> p n c", c=128)
        Qs = BIGB[:, 2 * G * T:2 * G * T + T]
        f0 = (2 * G * T + T + 1) // 2

        def ftile(k):
            return BIG[:, f0 + k * (KT + 4):f0 + k * (KT + 4) + KT + 4]
        assert f0 + 6 * (KT + 4) <= BIGN, (f0, KT, BIGN)
        E_ = [ftile(0), ftile(1)]; SP_ = [ftile(2), ftile(3)]; C_ = [ftile(4), ftile(5)]
        S.op(DVE, lambda e: e.memset(C_[0][:, 0:1], 0.0))
        S.op(DVE, lambda e: e.memset(C_[1][:, 0:1], 0.0))
        ev_c0 = S.sig(DVE)
        S.wait(DVE, ev_c0)
        hd_free = None
        e_free = [None, None]; sp_free = [None, None]; c_free = [None, None]; a_free = [None, None]; at_free = [None, None]
        z_free = [None, None]; pt_free = [None, None]; po_free = [None, None]; on_free = [None]
        nbk = [0]
        pos = [PS[2][:, 512:640], PS[2][:, 640:768]]
        tiles = [(o, kt) for o in reversed(range(G)) for kt in reversed(range(NKT))]
        ntl = len(tiles)
        for h in range(H):
            S.wait(SP, hd_free, ev_kall, ev_vall)
            S.dma_split(SP, KTs, d_kall[h].rearrange("(g d) t -> d g t", g=G), "at", G)
            S.dma_split(SP, Vs, d_vall[h].rearrange("(n p) c -> p n c", p=128), "at", 8)
            ev_ld = S.dma(SP, Qs, d_qn[h], "at")
            S.wait(PE, ev_ld)
            jobs = [(qb, ti) for qb in range(NQB) for ti in range(ntl)]
            info = {}
            ncar_box = [None]

            def stageA1pe(jn, h=h):
                qb, ti = jobs[jn]
                o, kt = tiles[ti]
                qs = slice(qb * 128, (qb + 1) * 128)
                m = jn % 2
                pz = PS[m][:, 0:KT]
                kb0 = kt * NSB
                nbef = min(max(qb - kb0, 0), NSB)
                hasd = (kb0 <= qb < kb0 + NSB)
                segs = []
                if nbef > 0:
                    segs.append((0, nbef * 128, "b"))
                if hasd:
                    segs.append((nbef * 128, nbef * 128 + 128, "d"))
                if (nbef + (1 if hasd else 0)) * 128 < KT:
                    segs.append(((nbef + (1 if hasd else 0)) * 128, KT, "a"))
                S.wait(PE, z_free[m])
                w = min(512, KT)
                for c5 in range(KT // w):
                    c0, c1_ = c5 * w, (c5 + 1) * w
                    sub = [(max(a_, c0), min(b_, c1_), k_) for (a_, b_, k_) in segs if max(a_, c0) < min(b_, c1_)]
                    S.op(PE, lambda e, c0=c0, c1_=c1_: e.matmul(pz[:, c0:c1_], lhsT=Qs[:, qs],
                                                                rhs=KTs[:, o, kt * KT + c0:kt * KT + c1_], start=True, stop=False))
                    for si_, (a_, b_, k_) in enumerate(sub):
                        lastf = (si_ == len(sub) - 1)
                        if k_ == "d":
                            S.op(PE, lambda e, a_=a_, b_=b_, lastf=lastf: e.matmul(pz[:, a_:b_], lhsT=c_id[:, :], rhs=NEGM[:, o, :],
                                                                                  start=False, stop=lastf))
                        else:
                            FB = FB1 if k_ == "b" else FB2
                            S.op(PE, lambda e, a_=a_, b_=b_, lastf=lastf, FB=FB: e.matmul(pz[:, a_:b_], lhsT=FB[0:1, o, :], rhs=ONESROW[0:1, 0:b_ - a_],
                                                                                         start=False, stop=lastf))
                info[jn] = dict(m=m, o=o, kt=kt, qb=qb, ti=ti, pz=pz, ev_z=S.sig(PE))

            def stageA1(jn, h=h):
                d = info[jn]
                m, pz, ev_z = d["m"], d["pz"], d["ev_z"]
                Et, SPt = E_[m], SP_[m]
                S.wait(ACT, ev_z)
                S.op(ACT, lambda e: e.activation(out=Et[:, 0:KT], in_=pz, func=AF.Exp))
                S.wait(ACT, S.sig(ACT), sp_free[m])
                S.op(ACT, lambda e: e.activation(out=SPt[:, 0:KT], in_=Et[:, 0:KT], func=AF.Ln, scale=1.0, bias=c_onesf[:, 0:1]))
                d.update(ev_sp=S.sig(ACT))

            def stageA2(jn):
                d = info[jn]
                m, o, pz = d["m"], d["o"], d["pz"]
                SPt, Ct = SP_[m], C_[m]
                S.wait(DVE, d["ev_sp"], c_free[m])
                S.op(DVE, lambda e: e.tensor_tensor_scan(out=Ct[:, 1:KT + 1], data0=c_onesf[:, 0:KT], data1=SPt[:, 0:KT],
                                                         initial=0.0, op0=ALU.mult, op1=ALU.add))
                ev_c = S.sig(DVE)
                S.wait(DVE, ev_c)
                k3 = (nbk[0] % 16) * 3
                nbk[0] += 1
                nbc = NB[:, k3:k3 + 1]
                ncar = ncar_box[0] if d["ti"] > 0 else None
                S.op(DVE, lambda e: e.tensor_add(out=SPt[:, 0:KT], in0=pz, in1=Ct[:, 0:KT]))
                ev_u = S.sig(DVE)
                if ncar is None:
                    S.op(DVE, lambda e: e.tensor_scalar(out=nbc, in0=Ct[:, KT:KT + 1], scalar1=-1.0, scalar2=None, op0=ALU.mult))
                else:
                    S.op(DVE, lambda e: e.tensor_sub(out=nbc, in0=ncar, in1=Ct[:, KT:KT + 1]))
                ev_nb = S.sig(DVE)
                ncar_box[0] = nbc
                z_free[m] = ev_u
                c_free[m] = ev_nb
                d.update(nbc=nbc, ev_u=[ev_u, ev_nb])

            def stageB1(jn, h=h):
                d = info[jn]
                m, nbc = d["m"], d["nbc"]
                SPt, At = SP_[m], As[m]
                S.wait(ACT, d["ev_u"], a_free[m])
                S.op(ACT, lambda e: e.activation(out=At[:, 0:KT], in_=SPt[:, 0:KT], func=AF.Exp, bias=nbc))
                ev_a = S.sig(ACT)
                sp_free[m] = ev_a
                pT = PST[:, m * 1024:m * 1024 + KT]
                S.wait(PE, ev_a, pt_free[m])
                for sbk in range(NSB):
                    S.op(PE, lambda e, sbk=sbk: e.transpose(out=pT[:, sbk * 128:(sbk + 1) * 128], in_=At[:, sbk * 128:(sbk + 1) * 128],
                                                            identity=c_id[:, :]))
                ev_t = S.sig(PE)
                a_free[m] = ev_t
                d.update(ev_t=ev_t, pT=pT)

            def stageB2(jn, h=h):
                d = info.pop(jn)
                m, o, kt, qb, ti, pT, ev_t = d["m"], d["o"], d["kt"], d["qb"], d["ti"], d["pT"], d["ev_t"]
                ATt = ATs[m]
                po = pos[qb % 2]
                S.wait(ACT, ev_t, at_free[m])
                S.op(ACT, lambda e: e.activation(out=ATt[:, 0:KT], in_=pT, func=AF.Copy))
                ev_at = S.sig(ACT)
                pt_free[m] = ev_at
                S.wait(PE, ev_at)
                if ti == 0:
                    S.wait(PE, po_free[qb % 2])
                for sbk in range(NSB):
                    blk = (o * T + kt * KT) // 128 + sbk
                    S.op(PE, lambda e, sbk=sbk, blk=blk, first=(ti == 0 and sbk == 0), last=(ti == ntl - 1 and sbk == NSB - 1):
                         e.matmul(po, lhsT=ATt[:, sbk * 128:(sbk + 1) * 128], rhs=Vs[:, blk, :], start=first, stop=last))
                at_free[m] = S.sig(PE)
                if ti == ntl - 1:
                    epilogue(qb, at_free[m], h)

            def epilogue(qb, ev_po, h, l=l):
                qs = slice(qb * 128, (qb + 1) * 128)
                po = pos[qb % 2]
                k3 = (nbk[0] % 16) * 3
                nbk[0] += 1
                ssq, rsd = NB[:, k3:k3 + 1], NB[:, k3 + 1:k3 + 2]
                S.op(DVE, lambda e: e.memset(ssq, 0.0))
                S.wait(ACT, S.sig(DVE), ev_po, on_free[0])
                S.op(ACT, lambda e: e.activation(out=ONb[:, :], in_=po, func=AF.Square, accum_out=ssq))
                S.wait(ACT, S.sig(ACT))
                S.op(ACT, lambda e: e.activation(out=rsd, in_=ssq, func=AF.Ln, scale=1.0 / 128.0, bias=c_eps[:, 0:1]))
                S.wait(ACT, S.sig(ACT))
                S.op(ACT, lambda e: e.activation(out=rsd, in_=rsd, func=AF.Exp, scale=-0.5))
                S.wait(ACT, S.sig(ACT))
                S.op(ACT, lambda e: e.activation(out=ONb[:, :], in_=po, func=AF.Copy, scale=rsd))
                ev_on = S.sig(ACT)
                po_free[qb % 2] = ev_on
                pT = PST[:, 0:128]
                S.wait(PE, ev_on, pt_free[0])
                S.op(PE, lambda e: e.transpose(out=pT, in_=ONb[:, :], identity=c_id[:, :]))
                ev_tr = S.sig(PE)
                on_free[0] = ev_tr
                S.wait(ACT, ev_tr)
                S.op(ACT, lambda e: e.activation(out=HT[:, H + h, qs], in_=pT, func=AF.Copy,
                                                 scale=p_sbog[:, l * H + h:l * H + h + 1]))
                pt_free[0] = S.sig(ACT)

            nj = len(jobs)
            for step in range(-1, nj + 2):
                if 0 <= step - 2 < nj:
                    stageB1(step - 2)
                if 0 <= step - 1 < nj:
                    stageA2(step - 1)
                if 0 <= step + 1 < nj:
                    stageA1pe(step + 1)
                if 0 <= step < nj:
                    stageA1(step)
                if 0 <= step - 2 < nj:
                    stageB2(step - 2)
            hd_free = [S.sig(PE), S.sig(ACT)]
        S.barrier()
        if "D" in dbg and l == 0:
            break

        def evac_res(gcol0, l=l):
            def f(blk, tt, pss, evp):
                ts = slice(tt * TT, (tt + 1) * TT)
                si = get_stg(SP)
                ev = S.dma(SP, STG[si][:, 0:TT], xs[:, blk, ts], f"sg{si}")
                S.wait(DVE, ev, evp)
                S.op(DVE, lambda e: e.scalar_tensor_tensor(out=STG[si][:, 0:TT], in0=pss, scalar=modl[:, l, gcol0 + blk:gcol0 + blk + 1],
                                                           in1=STG[si][:, 0:TT], op0=ALU.mult, op1=ALU.add))
                ev = S.sig(DVE)
                S.wait(SP, ev)
                stg_free[si] = S.dma(SP, xs[:, blk, ts], STG[si][:, 0:TT], f"sg{si}")
                return ev
            return f
        gemm_fm(lambda blk, l=l: wout[l, blk], KD, KD, lambda kc, tt: HT[:, kc, tt * TT:(tt + 1) * TT], NT, TT,
                evac_res(2 * KD), KD * 128)
        S.barrier()
        if "E" in dbg and l == 0:
            break

        evn = norm_to_HT(l, 1, p_n2g)
        S.wait(PE, evn)
        gate_sb = {}

        def evac_ffn_in(blk, tt, pss, evp, l=l):
            j, which = blk // 2, blk % 2
            ts = slice(tt * TT, (tt + 1) * TT)
            if which == 0:
                si = get_stg(ACT)
                S.wait(ACT, evp)
                S.op(ACT, lambda e: e.activation(out=STG[si][:, 0:TT], in_=pss, func=AF.Silu))
                ev = S.sig(ACT)
                gate_sb[(j, tt)] = (si, ev)
                return ev
            si, evg = gate_sb.pop((j, tt))
            so = get_stb(DVE)
            S.wait(DVE, evg, evp)
            S.op(DVE, lambda e: e.tensor_mul(out=STB[so][:, 0:TT], in0=pss, in1=STG[si][:, 0:TT]))
            ev = S.sig(DVE)
            stg_free[si] = ev
            S.wait(SP, ev)
            stb_free[so] = S.dma(SP, d_act[j, :, ts], STB[so][:, 0:TT], f"sb{so}")
            return ev

        gemm_fm(lambda blk, l=l: wfi[l, blk // 2, :, (blk % 2) * KD * 128:(blk % 2 + 1) * KD * 128], 2 * NF, KD,
                lambda kc, tt: HT[:, kc, tt * TT:(tt + 1) * TT], NT, TT, evac_ffn_in, KD * 128)
        S.barrier()
        ACTT = BIGB[:, 0:NF * TT].rearrange("p (k t) -> p k t", k=NF)
        assert NF * TT <= 2 * BIGN
        act_free = None
        for tt in range(NT):
            ts = slice(tt * TT, (tt + 1) * TT)
            S.wait(SP, act_free)
            ev_act = S.dma_split(SP, ACTT, d_act[:, :, ts].rearrange("k p t -> p k t"), "actt", 11)
            S.wait(PE, ev_act)
            er = evac_res(5 * KD)
            for blk in range(KD):
                i, ev = wload(wfo[l, blk], NF * 128)
                S.wait(PE, ev)
                pi = get_ps()
                for kc in range(NF):
                    S.op(PE, lambda e, i=i, kc=kc, pi=pi: e.matmul(psb[pi][:, 0:TT], lhsT=WR[i][:, kc * 128:(kc + 1) * 128], rhs=ACTT[:, kc, :],
                                                                   start=(kc == 0), stop=(kc == NF - 1)))
                evp = S.sig(PE)
                wfree[i] = evp
                psfree[pi] = er(blk, tt, psb[pi][:, 0:TT], evp)
            act_free = S.sig(PE)
        S.barrier()
    scr = dict(d_qa=d_qa, d_kk=d_kk, d_lf=d_lf, d_gs=d_gs, d_vhg=d_vhg, d_qn=d_qn, d_stloc=d_stloc, d_stall=d_stall, d_oloc=d_oloc, d_qh=d_qh, d_act=d_act,
               d_modall=d_modall)
    if "HT" in dbg:
        o = nc.dram_tensor("dbg_HT", [128, KD, T], BF16, kind="ExternalOutput").ap()
        S.dma(SP, o, HT[:, :, :], "dbgout")
    for name in dbg:
        if name in scr:
            src = scr[name]
            o = nc.dram_tensor("dbg_" + name, list(src.shape), src.dtype, kind="ExternalOutput").ap()
            S.dma(SP, o, src, "dbgout")
    ev = S.dma_split(SP, yT[:, :, :], xs[:, :, :], "out", KD)
    S.wait(SP, ev)
    S.barrier()
    with nc.Block() as block:
        S.emit(block)
    stack.close()
    return nc, S


def _bf(a):
    return a


def prep_inputs(cfg, inp):
    D, T, L, KD, H, NF, NMB = cfg.D, cfg.T, cfg.L, cfg.KD, cfg.H, cfg.NF, cfg.NMB
    HW = H * 128
    f = lambda a: np.ascontiguousarray(a, dtype=np.float32)
    x = np.asarray(inp["x"], np.float32)
    c = np.asarray(inp["c"], np.float32)

    def colvec(a, n):
        a = np.asarray(a, np.float32).reshape(L, n, 128)
        return f(a.transpose(2, 0, 1).reshape(128, L * n))

    w_in = np.asarray(inp["w_in"], np.float32)
    segs = [0, 3, 1, 4, 5]
    fm = np.concatenate([w_in[:, :, s * HW:(s + 1) * HW] for s in segs], axis=2)
    fm = fm.reshape(L, KD, 128, 5 * H, 128).transpose(0, 3, 2, 1, 4)
    win_fm = f(fm.reshape(L, 5 * H, 128, KD * 128))
    tm = np.stack([w_in[:, :, 2 * HW:3 * HW], w_in[:, :, 6 * HW:7 * HW]], axis=1)
    tm = tm.reshape(L, 2, KD, 128, HW).transpose(0, 1, 3, 2, 4)
    win_tm = f(tm.reshape(L, 2, 128, KD * HW))
    wo = np.asarray(inp["w_out"], np.float32).reshape(L, KD, 128, KD, 128).transpose(0, 3, 2, 1, 4)
    wout = f(wo.reshape(L, KD, 128, KD * 128))
    wi = np.asarray(inp["w_ffn_in"], np.float32).reshape(L, KD, 128, 2, NF, 128).transpose(0, 4, 2, 3, 1, 5)
    wfi = f(wi.reshape(L, NF, 128, 2 * KD * 128))
    wf = np.asarray(inp["w_ffn_out"], np.float32).reshape(L, NF, 128, KD, 128).transpose(0, 3, 2, 1, 4)
    wfo = f(wf.reshape(L, KD, 128, NF * 128))
    wa = np.asarray(inp["w_ada"], np.float32).reshape(L, KD, 128, 6 * KD, 128).transpose(0, 3, 2, 1, 4)
    ba = np.asarray(inp["b_ada"], np.float32).reshape(L, 6 * KD, 128)
    cTt = f(c.T.reshape(KD, 128, 2).transpose(1, 0, 2))
    common = dict(
        cT=cTt, n1g=colvec(inp["norm1_g"], KD), n2g=colvec(inp["norm2_g"], KD),
        hgog=colvec(inp["hg_out_g"], H), sbog=colvec(inp["sb_out_g"], H),
        sbqg=colvec(inp["sb_q_g"], 1), sbkg=colvec(inp["sb_k_g"], 1), lbl=colvec(inp["hg_lb_logits"], H),
        win_fm=win_fm, win_tm=win_tm, wout=wout, wfi=wfi, wfo=wfo,
    )
    s_idx = np.arange(128)
    mhg = ((s_idx[:, None] <= (s_idx[None, :] % 64)) & (s_idx[:, None] < 64)).astype(np.float32)
    msb = (s_idx[None, :] < s_idx[:, None]).astype(np.float32)
    ident = np.eye(128, dtype=np.float32)
    scan = np.ones((128, min(T, 1024)), np.float32)
    scan[:, ::64] = 0.0
    maps = []
    for r in range(NCORES):
        b, j = r // G, r % G
        xt = x[b, j * T:(j + 1) * T, :].T.reshape(KD, 128, T).transpose(1, 0, 2)
        fl = np.zeros((128, 32), np.float32)
        for o in range(G):
            fl[:, o] = 1.0 if o <= j else 0.0
            fl[:, 4 + o] = 1.0 if o < j else 0.0
            fl[:, 8 + o] = 1.0 if o == j else 0.0
            fl[:, 12 + o] = 1.0 if o == j - 1 else 0.0
            fl[:, 16 + o] = 0.0 if o <= j else BIGNEG
            fl[:, 20 + o] = 0.0 if o < j else BIGNEG
        fl[:, 24] = 1.0 if b == 0 else 0.0
        fl[:, 25] = 1.0 if b == 1 else 0.0
        wad = f(wa[:, j * NMB:(j + 1) * NMB].reshape(L, NMB, 128, KD * 128))
        bad = ba[:, j * NMB:(j + 1) * NMB, :].transpose(2, 0, 1)
        bad = f(np.repeat(bad[:, :, :, None], 2, axis=3).reshape(128, L * NMB * 2))
        m = dict(common)
        m.update(xT=f(xt), wada=wad, bada=bad, flags=fl, cmask_hg=mhg, cmask_sb=msb, cident=ident, cscan=scan)
        maps.append(m)
    return maps


def assemble(cfg, results):
    D, T, KD = cfg.D, cfg.T, cfg.KD
    out = np.zeros((2, cfg.S, D), np.float32)
    for r in range(NCORES):
        b, j = r // G, r % G
        y = np.asarray(results[r]["yT"], np.float32)
        out[b, j * T:(j + 1) * T, :] = y.transpose(1, 0, 2).reshape(D, T).T
    return out


_CACHE = {}


def kernel(**inputs):
    x = inputs["x"]
    B, Sq, D = x.shape
    L = inputs["w_in"].shape[0]
    DFF = inputs["w_ffn_out"].shape[1]
    cfg = Cfg(D, Sq, L, DFF)
    key = (D, Sq, L, DFF)
    if key not in _CACHE:
        _CACHE[key] = build(cfg)[0]
    nc = _CACHE[key]
    maps = prep_inputs(cfg, inputs)
    res = run_bass_kernel_spmd(nc, maps, core_ids=list(range(NCORES)))
    return assemble(cfg, res.results)
```

```python
import contextlib
import numpy as np
import concourse.bass as bass
import concourse.mybir as mybir
from concourse.bass_utils import run_bass_kernel_spmd

F32 = mybir.dt.float32
BF16 = mybir.dt.bfloat16
AF = mybir.ActivationFunctionType
ALU = mybir.AluOpType
AX = mybir.AxisListType

EPS = 1e-6
TINY = 1e-30
BIGNEG = -30000.0
NCORES = 8
G = 4


class Cfg:
    def __init__(self, D, S, L, DFF):
        self.D, self.S, self.L, self.DFF = D, S, L, DFF
        self.T = S // G
        self.KD = D // 128
        self.H = D // 2 // 128
        self.NF = DFF // 128
        self.TT = min(512, self.T)
        self.NT = self.T // self.TT
        self.NMB = 6 * self.KD // G
        self.NCH = self.T // 64
        self.NQB = self.T // 128
        self.KT = min(1024, self.T)
        self.NKT = self.T // self.KT


class Sched:
    ENG = ("pe", "act", "dve", "pool", "sp")

    def __init__(self, nc, stack):
        self.nc = nc
        self.stack = stack
        self.ops = {e: [] for e in self.ENG}
        self.last = {e: None for e in self.ENG}
        self.cnt = {}
        self.sems = {}
        self.seen = {e: {} for e in self.ENG}
        self.pending = {}
        for e in self.ENG:
            self._mksem(e)

    def _mksem(self, key):
        self.sems[key] = self.stack.enter_context(self.nc.semaphore("s_" + key))
        self.cnt[key] = 0

    def op(self, eng, fn):
        rec = {"fn": fn, "sem": None, "val": None, "inc": None}
        self.ops[eng].append(rec)
        self.last[eng] = rec
        return rec

    def sig(self, eng):
        rec = self.last[eng]
        assert rec is not None
        if rec["sem"] is None:
            self.cnt[eng] += 1
            rec["sem"], rec["val"], rec["inc"] = eng, self.cnt[eng], 1
        assert rec["sem"] == eng or rec["inc"] != 1, "sig on dma op"
        return (rec["sem"], rec["val"])

    def wait(self, eng, *evs):
        for ev in evs:
            if ev is None:
                continue
            if isinstance(ev, list):
                self.wait(eng, *ev)
                continue
            key, val = ev
            if self.seen[eng].get(key, 0) >= val:
                continue
            self.seen[eng][key] = val
            self.ops[eng].append({"wait": key, "val": val})

    def dma(self, q, out, in_, chan):
        if chan not in self.sems:
            self._mksem(chan)
        rec = self.op(q, lambda e: e.dma_start(out=out, in_=in_))
        self.cnt[chan] += 16
        rec["sem"], rec["val"], rec["inc"] = chan, self.cnt[chan], 16
        ev = (chan, self.cnt[chan])
        self.pending[chan] = ev
        return ev

    def dma_split(self, q, out, in_, chan, n):
        d = out.shape[1]
        n = min(n, d)
        step = (d + n - 1) // n
        ev = None
        for a in range(0, d, step):
            b = min(d, a + step)
            ev = self.dma(q, out[:, a:b], in_[:, a:b], chan)
        return ev

    def coll(self, kind, groups, in_, out, chan):
        if chan not in self.sems:
            self._mksem(chan)
        rec = self.op("pool", lambda e: e.collective_compute(
            kind, ALU.bypass, replica_groups=groups, ins=[in_], outs=[out]))
        self.cnt[chan] += 1
        rec["sem"], rec["val"], rec["inc"] = chan, self.cnt[chan], None
        ev = (chan, self.cnt[chan])
        self.pending[chan] = ev
        return ev

    def barrier(self):
        evs = []
        for e in ("pe", "act", "dve", "pool"):
            rec = self.last[e]
            if rec is not None and (rec["sem"] is None or rec["sem"] == e):
                evs.append(self.sig(e))
        evs += list(self.pending.values())
        for e in self.ENG:
            self.wait(e, *evs)
        return evs

    def emit(self, block):
        def mk(eng):
            def body(e):
                for rec in self.ops[eng]:
                    if "wait" in rec:
                        e.wait_ge(self.sems[rec["wait"]], rec["val"])
                    else:
                        ins = rec["fn"](e)
                        if rec["sem"] is not None:
                            if rec["inc"] is None:
                                ins.then_inc(self.sems[rec["sem"]])
                            else:
                                ins.then_inc(self.sems[rec["sem"]], rec["inc"])
            return body
        block.tensor(mk("pe"))
        block.scalar(mk("act"))
        block.vector(mk("dve"))
        block.gpsimd(mk("pool"))
        block.sync(mk("sp"))


def build(cfg, dbg=()):
    D, T, L, KD, H, NF, TT, NT, NMB = cfg.D, cfg.T, cfg.L, cfg.KD, cfg.H, cfg.NF, cfg.TT, cfg.NT, cfg.NMB
    NCH, NQB, KT, NKT = cfg.NCH, cfg.NQB, cfg.KT, cfg.NKT
    HW = H * 128
    nc = bass.Bass("TRN2", target_bir_lowering=False)
    stack = contextlib.ExitStack()

    def din(name, shape, dt=F32):
        return nc.dram_tensor(name, list(shape), dt, kind="ExternalInput").ap()

    def dscr(name, shape, dt=F32):
        return nc.dram_tensor(name, list(shape), dt).ap()

    xT = din("xT", [128, KD, T])
    cT = din("cT", [128, KD, 2])
    wada = din("wada", [L, NMB, 128, KD * 128])
    bada = din("bada", [128, L * NMB * 2])
    n1g = din("n1g", [128, L * KD])
    n2g = din("n2g", [128, L * KD])
    hgog = din("hgog", [128, L * H])
    sbog = din("sbog", [128, L * H])
    sbqg = din("sbqg", [128, L])
    sbkg = din("sbkg", [128, L])
    lbl = din("lbl", [128, L * H])
    win_fm = din("win_fm", [L, 5 * H, 128, KD * 128])
    win_tm = din("win_tm", [L, 2, 128, KD * HW])
    wout = din("wout", [L, KD, 128, KD * 128])
    wfi = din("wfi", [L, NF, 128, 2 * KD * 128])
    wfo = din("wfo", [L, KD, 128, NF * 128])
    flags = din("flags", [128, 32])
    cmask_hg = din("cmask_hg", [128, 128])
    cmask_sb = din("cmask_sb", [128, 128])
    cident = din("cident", [128, 128])
    cscan = din("cscan", [128, min(T, 1024)])
    yT = nc.dram_tensor("yT", [128, KD, T], F32, kind="ExternalOutput").ap()
    dbg_out = {}

    xs = dscr("xs", [128, KD, T])
    d_qa = dscr("d_qa", [H, 128, T])
    d_kk = dscr("d_kk", [H, 128, T])
    d_lf = dscr("d_lf", [H, 128, T])
    d_gs = dscr("d_gs", [H, 128, T], BF16)
    d_vhg = dscr("d_vhg", [T, HW], BF16)
    d_qn = [dscr(f"d_qn{h}", [128, T], BF16) for h in range(H)]
    d_qall = [dscr(f"d_qall{h}", [G * 128, T], BF16) for h in range(H)]
    d_ol = [dscr(f"d_ol{h}", [128, T], BF16) for h in range(H)]
    d_oall = [dscr(f"d_oall{h}", [G * 128, T], BF16) for h in range(H)]
    d_kloc = [dscr(f"d_kloc{h}", [128, T], BF16) for h in range(H)]
    d_kall = [dscr(f"d_kall{h}", [G * 128, T], BF16) for h in range(H)]
    d_vloc = [dscr(f"d_vloc{h}", [T, 128], BF16) for h in range(H)]
    d_vall = [dscr(f"d_vall{h}", [G * T, 128], BF16) for h in range(H)]
    d_stloc = dscr("d_stloc", [H * 128, 132])
    d_stall = dscr("d_stall", [G * H * 128, 132])
    d_oloc = dscr("d_oloc", [H, 128, T])
    d_qh = dscr("d_qh", [H, 128, T], BF16)
    d_act = dscr("d_act", [NF, 128, T], BF16)
    d_modsrc = dscr("d_modsrc", [128, L * NMB * 2])
    d_modall = dscr("d_modall", [G * 128, L * NMB * 2])

    S = Sched(nc, stack)

    def sb(name, shape, dt=F32):
        return stack.enter_context(nc.sbuf_tensor(name, list(shape), dt))

    def ps(name, shape, dt=F32):
        return stack.enter_context(nc.psum_tensor(name, list(shape), dt))

    HT = sb("HT", [128, KD, T], BF16)
    WRN = max(NF * 128, 2 * KD * 128)
    WR = [sb(f"WR{i}", [128, WRN], BF16) for i in range(2)]
    BIGN = 18432
    BIG = sb("BIG", [128, BIGN], F32)
    BIGB = BIG[:, :].bitcast(BF16)
    NSTG = 4
    STG = [sb(f"STG{i}", [128, 512], F32) for i in range(NSTG)]
    NSTB = 4
    STB = [sb(f"STB{i}", [128, 512], BF16) for i in range(NSTB)]
    RSTD = [sb(f"RSTD{i}", [128, 512], F32) for i in range(2)]
    c_flags = sb("c_flags", [128, 32])
    c_mhg = sb("c_mhg", [128, 128])
    c_msb = sb("c_msb", [128, 128])
    c_idf = sb("c_idf", [128, 128])
    c_id = sb("c_id", [128, 128], BF16)
    c_ones = sb("c_ones", [128, 128], BF16)
    c_scan = sb("c_scan", [128, min(T, 1024)])
    p_n1g = sb("p_n1g", [128, L * KD])
    p_n2g = sb("p_n2g", [128, L * KD])
    p_hgog = sb("p_hgog", [128, L * H])
    p_sbog = sb("p_sbog", [128, L * H])
    p_sbqg = sb("p_sbqg", [128, L])
    p_sbkg = sb("p_sbkg", [128, L])
    p_lbl = sb("p_lbl", [128, L * H])
    p_bada = sb("p_bada", [128, L * NMB * 2])
    p_cT = sb("p_cT", [128, KD * 2])
    p_cond = sb("p_cond", [128, KD * 2], BF16)
    modloc = sb("modloc", [128, L * NMB * 2])
    MODA = sb("MODA", [128, max(G * L * NMB * 2, 768)])
    modall = MODA[:, 0:G * L * NMB * 2].rearrange("p (g f) -> p g f", g=G)
    MODB = MODA[:, 0:768].bitcast(BF16)
    FB1 = MODB[0:1, 0:512].rearrange("p (g c) -> p g c", g=G)
    FB2 = MODB[0:1, 512:1024].rearrange("p (g c) -> p g c", g=G)
    ONESROW = MODB[0:1, 1024:1536]
    modl = sb("modl", [128, L, 6 * KD])
    lb_e = sb("lb_e", [128, L * H])
    lb_s = sb("lb_s", [128, H])
    lb_lb = sb("lb_lb", [128, L * H])
    lb_om = sb("lb_om", [128, L * H])
    lb_nom = sb("lb_nom", [128, L * H])
    lb_tm = sb("lb_tm", [128, L * H])
    colA = sb("colA", [128, 2 * KD])
    colq = sb("colq", [128, 2])
    c_eps = sb("c_eps", [128, 1])
    c_onesf = sb("c_onesf", [128, 1024])
    S32 = sb("S32", [128, 128])
    Sb = sb("Sb", [128, 128], BF16)
    STALL = sb("STALL", [128, G, 132])
    ACC = sb("ACC", [128, 128])
    SIN = sb("SIN", [128, 128])
    DTOT = sb("DTOT", [128, 4])
    RC = sb("RC", [128, 32])
    MD = sb("MD", [128, G, 128])
    NB = sb("NB", [128, 64])
    ATs = [sb(f"AT{i}", [128, 1024], BF16) for i in range(2)]
    As = [sb(f"A{i}", [128, 1024], BF16) for i in range(2)]
    ONb = sb("ONb", [128, 128], BF16)
    NEGM = sb("NEGM", [128, G, 128], BF16)

    PS = [ps(f"PS{i}", [128, 1024]) for i in range(3)]
    PST = ps("PST", [128, 2048], BF16)
    psb = [PS[i // 2][:, (i % 2) * 512:(i % 2) * 512 + 512] for i in range(6)]

    SP, PE, ACT, DVE, POOL = "sp", "pe", "act", "dve", "pool"

    evs = []
    for dst, src in ((c_flags, flags), (c_mhg, cmask_hg), (c_msb, cmask_sb), (c_idf, cident), (c_scan, cscan),
                     (p_n1g, n1g), (p_n2g, n2g), (p_hgog, hgog), (p_sbog, sbog), (p_sbqg, sbqg),
                     (p_sbkg, sbkg), (p_lbl, lbl), (p_bada, bada)):
        evs.append(S.dma(SP, dst[:, :], src[:, :], "ld0"))
    evs.append(S.dma(SP, p_cT[:, :], cT.rearrange("p k b -> p (k b)"), "ld0"))
    ev_x0 = S.dma_split(SP, xs[:, :, :], xT[:, :, :], "ldx", KD)
    ld0 = evs[-1]
    for e in (ACT, DVE, POOL, PE):
        S.wait(e, ld0)
    S.op(DVE, lambda e: e.tensor_copy(out=c_id[:, :], in_=c_idf[:, :]))
    S.op(DVE, lambda e: e.memset(c_ones[:, :], 1.0))
    S.op(DVE, lambda e: e.memset(c_eps[:, :], EPS))
    S.op(DVE, lambda e: e.memset(c_onesf[:, :], 1.0))
    S.op(DVE, lambda e: e.memset(DTOT[:, :], 0.0))
    for o in range(G):
        S.op(DVE, lambda e, o=o: e.tensor_scalar(out=MD[:, o, :], in0=c_msb[:, :], scalar1=c_flags[:, 8 + o:9 + o],
                                                 scalar2=c_flags[:, 4 + o:5 + o], op0=ALU.mult, op1=ALU.add))
    S.wait(DVE, S.sig(DVE))
    for o in range(G):
        S.op(DVE, lambda e, o=o: e.tensor_scalar(out=NEGM[:, o, :], in0=MD[:, o, :], scalar1=-BIGNEG, scalar2=BIGNEG, op0=ALU.mult, op1=ALU.add))
    ev_const = S.sig(DVE)
    S.op(ACT, lambda e: e.activation(out=p_cond[:, :], in_=p_cT[:, :], func=AF.Silu))
    ev_cond = S.sig(ACT)
    S.op(ACT, lambda e: e.activation(out=lb_e[:, :], in_=p_lbl[:, :], func=AF.Exp))
    ev = S.sig(ACT)
    S.wait(DVE, ev)
    S.op(DVE, lambda e: e.tensor_copy(out=lb_s[:, :], in_=lb_e[:, 0:H]))
    for l in range(1, L):
        S.wait(DVE, S.sig(DVE))
        S.op(DVE, lambda e, l=l: e.tensor_add(out=lb_s[:, :], in0=lb_s[:, :], in1=lb_e[:, l * H:(l + 1) * H]))
    S.wait(DVE, S.sig(DVE))
    S.op(DVE, lambda e: e.reciprocal(out=lb_s[:, :], in_=lb_s[:, :]))
    S.wait(DVE, S.sig(DVE))
    S.op(DVE, lambda e: e.memset(lb_lb[:, 0:H], 0.0))
    for l in range(1, L):
        S.wait(DVE, S.sig(DVE))
        S.op(DVE, lambda e, l=l: e.tensor_mul(out=lb_e[:, l * H:(l + 1) * H], in0=lb_e[:, l * H:(l + 1) * H], in1=lb_s[:, :]))
        S.wait(DVE, S.sig(DVE))
        S.op(DVE, lambda e, l=l: e.tensor_add(out=lb_lb[:, l * H:(l + 1) * H], in0=lb_lb[:, (l - 1) * H:l * H],
                                              in1=lb_e[:, l * H:(l + 1) * H]))
    S.wait(DVE, S.sig(DVE))
    S.op(DVE, lambda e: e.tensor_scalar(out=lb_om[:, :], in0=lb_lb[:, :], scalar1=-1.0, scalar2=1.0, op0=ALU.mult, op1=ALU.add))
    S.op(DVE, lambda e: e.tensor_scalar(out=lb_tm[:, :], in0=lb_lb[:, :], scalar1=-1.0, scalar2=TINY, op0=ALU.mult, op1=ALU.add))
    S.wait(DVE, S.sig(DVE))
    S.op(DVE, lambda e: e.tensor_scalar(out=lb_nom[:, :], in0=lb_om[:, :], scalar1=-1.0, scalar2=None, op0=ALU.mult))
    ev_lb = S.sig(DVE)

    S.wait(PE, ev_cond)
    wfree = [None, None]
    widx = [0]

    def wload(src_ap, ncols):
        i = widx[0] % 2
        widx[0] += 1
        S.wait(POOL, wfree[i])
        ev = S.dma(POOL, WR[i][:, 0:ncols], src_ap, f"w{i}")
        return i, ev

    pmod = psb[0]
    for l in range(L):
        for jb in range(NMB):
            i, ev = wload(wada[l, jb], KD * 128)
            S.wait(PE, ev)
            col = (l * NMB + jb) * 2
            for kc in range(KD):
                S.op(PE, lambda e, i=i, kc=kc, col=col: e.matmul(
                    pmod[:, col:col + 2], lhsT=WR[i][:, kc * 128:(kc + 1) * 128], rhs=p_cond[:, kc * 2:kc * 2 + 2],
                    start=(kc == 0), stop=(kc == KD - 1)))
            wfree[i] = S.sig(PE)
    S.wait(DVE, S.sig(PE))
    NM = L * NMB * 2
    S.op(DVE, lambda e: e.tensor_add(out=modloc[:, :], in0=pmod[:, 0:NM], in1=p_bada[:, :]))
    ev = S.sig(DVE)
    S.wait(SP, ev)
    ev = S.dma(SP, d_modsrc[:, :], modloc[:, :], "mod")
    S.wait(POOL, ev)
    ev = S.coll("AllGather", [[0, 1, 2, 3], [4, 5, 6, 7]], d_modsrc, d_modall, "cc")
    S.wait(SP, ev)
    ev = S.dma(SP, modall[:, :, :], d_modall.rearrange("(r p) f -> p r f", p=128), "mod")
    S.wait(DVE, ev)
    for r in range(G):
        src = modall[:, r, :].rearrange("p (l j b) -> p l j b", l=L, j=NMB, b=2)
        dst = modl[:, :, r * NMB:(r + 1) * NMB]
        S.op(DVE, lambda e, src=src, dst=dst: e.tensor_scalar(out=dst, in0=src[:, :, :, 0], scalar1=c_flags[:, 24:25],
                                                              scalar2=None, op0=ALU.mult))
        S.wait(DVE, S.sig(DVE))
        S.op(DVE, lambda e, src=src, dst=dst: e.scalar_tensor_tensor(out=dst, in0=src[:, :, :, 1], scalar=c_flags[:, 25:26],
                                                                     in1=dst, op0=ALU.mult, op1=ALU.add))
    ev_mod = S.sig(DVE)
    S.wait(DVE, ev_mod)
    for o in range(G):
        S.op(DVE, lambda e, o=o: e.tensor_scalar(out=FB1[0:1, o, :], in0=c_onesf[0:1, 0:128], scalar1=c_flags[0:1, 16 + o:17 + o],
                                                 scalar2=None, op0=ALU.mult))
        S.op(DVE, lambda e, o=o: e.tensor_scalar(out=FB2[0:1, o, :], in0=c_onesf[0:1, 0:128], scalar1=c_flags[0:1, 20 + o:21 + o],
                                                 scalar2=None, op0=ALU.mult))
    S.op(DVE, lambda e: e.memset(ONESROW, 1.0))
    S.wait(SP, ev_x0)
    S.barrier()

    ones_col = c_ones

    def rstd_from(pss, out, n, width, evp=None):
        S.wait(ACT, evp)
        S.op(ACT, lambda e: e.activation(out=out, in_=pss, func=AF.Ln, scale=1.0 / n, bias=c_eps[:, 0:1]))
        S.wait(ACT, S.sig(ACT))
        S.op(ACT, lambda e: e.activation(out=out, in_=out, func=AF.Exp, scale=-0.5))
        return S.sig(ACT)

    def scan_T(out, d0tile, d1):
        PW = min(T, 1024)
        ev = None
        for pc in range(T // PW):
            a = pc * PW
            S.wait(DVE, ev)
            ini = 0.0 if pc == 0 else out[:, a - 1:a]
            S.op(DVE, lambda e, a=a, ini=ini: e.tensor_tensor_scan(out=out[:, a:a + PW], data0=d0tile[:, 0:PW], data1=d1[:, a:a + PW],
                                                                   initial=ini, op0=ALU.mult, op1=ALU.add))
            ev = S.sig(DVE)
        return ev

    stg_free = [None] * NSTG
    stg_i = [0]
    stb_free = [None] * NSTB
    stb_i = [0]

    def get_stg(eng):
        i = stg_i[0] % NSTG
        stg_i[0] += 1
        S.wait(eng, stg_free[i])
        return i

    def get_stb(eng):
        i = stb_i[0] % NSTB
        stb_i[0] += 1
        S.wait(eng, stb_free[i])
        return i

    psfree = [None] * 6
    psidx = [0]

    def get_ps(k=4):
        i = psidx[0] % k
        psidx[0] += 1
        S.wait(PE, psfree[i])
        return i

    def norm_to_HT(l, which, g_sb):
        base = 0 if which == 0 else 3 * KD
        A = colA[:, which * KD:(which + 1) * KD]
        S.op(DVE, lambda e: e.scalar_tensor_tensor(out=A, in0=modl[:, l, base + KD:base + 2 * KD], scalar=1.0,
                                                   in1=g_sb[:, l * KD:(l + 1) * KD], op0=ALU.add, op1=ALU.mult))
        evA = S.sig(DVE)
        XT = BIG[:, 0:KD * TT].rearrange("p (k t) -> p k t", k=KD)
        SQ = BIGB[:, 2 * KD * TT:2 * KD * TT + KD * TT].rearrange("p (k t) -> p k t", k=KD)
        xt_free = None
        for tt in range(NT):
            ts = slice(tt * TT, (tt + 1) * TT)
            S.wait(SP, xt_free)
            ev = S.dma_split(SP, XT, xs[:, :, ts], "xt", 4)
            S.wait(ACT, ev)
            S.wait(DVE, ev)
            S.op(ACT, lambda e: e.activation(out=SQ, in_=XT, func=AF.Square))
            evq = S.sig(ACT)
            pi = get_ps()
            S.wait(PE, evq, ev_const)
            for kc in range(KD):
                S.op(PE, lambda e, kc=kc, pi=pi: e.matmul(psb[pi][:, 0:TT], lhsT=c_ones[:, :], rhs=SQ[:, kc, :],
                                                          start=(kc == 0), stop=(kc == KD - 1)))
            evp = S.sig(PE)
            rs = RSTD[tt % 2]
            evr = rstd_from(psb[pi][:, 0:TT], rs[:, 0:TT], D, TT, evp)
            psfree[pi] = evr
            S.wait(DVE, evr)
            for kc in range(KD):
                S.op(DVE, lambda e, kc=kc, rs=rs: e.tensor_mul(out=XT[:, kc, :], in0=XT[:, kc, :], in1=rs[:, 0:TT]))
            evm = S.sig(DVE)
            S.wait(ACT, evm, evA, ev_mod)
            for kc in range(KD):
                S.op(ACT, lambda e, kc=kc, ts=ts: e.activation(out=HT[:, kc, ts], in_=XT[:, kc, :], func=AF.Identity,
                                                               scale=A[:, kc:kc + 1],
                                                               bias=modl[:, l, base + kc:base + kc + 1]))
            xt_free = S.sig(ACT)
        return xt_free

    def gemm_fm(wsrc_fn, nblk, nk, rhs_fn, ntt, tw, evac_fn, wcols):
        for blk in range(nblk):
            i, ev = wload(wsrc_fn(blk), wcols)
            S.wait(PE, ev)
            for tt in range(ntt):
                pi = get_ps()
                for kc in range(nk):
                    S.op(PE, lambda e, i=i, kc=kc, tt=tt, pi=pi: e.matmul(
                        psb[pi][:, 0:tw], lhsT=WR[i][:, kc * 128:(kc + 1) * 128], rhs=rhs_fn(kc, tt),
                        start=(kc == 0), stop=(kc == nk - 1)))
                evp = S.sig(PE)
                if tt == ntt - 1:
                    wfree[i] = evp
                psfree[pi] = evac_fn(blk, tt, psb[pi][:, 0:tw], evp)

    for l in range(L):
        evn = norm_to_HT(l, 0, p_n1g)
        S.wait(PE, evn)
        S.op(DVE, lambda e, l=l: e.tensor_scalar(out=colq[:, 0:1], in0=p_sbqg[:, l:l + 1], scalar1=128.0 ** -0.5, scalar2=None, op0=ALU.mult))
        S.op(DVE, lambda e, l=l: e.tensor_copy(out=colq[:, 1:2], in_=p_sbkg[:, l:l + 1]))
        ev_colq = S.sig(DVE)

        def evac_in(blk, tt, pss, evp, l=l):
            kind, h = blk // H, blk % H
            ts = slice(tt * TT, (tt + 1) * TT)
            lh = l * H + h
            if kind == 0:
                si = get_stg(ACT)
                S.wait(ACT, evp)
                S.op(ACT, lambda e: e.activation(out=STG[si][:, 0:TT], in_=pss, func=AF.Silu))
                ev = S.sig(ACT)
                S.wait(SP, ev)
                stg_free[si] = S.dma(SP, d_qa[h, :, ts], STG[si][:, 0:TT], f"sg{si}")
                return ev
            if kind == 1:
                si = get_stb(ACT)
                S.wait(ACT, evp)
                S.op(ACT, lambda e: e.activation(out=STB[si][:, 0:TT], in_=pss, func=AF.Silu))
                ev = S.sig(ACT)
                S.wait(SP, ev)
                stb_free[si] = S.dma(SP, d_gs[h, :, ts], STB[si][:, 0:TT], f"sb{si}")
                return ev
            if kind == 2:
                s0 = get_stg(ACT)
                S.wait(ACT, evp, ev_lb)
                S.op(ACT, lambda e: e.activation(out=STG[s0][:, 0:TT], in_=pss, func=AF.Sigmoid))
                ev0 = S.sig(ACT)
                s1 = get_stg(DVE)
                s2 = get_stg(DVE)
                S.wait(DVE, ev0)
                S.op(DVE, lambda e: e.tensor_scalar(out=STG[s1][:, 0:TT], in0=STG[s0][:, 0:TT], scalar1=lb_om[:, lh:lh + 1],
                                                    scalar2=lb_tm[:, lh:lh + 1], op0=ALU.mult, op1=ALU.max))
                ev1 = S.sig(DVE)
                S.op(DVE, lambda e: e.tensor_scalar(out=STG[s2][:, 0:TT], in0=STG[s0][:, 0:TT], scalar1=lb_nom[:, lh:lh + 1],
                                                    scalar2=lb_om[:, lh:lh + 1], op0=ALU.mult, op1=ALU.add))
                ev2 = S.sig(DVE)
                stg_free[s0] = ev2
                S.wait(SP, ev2)
                stg_free[s2] = S.dma(SP, d_kk[h, :, ts], STG[s2][:, 0:TT], f"sg{s2}")
                S.wait(ACT, ev1)
                S.op(ACT, lambda e: e.activation(out=STG[s1][:, 0:TT], in_=STG[s1][:, 0:TT], func=AF.Ln,
                                                 bias=lb_lb[:, lh:lh + 1], scale=1.0))
                ev3 = S.sig(ACT)
                S.wait(SP, ev3)
                stg_free[s1] = S.dma(SP, d_lf[h, :, ts], STG[s1][:, 0:TT], f"sg{s1}")
                return ev0
            sq = get_stb(ACT)
            S.wait(ACT, evp)
            S.op(ACT, lambda e: e.activation(out=STB[sq][:, 0:TT], in_=pss, func=AF.Square))
            ev0 = S.sig(ACT)
            S.wait(PE, ev0)
            S.wait(PE, psfree[4])
            S.op(PE, lambda e: e.matmul(psb[4][:, 0:TT], lhsT=c_ones[:, :], rhs=STB[sq][:, 0:TT], start=True, stop=True))
            ev1 = S.sig(PE)
            stb_free[sq] = ev1
            rs = RSTD[0]
            evr = rstd_from(psb[4][:, 0:TT], rs[:, 0:TT], 128.0, TT, ev1)
            psfree[4] = evr
            si = get_stg(DVE)
            S.wait(DVE, evr, evp)
            S.op(DVE, lambda e: e.tensor_mul(out=STG[si][:, 0:TT], in0=pss, in1=rs[:, 0:TT]))
            ev2 = S.sig(DVE)
            so = get_stb(ACT)
            S.wait(ACT, ev2, ev_colq)
            S.op(ACT, lambda e: e.activation(out=STB[so][:, 0:TT], in_=STG[si][:, 0:TT], func=AF.Copy,
                                             scale=colq[:, kind - 3:kind - 2]))
            ev3 = S.sig(ACT)
            stg_free[si] = ev3
            S.wait(SP, ev3)
            if kind == 3:
                stb_free[so] = S.dma(SP, d_qn[h][:, ts], STB[so][:, 0:TT], f"sb{so}")
            else:
                stb_free[so] = S.dma(SP, d_kloc[h][:, ts], STB[so][:, 0:TT], f"sb{so}")
            return ev2

        gemm_fm(lambda blk, l=l: win_fm[l, blk], 5 * H, KD, lambda kc, tt: HT[:, kc, tt * TT:(tt + 1) * TT], NT, TT,
                evac_in, KD * 128)

        WV = BIGB[:, 0:KD * HW].rearrange("p (k c) -> p k c", k=KD)
        CW = min(512, HW)
        wv_free = S.sig(ACT)
        for which, dst in ((0, d_vhg), (1, None)):
            S.wait(POOL, wv_free, S.sig(PE) if S.last[PE]["sem"] in (None, PE) else None)
            ev = S.dma_split(POOL, WV, win_tm[l, which].rearrange("p (k c) -> p k c", k=KD), "wv", KD)
            S.wait(PE, ev)
            for tb in range(T // 128):
                for cg in range(HW // CW):
                    pi = get_ps()
                    for kc in range(KD):
                        S.op(PE, lambda e, kc=kc, tb=tb, cg=cg, pi=pi: e.matmul(
                            psb[pi][:, 0:CW], lhsT=HT[:, kc, tb * 128:(tb + 1) * 128], rhs=WV[:, kc, cg * CW:(cg + 1) * CW],
                            start=(kc == 0), stop=(kc == KD - 1)))
                    evp = S.sig(PE)
                    si = get_stb(DVE)
                    S.wait(DVE, evp)
                    S.op(DVE, lambda e, si=si, pi=pi: e.tensor_copy(out=STB[si][:, 0:CW], in_=psb[pi][:, 0:CW]))
                    ev = S.sig(DVE)
                    psfree[pi] = ev
                    S.wait(SP, ev)
                    if dst is not None:
                        stb_free[si] = S.dma(SP, dst[tb * 128:(tb + 1) * 128, cg * CW:(cg + 1) * CW], STB[si][:, 0:CW], f"sb{si}")
                    else:
                        for hh in range(CW // 128):
                            stb_free[si] = S.dma(SP, d_vloc[cg * (CW // 128) + hh][tb * 128:(tb + 1) * 128, :],
                                                 STB[si][:, hh * 128:(hh + 1) * 128], f"sb{si}")
            wv_free = S.sig(PE)
        S.barrier()
        if "B" in dbg and l == 0:
            break

        GR = [[0, 1, 2, 3], [4, 5, 6, 7]]
        ev_kall = [S.coll("AllGather", GR, d_kloc[h], d_kall[h], "cc") for h in range(H)][-1]
        ev_vall = [S.coll("AllGather", GR, d_vloc[h], d_vall[h], "cc") for h in range(H)][-1]
        ev_qall = [S.coll("AllGather", GR, d_qn[h], d_qall[h], "cc") for h in range(H)][-1]

        if "C0" in dbg and l == 0:
            S.barrier()
            break
        QA = BIG[:, 0:T]; KK = BIG[:, T:2 * T]; LF = BIG[:, 2 * T:3 * T]; X1 = BIG[:, 3 * T:4 * T]; X2 = BIG[:, 4 * T:5 * T]
        X3 = LF
        OT = X1
        bo = 10 * T
        QTl = BIGB[:, bo:bo + T]; KTl = BIGB[:, bo + T:bo + 2 * T]; KE = BIGB[:, bo + 2 * T:bo + 3 * T]
        KTOK = BIGB[0:64, bo + 3 * T:bo + 5 * T].rearrange("p (n c) -> p n c", c=128)
        VH = BIGB[0:64, bo + 5 * T:bo + 7 * T].rearrange("p (n c) -> p n c", c=128)
        QH = BIGB[:, bo + 7 * T:bo + 8 * T]
        QI = QH
        assert bo + 8 * T <= 2 * BIGN
        hfree = None
        CPR = min(NCH, 16)
        for h in range(H):
            S.wait(SP, hfree)
            S.dma(SP, QA, d_qa[h], "hg")
            S.dma(SP, KK, d_kk[h], "hg")
            S.dma(SP, LF, d_lf[h], "hg")
            ev = S.dma_split(SP, VH, d_vhg.rearrange("(n p) c -> p n c", p=64)[:, :, h * 128:(h + 1) * 128], "hg", 4)
            S.wait(DVE, ev); S.wait(ACT, ev); S.wait(PE, ev)
            ev = scan_T(X1, c_onesf, LF)
            S.wait(ACT, ev)
            S.op(ACT, lambda e: e.activation(out=X1, in_=X1, func=AF.Exp))
            ev = S.sig(ACT)
            S.wait(DVE, ev)
            S.op(DVE, lambda e: e.tensor_mul(out=QH, in0=QA, in1=X1))
            S.op(DVE, lambda e: e.tensor_copy(out=DTOT[:, 0:1], in_=X1[:, T - 1:T]))
            ev = S.sig(DVE)
            S.wait(SP, ev)
            S.dma(SP, d_qh[h], QH, "hgo")
            ev_qh = S.dma(SP, d_stloc[h * 128:(h + 1) * 128, 128:132], DTOT[:, 0:4], "hgo")
            S.wait(DVE, ev)
            ev = scan_T(X1, c_scan, LF)
            S.wait(ACT, ev)
            S.op(ACT, lambda e: e.activation(out=X2, in_=X1, func=AF.Exp))
            ev_x2 = S.sig(ACT)
            S.wait(DVE, ev_x2)
            S.op(DVE, lambda e: e.tensor_copy(out=RC[:, 0:NCH], in_=X1.rearrange("p (n c) -> p n c", c=64)[:, :, 31]))
            S.wait(DVE, S.sig(DVE))
            for c in range(NCH):
                S.op(DVE, lambda e, c=c: e.tensor_scalar(out=X1[:, c * 64:(c + 1) * 64], in0=X1[:, c * 64:(c + 1) * 64],
                                                         scalar1=RC[:, c:c + 1], scalar2=None, op0=ALU.subtract))
            ev = S.sig(DVE)
            S.wait(ACT, ev)
            S.op(ACT, lambda e: e.activation(out=X3, in_=X1, func=AF.Exp))
            S.wait(ACT, S.sig(ACT))
            S.op(ACT, lambda e: e.activation(out=X1, in_=X1, func=AF.Exp, scale=-1.0))
            ev = S.sig(ACT)
            S.wait(DVE, ev, ev_qh)
            S.op(DVE, lambda e: e.tensor_mul(out=QTl, in0=QA, in1=X3))
            S.op(DVE, lambda e: e.tensor_mul(out=KTl, in0=KK, in1=X1))
            S.op(DVE, lambda e: e.tensor_mul(out=QI, in0=QA, in1=X2))
            S.wait(DVE, S.sig(DVE))
            for c in range(NCH):
                S.op(DVE, lambda e, c=c: e.tensor_scalar(out=KE[:, c * 64:(c + 1) * 64], in0=KTl[:, c * 64:(c + 1) * 64],
                                                         scalar1=X3[:, c * 64 + 63:c * 64 + 64], scalar2=None, op0=ALU.mult))
            SS = [S32, ACC]
            assert NCH % 2 == 0
            S.op(DVE, lambda e: e.memset(S32[:, :], 0.0))
            S.op(DVE, lambda e: e.memset(Sb[:, :], 0.0))
            ev_prep = S.sig(DVE)
            S.wait(PE, ev_prep)
            ev_ktok = None
            for rnd in range(NCH // CPR):
                S.wait(PE, ev_ktok)
                for cc in range(CPR):
                    c = rnd * CPR + cc
                    S.op(PE, lambda e, c=c, cc=cc: e.transpose(out=PST[0:64, cc * 128:(cc + 1) * 128], in_=KE[:, c * 64:(c + 1) * 64], identity=c_id[:, :]))
                ev = S.sig(PE)
                S.wait(ACT, ev)
                S.op(ACT, lambda e, rnd=rnd: e.activation(out=KTOK[:, rnd * CPR:(rnd + 1) * CPR, :],
                                                          in_=PST[0:64, 0:CPR * 128].rearrange("p (n c) -> p n c", c=128), func=AF.Copy))
                ev_ktok = S.sig(ACT)
            S.wait(PE, ev_ktok)
            sc_free = [None, None]
            sct_free = [None, None]
            scs = [STB[0], STB[1]]
            u_free = [None, None]
            st_ev = ev_prep
            ev_s = ev_prep
            NBLK = T // 128
            for tb in range(NBLK):
                k2 = tb % 2
                pscore = psb[2][0:64, k2 * 128:(k2 + 1) * 128]
                S.wait(PE, sc_free[k2])
                for half in range(2):
                    b0 = tb * 128 + half * 64
                    pc = pscore[:, half * 64:(half + 1) * 64]
                    S.op(PE, lambda e, b0=b0, pc=pc: e.matmul(pc[0:32, 0:32], lhsT=KTl[:, b0:b0 + 32], rhs=QTl[:, b0:b0 + 32], start=True, stop=True))
                    S.op(PE, lambda e, b0=b0, pc=pc: e.matmul(pc[0:64, 32:64], lhsT=KTl[:, b0:b0 + 64], rhs=QTl[:, b0 + 32:b0 + 64], start=True, stop=True))
                ev = S.sig(PE)
                S.wait(DVE, ev)
                sct = scs[k2][0:64, 0:128]
                S.wait(DVE, sct_free[k2])
                S.op(DVE, lambda e, sct=sct, pscore=pscore: e.tensor_mul(out=sct, in0=pscore, in1=c_mhg[0:64, :]))
                ev = S.sig(DVE)
                S.wait(PE, ev)
                pot = psb[(tb // 4) % 2][:, (tb % 4) * 128:(tb % 4) * 128 + 128]
                if tb % 4 == 0:
                    S.wait(PE, psfree[(tb // 4) % 2])
                for half in range(2):
                    c = tb * 2 + half
                    cs = slice(c * 64, (c + 1) * 64)
                    hs = slice(half * 64, (half + 1) * 64)
                    S.op(PE, lambda e, c=c, sct=sct, pot=pot, hs=hs: e.matmul(pot[:, hs], lhsT=VH[:, c, :], rhs=sct[:, hs], start=True, stop=False))
                    if half == 1:
                        sct_free[k2] = S.sig(PE)
                    S.wait(PE, st_ev)
                    S.op(PE, lambda e, cs=cs, pot=pot, hs=hs: e.matmul(pot[:, hs], lhsT=Sb[:, :], rhs=QI[:, cs], start=False, stop=True))
                    ev_o = S.sig(PE)
                    pu = psb[3][:, (c % 2) * 128:(c % 2) * 128 + 128]
                    S.wait(PE, u_free[c % 2])
                    S.op(PE, lambda e, c=c, pu=pu: e.matmul(pu, lhsT=KTOK[:, c, :], rhs=VH[:, c, :], start=True, stop=True))
                    ev_u = S.sig(PE)
                    S.wait(DVE, ev_u, ev_o)
                    Sc, Sn = SS[c % 2], SS[(c + 1) % 2]
                    S.op(DVE, lambda e, c=c, pu=pu, Sc=Sc: e.scalar_tensor_tensor(out=Sb[:, :], in0=Sc[:, :], scalar=X2[:, c * 64 + 63:c * 64 + 64],
                                                                                  in1=pu, op0=ALU.mult, op1=ALU.add))
                    st_ev = S.sig(DVE)
                    S.op(DVE, lambda e, c=c, pu=pu, Sc=Sc, Sn=Sn: e.scalar_tensor_tensor(out=Sn[:, :], in0=Sc[:, :], scalar=X2[:, c * 64 + 63:c * 64 + 64],
                                                                                         in1=pu, op0=ALU.mult, op1=ALU.add))
                    ev_s = S.sig(DVE)
                    u_free[c % 2] = ev_s
                sc_free[k2] = ev_o
                if tb % 4 == 3 or tb == NBLK - 1:
                    nb_ = tb % 4 + 1
                    t0 = (tb - nb_ + 1) * 128
                    pbank = psb[(tb // 4) % 2][:, 0:nb_ * 128]
                    S.wait(ACT, ev_o)
                    S.op(ACT, lambda e, t0=t0, nb_=nb_, pbank=pbank: e.activation(out=OT[:, t0:t0 + nb_ * 128], in_=pbank, func=AF.Copy))
                    psfree[(tb // 4) % 2] = S.sig(ACT)
            ev = S.sig(ACT)
            S.wait(SP, ev, st_ev, ev_s)
            S.dma(SP, d_oloc[h], OT, "hgo")
            ev = S.dma(SP, d_stloc[h * 128:(h + 1) * 128, 0:128], S32[:, :], "hgo")
            hfree = [ev, S.sig(PE)]
        S.wait(POOL, hfree)
        ev_stall = S.coll("AllGather", GR, d_stloc, d_stall, "cc")
        S.barrier()

        if "C1" in dbg and l == 0:
            break
        O2 = BIG[:, 0:T]; QH2 = BIGB[:, 2 * T:3 * T]; GS2 = BIGB[:, 3 * T:4 * T]
        c2free = None
        for h in range(H):
            S.wait(SP, c2free)
            S.dma(SP, O2, d_oloc[h], "c2")
            S.dma(SP, QH2, d_qh[h], "c2")
            S.dma(SP, GS2, d_gs[h], "c2")
            ev = S.dma(SP, STALL[:, :, :], d_stall.rearrange("(g hh p) c -> hh p g c", g=G, hh=H)[h], "c2")
            S.wait(DVE, ev); S.wait(PE, ev); S.wait(ACT, ev)
            S.op(DVE, lambda e: e.memset(ACC[:, :], 0.0))
            S.op(DVE, lambda e: e.memset(SIN[:, :], 0.0))
            for i in range(G):
                S.wait(DVE, S.sig(DVE))
                S.op(DVE, lambda e, i=i: e.scalar_tensor_tensor(out=ACC[:, :], in0=ACC[:, :], scalar=STALL[:, i, 128:129], in1=STALL[:, i, 0:128],
                                                                op0=ALU.mult, op1=ALU.add))
                S.wait(DVE, S.sig(DVE))
                S.op(DVE, lambda e, i=i: e.scalar_tensor_tensor(out=SIN[:, :], in0=ACC[:, :], scalar=c_flags[:, 12 + i:13 + i], in1=SIN[:, :],
                                                                op0=ALU.mult, op1=ALU.add))
            S.wait(DVE, S.sig(DVE))
            S.op(DVE, lambda e: e.tensor_copy(out=Sb[:, :], in_=SIN[:, :]))
            ev = S.sig(DVE)
            S.wait(PE, ev)
            for tt in range(NT):
                ts = slice(tt * TT, (tt + 1) * TT)
                pi = get_ps()
                S.op(PE, lambda e, pi=pi, ts=ts: e.matmul(psb[pi][:, 0:TT], lhsT=Sb[:, :], rhs=QH2[:, ts], start=True, stop=True))
                evp = S.sig(PE)
                S.wait(DVE, evp)
                S.op(DVE, lambda e, pi=pi, ts=ts: e.tensor_add(out=O2[:, ts], in0=psb[pi][:, 0:TT], in1=O2[:, ts]))
                ev = S.sig(DVE)
                psfree[pi] = ev
                sq = get_stb(ACT)
                S.wait(ACT, ev)
                S.op(ACT, lambda e, sq=sq, ts=ts: e.activation(out=STB[sq][:, 0:TT], in_=O2[:, ts], func=AF.Square))
                ev0 = S.sig(ACT)
                S.wait(PE, ev0, psfree[4])
                S.op(PE, lambda e, sq=sq: e.matmul(psb[4][:, 0:TT], lhsT=c_ones[:, :], rhs=STB[sq][:, 0:TT], start=True, stop=True))
                ev1 = S.sig(PE)
                stb_free[sq] = ev1
                rs = RSTD[tt % 2]
                evr = rstd_from(psb[4][:, 0:TT], rs[:, 0:TT], 128.0, TT, ev1)
                psfree[4] = evr
                S.wait(DVE, evr)
                S.op(DVE, lambda e, ts=ts, rs=rs: e.tensor_mul(out=O2[:, ts], in0=O2[:, ts], in1=rs[:, 0:TT]))
                S.wait(DVE, S.sig(DVE))
                S.op(DVE, lambda e, ts=ts: e.tensor_mul(out=O2[:, ts], in0=O2[:, ts], in1=GS2[:, ts]))
                ev = S.sig(DVE)
                S.wait(ACT, ev)
                S.op(ACT, lambda e, ts=ts, h=h, l=l: e.activation(out=HT[:, h, ts], in_=O2[:, ts], func=AF.Copy,
                                                                  scale=p_hgog[:, l * H + h:l * H + h + 1]))
            ev = S.sig(ACT)
            c2free = [ev, S.sig(PE)]
        S.barrier()
        if "C" in dbg and l == 0:
            break

        KTC = 1024
        GT = G * T
        DBUF = (H * T >= 2 * GT) and (T <= 2048)
        HTB = HT[:, H:2 * H, :].rearrange("p k t -> p (k t)")
        KTf_ = [BIGB[:, 0:GT], HTB[:, 0:GT] if DBUF else BIGB[:, 0:GT]]
        Vs_ = [BIGB[:, GT:2 * GT].rearrange("p (n c) -> p n c", c=128),
               (HTB[:, GT:2 * GT] if DBUF else BIGB[:, GT:2 * GT]).rearrange("p (n c) -> p n c", c=128)]
        Qs0 = BIGB[:, 2 * GT:2 * GT + T]
        STGB = [STG[k][:, :].bitcast(BF16) for k in range(2)]

        def qcols(hb, c0, n):
            if hb == 0 or not DBUF:
                return Qs0[:, c0:c0 + n]
            k, o_ = c0 // 1024, c0 % 1024
            assert o_ + n <= 1024
            return STGB[k][:, o_:o_ + n]
        f0 = (2 * GT + T + 1) // 2

        def ftile(k):
            return BIG[:, f0 + k * (KTC + 4):f0 + k * (KTC + 4) + KTC + 4]
        fe = f0 + 6 * (KTC + 4)
        CHT = min(GT, 2048)
        QST = BIG[:, fe:fe + CHT // 2].bitcast(BF16)
        OL = BIG[:, fe + CHT // 2:fe + CHT // 2 + T // 2].bitcast(BF16)
        assert fe + CHT // 2 + T // 2 <= BIGN, (fe, BIGN)
        E_ = [ftile(0), ftile(1)]; SP_ = [ftile(2), ftile(3)]; C_ = [ftile(4), ftile(5)]
        S.op(DVE, lambda e: e.memset(C_[0][:, 0:1], 0.0))
        S.op(DVE, lambda e: e.memset(C_[1][:, 0:1], 0.0))
        S.wait(DVE, S.sig(DVE))
        hd_free = None
        sp_free = [None, None]; c_free = [None, None]; a_free = [None, None]; at_free = [None, None]
        z_free = [None, None]; pt_free = [None, None]; po_free = [None, None]; on_free = [None]
        nbk = [0]
        pos = [PS[2][:, 512:640], PS[2][:, 640:768]]
        RPC = CHT // T
        IPC = CHT // 512
        ev_oall = []
        hd_free = [None, None]
        ld_ev = {}
        qst_free = [None]

        def load_kv(h):
            hb = h % 2 if DBUF else 0
            S.wait(SP, hd_free[hb], ev_kall, ev_vall, ev_qall)
            S.dma_split(SP, KTf_[hb].rearrange("p (g t) -> p g t", g=G), d_kall[h].rearrange("(g d) t -> d g t", g=G), "at", G)
            ld_ev[(h, "kv")] = S.dma_split(SP, Vs_[hb], d_vall[h].rearrange("(n p) c -> p n c", p=128), "at", 8)

        def load_q_dma(h, cq):
            S.wait(SP, qst_free[0])
            if cq == 0 and not DBUF:
                S.wait(SP, hd_free[0])
            ld_ev[(h, "qd", cq)] = S.dma(SP, QST.rearrange("p (r t) -> p r t", r=RPC),
                                         d_qall[h].rearrange("(r d) t -> d r t", r=G)[:, cq * RPC:(cq + 1) * RPC, :], "xt")

        def load_q_sel(h, cq):
            hb = h % 2 if DBUF else 0
            S.wait(DVE, ld_ev[(h, "qd", cq)], hd_free[hb])
            dst = qcols(hb, cq * IPC * 128, IPC * 128).rearrange("p (i c) -> p i c", c=128)
            srcv = QST.rearrange("p (i m c) -> p i m c", m=4, c=128)
            for m4 in range(4):
                if m4 == 0:
                    S.op(DVE, lambda e, dst=dst, srcv=srcv: e.tensor_scalar(out=dst, in0=srcv[:, :, 0, :], scalar1=c_flags[:, 8:9], scalar2=None, op0=ALU.mult))
                else:
                    S.wait(DVE, S.sig(DVE))
                    S.op(DVE, lambda e, dst=dst, srcv=srcv, m4=m4: e.scalar_tensor_tensor(out=dst, in0=srcv[:, :, m4, :], scalar=c_flags[:, 8 + m4:9 + m4],
                                                                                            in1=dst, op0=ALU.mult, op1=ALU.add))
            qst_free[0] = S.sig(DVE)
            ld_ev[(h, "q")] = qst_free[0]

        NCQ = G // RPC

        def load_all(h):
            load_kv(h)
            for cq in range(NCQ):
                load_q_dma(h, cq)
                load_q_sel(h, cq)

        load_all(0)
        for h in range(H):
            hb = h % 2 if DBUF else 0
            KTf, Vs = KTf_[hb], Vs_[hb]
            if not DBUF and h > 0:
                load_all(h)
            S.wait(PE, ld_ev[(h, "q")], ld_ev[(h, "kv")])
            jobs = []
            for i in range(NQB):
                nk = (i + 1) * 512
                p_ = nk
                first = True
                tl = []
                while p_ > 0:
                    w = 512 if (p_ % 1024 == 512) else 1024
                    tl.append((p_ - w, w, first))
                    first = False
                    p_ -= w
                for ti, (k0, w, fst) in enumerate(tl):
                    jobs.append(dict(i=i, ti=ti, ntl=len(tl), k0=k0, kw=w, first=fst))
            info = {}
            ncar_box = [None]

            def stageA1pe(jn, h=h, hb=hb, KTf=KTf):
                jb = jobs[jn]
                i, kw, k0 = jb["i"], jb["kw"], jb["k0"]
                m = jn % 2
                pz = PS[m][:, 0:kw]
                S.wait(PE, z_free[m])
                for c5 in range(kw // 512):
                    c0, c1_ = c5 * 512, (c5 + 1) * 512
                    masked = jb["first"] and (c1_ == kw)
                    S.op(PE, lambda e, c0=c0, c1_=c1_, masked=masked, KTf=KTf, qa=qcols(hb, i * 128, 128): e.matmul(pz[:, c0:c1_], lhsT=qa, rhs=KTf[:, k0 + c0:k0 + c1_],
                                                                             start=True, stop=(not masked)))
                    if masked:
                        for m4 in range(4):
                            S.op(PE, lambda e, c0=c0, m4=m4: e.matmul(pz[:, c0 + m4 * 128:c0 + (m4 + 1) * 128], lhsT=c_id[:, :], rhs=NEGM[:, m4, :],
                                                                      start=False, stop=(m4 == 3)))
                info[jn] = dict(m=m, pz=pz, ev_z=S.sig(PE), kw=kw, jb=jb)

            def stageA1(jn, h=h):
                d = info[jn]
                m, pz, ev_z, kw = d["m"], d["pz"], d["ev_z"], d["kw"]
                Et, SPt = E_[m], SP_[m]
                S.wait(ACT, ev_z)
                S.op(ACT, lambda e: e.activation(out=Et[:, 0:kw], in_=pz, func=AF.Exp))
                S.wait(ACT, S.sig(ACT), sp_free[m])
                S.op(ACT, lambda e: e.activation(out=SPt[:, 0:kw], in_=Et[:, 0:kw], func=AF.Ln, scale=1.0, bias=c_onesf[:, 0:1]))
                d.update(ev_sp=S.sig(ACT))

            def stageA2(jn):
                d = info[jn]
                m, pz, kw = d["m"], d["pz"], d["kw"]
                SPt, Ct = SP_[m], C_[m]
                S.wait(DVE, d["ev_sp"], c_free[m])
                S.op(DVE, lambda e: e.tensor_tensor_scan(out=Ct[:, 1:kw + 1], data0=c_onesf[:, 0:kw], data1=SPt[:, 0:kw],
                                                         initial=0.0, op0=ALU.mult, op1=ALU.add))
                ev_c = S.sig(DVE)
                S.wait(DVE, ev_c)
                k3 = (nbk[0] % 16) * 3
                nbk[0] += 1
                nbc = NB[:, k3:k3 + 1]
                ncar = ncar_box[0] if d["jb"]["ti"] > 0 else None
                S.op(DVE, lambda e: e.tensor_add(out=SPt[:, 0:kw], in0=pz, in1=Ct[:, 0:kw]))
                ev_u = S.sig(DVE)
                if ncar is None:
                    S.op(DVE, lambda e: e.tensor_scalar(out=nbc, in0=Ct[:, kw:kw + 1], scalar1=-1.0, scalar2=None, op0=ALU.mult))
                else:
                    S.op(DVE, lambda e: e.tensor_sub(out=nbc, in0=ncar, in1=Ct[:, kw:kw + 1]))
                ev_nb = S.sig(DVE)
                ncar_box[0] = nbc
                z_free[m] = ev_u
                c_free[m] = ev_nb
                d.update(nbc=nbc, ev_u=[ev_u, ev_nb])

            def stageB1(jn, h=h):
                d = info[jn]
                m, nbc, kw = d["m"], d["nbc"], d["kw"]
                SPt, At = SP_[m], As[m]
                S.wait(ACT, d["ev_u"], a_free[m])
                S.op(ACT, lambda e: e.activation(out=At[:, 0:kw], in_=SPt[:, 0:kw], func=AF.Exp, bias=nbc))
                ev_a = S.sig(ACT)
                sp_free[m] = ev_a
                pT = PST[:, m * 1024:m * 1024 + kw]
                S.wait(PE, ev_a, pt_free[m])
                for sbk in range(kw // 128):
                    S.op(PE, lambda e, sbk=sbk: e.transpose(out=pT[:, sbk * 128:(sbk + 1) * 128], in_=At[:, sbk * 128:(sbk + 1) * 128],
                                                            identity=c_id[:, :]))
                ev_t = S.sig(PE)
                a_free[m] = ev_t
                d.update(ev_t=ev_t, pT=pT)

            def stageB2(jn, h=h, Vs=Vs):
                d = info.pop(jn)
                m, pT, ev_t, kw, jb = d["m"], d["pT"], d["ev_t"], d["kw"], d["jb"]
                i, ti, ntl, k0 = jb["i"], jb["ti"], jb["ntl"], jb["k0"]
                ATt = ATs[m]
                po = pos[i % 2]
                S.wait(ACT, ev_t, at_free[m])
                S.op(ACT, lambda e: e.activation(out=ATt[:, 0:kw], in_=pT, func=AF.Copy))
                ev_at = S.sig(ACT)
                pt_free[m] = ev_at
                S.wait(PE, ev_at)
                if ti == 0:
                    S.wait(PE, po_free[i % 2])
                nsb = kw // 128
                for sbk in range(nsb):
                    blk = k0 // 128 + sbk
                    S.op(PE, lambda e, sbk=sbk, blk=blk, first=(ti == 0 and sbk == 0), last=(ti == ntl - 1 and sbk == nsb - 1):
                         e.matmul(po, lhsT=ATt[:, sbk * 128:(sbk + 1) * 128], rhs=Vs[:, blk, :], start=first, stop=last))
                at_free[m] = S.sig(PE)
                if ti == ntl - 1:
                    epilogue(i, at_free[m], h)

            def epilogue(i, ev_po, h, l=l):
                qs = slice(i * 128, (i + 1) * 128)
                po = pos[i % 2]
                k3 = (nbk[0] % 16) * 3
                nbk[0] += 1
                ssq, rsd = NB[:, k3 + 1:k3 + 2], NB[:, k3 + 2:k3 + 3]
                S.op(DVE, lambda e: e.memset(ssq, 0.0))
                S.wait(ACT, S.sig(DVE), ev_po, on_free[0])
                S.op(ACT, lambda e: e.activation(out=ONb[:, :], in_=po, func=AF.Square, accum_out=ssq))
                S.wait(ACT, S.sig(ACT))
                S.op(ACT, lambda e: e.activation(out=rsd, in_=ssq, func=AF.Ln, scale=1.0 / 128.0, bias=c_eps[:, 0:1]))
                S.wait(ACT, S.sig(ACT))
                S.op(ACT, lambda e: e.activation(out=rsd, in_=rsd, func=AF.Exp, scale=-0.5))
                S.wait(ACT, S.sig(ACT))
                S.op(ACT, lambda e: e.activation(out=ONb[:, :], in_=po, func=AF.Copy, scale=rsd))
                ev_on = S.sig(ACT)
                po_free[i % 2] = ev_on
                pT = PST[:, 0:128]
                S.wait(PE, ev_on, pt_free[0])
                S.op(PE, lambda e: e.transpose(out=pT, in_=ONb[:, :], identity=c_id[:, :]))
                ev_tr = S.sig(PE)
                on_free[0] = ev_tr
                S.wait(ACT, ev_tr)
                S.op(ACT, lambda e: e.activation(out=OL[:, qs], in_=pT, func=AF.Copy,
                                                 scale=p_sbog[:, l * H + h:l * H + h + 1]))
                pt_free[0] = S.sig(ACT)

            nj = len(jobs)
            pre = {}
            if DBUF and h + 1 < H:
                pre[min(2, nj + 1)] = [lambda h=h: load_kv(h + 1)]
                for cq in range(NCQ):
                    pre.setdefault(min(3 + 12 * cq, nj + 1), []).append(lambda h=h, cq=cq: load_q_dma(h + 1, cq))
                    pre.setdefault(min(11 + 12 * cq, nj + 1), []).append(lambda h=h, cq=cq: load_q_sel(h + 1, cq))
            for step in range(-1, nj + 2):
                for fn in pre.get(step, []):
                    fn()
                if 0 <= step - 2 < nj:
                    stageB1(step - 2)
                if 0 <= step - 1 < nj:
                    stageA2(step - 1)
                if 0 <= step + 1 < nj:
                    stageA1pe(step + 1)
                if 0 <= step < nj:
                    stageA1(step)
                if 0 <= step - 2 < nj:
                    stageB2(step - 2)
            ev_last = S.sig(ACT)
            S.wait(SP, ev_last)
            ev_old = S.dma(SP, d_ol[h], OL, "hgo")
            S.wait(POOL, ev_old)
            ev_oall.append(S.coll("AllGather", GR, d_ol[h], d_oall[h], "cc"))
            hd_free[hb] = [S.sig(PE), ev_last, ev_old]
        S.barrier()
        OAs = [BIGB[:, k * GT:(k + 1) * GT].rearrange("p (r t) -> p r t", r=G) for k in range(2)]
        oa_free = [None, None]
        for h in range(H):
            k = h % 2
            S.wait(SP, oa_free[k], ev_oall[h])
            ev = S.dma_split(SP, OAs[k], d_oall[h].rearrange("(r d) t -> d r t", r=G), ("c2", "actt")[k], G)
            S.wait(DVE, ev)
            for bq in range(NQB):
                dst = HT[:, H + h, bq * 128:(bq + 1) * 128]
                for s4 in range(G):
                    gb = s4 * NQB + bq
                    m4, ii = gb % 4, gb // 4
                    src = OAs[k][:, m4, ii * 128:(ii + 1) * 128]
                    if s4 == 0:
                        S.op(DVE, lambda e, dst=dst, src=src: e.tensor_scalar(out=dst, in0=src, scalar1=c_flags[:, 8:9], scalar2=None, op0=ALU.mult))
                    else:
                        S.wait(DVE, S.sig(DVE))
                        S.op(DVE, lambda e, dst=dst, src=src, s4=s4: e.scalar_tensor_tensor(out=dst, in0=src, scalar=c_flags[:, 8 + s4:9 + s4], in1=dst,
                                                                                              op0=ALU.mult, op1=ALU.add))
            oa_free[k] = S.sig(DVE)
        S.barrier()
        if "D" in dbg and l == 0:
            break

        def evac_res(gcol0, l=l):
            def f(blk, tt, pss, evp):
                ts = slice(tt * TT, (tt + 1) * TT)
                si = get_stg(SP)
                ev = S.dma(SP, STG[si][:, 0:TT], xs[:, blk, ts], f"sg{si}")
                S.wait(DVE, ev, evp)
                S.op(DVE, lambda e: e.scalar_tensor_tensor(out=STG[si][:, 0:TT], in0=pss, scalar=modl[:, l, gcol0 + blk:gcol0 + blk + 1],
                                                           in1=STG[si][:, 0:TT], op0=ALU.mult, op1=ALU.add))
                ev = S.sig(DVE)
                S.wait(SP, ev)
                stg_free[si] = S.dma(SP, xs[:, blk, ts], STG[si][:, 0:TT], f"sg{si}")
                return ev
            return f
        gemm_fm(lambda blk, l=l: wout[l, blk], KD, KD, lambda kc, tt: HT[:, kc, tt * TT:(tt + 1) * TT], NT, TT,
                evac_res(2 * KD), KD * 128)
        S.barrier()
        if "E" in dbg and l == 0:
            break

        evn = norm_to_HT(l, 1, p_n2g)
        S.wait(PE, evn)
        gate_sb = {}

        def evac_ffn_in(blk, tt, pss, evp, l=l):
            j, which = blk // 2, blk % 2
            ts = slice(tt * TT, (tt + 1) * TT)
            if which == 0:
                si = get_stg(ACT)
                S.wait(ACT, evp)
                S.op(ACT, lambda e: e.activation(out=STG[si][:, 0:TT], in_=pss, func=AF.Silu))
                ev = S.sig(ACT)
                gate_sb[(j, tt)] = (si, ev)
                return ev
            si, evg = gate_sb.pop((j, tt))
            so = get_stb(DVE)
            S.wait(DVE, evg, evp)
            S.op(DVE, lambda e: e.tensor_mul(out=STB[so][:, 0:TT], in0=pss, in1=STG[si][:, 0:TT]))
            ev = S.sig(DVE)
            stg_free[si] = ev
            S.wait(SP, ev)
            stb_free[so] = S.dma(SP, d_act[j, :, ts], STB[so][:, 0:TT], f"sb{so}")
            return ev

        gemm_fm(lambda blk, l=l: wfi[l, blk // 2, :, (blk % 2) * KD * 128:(blk % 2 + 1) * KD * 128], 2 * NF, KD,
                lambda kc, tt: HT[:, kc, tt * TT:(tt + 1) * TT], NT, TT, evac_ffn_in, KD * 128)
        S.barrier()
        ACTT = BIGB[:, 0:NF * TT].rearrange("p (k t) -> p k t", k=NF)
        assert NF * TT <= 2 * BIGN
        act_free = None
        for tt in range(NT):
            ts = slice(tt * TT, (tt + 1) * TT)
            S.wait(SP, act_free)
            ev_act = S.dma_split(SP, ACTT, d_act[:, :, ts].rearrange("k p t -> p k t"), "actt", 11)
            S.wait(PE, ev_act)
            er = evac_res(5 * KD)
            for blk in range(KD):
                i, ev = wload(wfo[l, blk], NF * 128)
                S.wait(PE, ev)
                pi = get_ps()
                for kc in range(NF):
                    S.op(PE, lambda e, i=i, kc=kc, pi=pi: e.matmul(psb[pi][:, 0:TT], lhsT=WR[i][:, kc * 128:(kc + 1) * 128], rhs=ACTT[:, kc, :],
                                                                   start=(kc == 0), stop=(kc == NF - 1)))
                evp = S.sig(PE)
                wfree[i] = evp
                psfree[pi] = er(blk, tt, psb[pi][:, 0:TT], evp)
            act_free = S.sig(PE)
        S.barrier()
    scr = dict(d_qa=d_qa, d_kk=d_kk, d_lf=d_lf, d_gs=d_gs, d_vhg=d_vhg, d_stloc=d_stloc, d_stall=d_stall, d_oloc=d_oloc, d_qh=d_qh, d_act=d_act,
               d_modall=d_modall)
    if "HT" in dbg:
        o = nc.dram_tensor("dbg_HT", [128, KD, T], BF16, kind="ExternalOutput").ap()
        S.dma(SP, o, HT[:, :, :], "dbgout")
    for name in dbg:
        if name in scr:
            src = scr[name]
            o = nc.dram_tensor("dbg_" + name, list(src.shape), src.dtype, kind="ExternalOutput").ap()
            S.dma(SP, o, src, "dbgout")
    ev = S.dma_split(SP, yT[:, :, :], xs[:, :, :], "out", KD)
    S.wait(SP, ev)
    S.barrier()
    with nc.Block() as block:
        S.emit(block)
    stack.close()
    return nc, S


def _bf(a):
    return a


def prep_inputs(cfg, inp):
    D, T, L, KD, H, NF, NMB = cfg.D, cfg.T, cfg.L, cfg.KD, cfg.H, cfg.NF, cfg.NMB
    HW = H * 128
    f = lambda a: np.ascontiguousarray(a, dtype=np.float32)
    x = np.asarray(inp["x"], np.float32)
    c = np.asarray(inp["c"], np.float32)

    def colvec(a, n):
        a = np.asarray(a, np.float32).reshape(L, n, 128)
        return f(a.transpose(2, 0, 1).reshape(128, L * n))

    w_in = np.asarray(inp["w_in"], np.float32)
    segs = [0, 3, 1, 4, 5]
    fm = np.concatenate([w_in[:, :, s * HW:(s + 1) * HW] for s in segs], axis=2)
    fm = fm.reshape(L, KD, 128, 5 * H, 128).transpose(0, 3, 2, 1, 4)
    win_fm = f(fm.reshape(L, 5 * H, 128, KD * 128))
    tm = np.stack([w_in[:, :, 2 * HW:3 * HW], w_in[:, :, 6 * HW:7 * HW]], axis=1)
    tm = tm.reshape(L, 2, KD, 128, HW).transpose(0, 1, 3, 2, 4)
    win_tm = f(tm.reshape(L, 2, 128, KD * HW))
    wo = np.asarray(inp["w_out"], np.float32).reshape(L, KD, 128, KD, 128).transpose(0, 3, 2, 1, 4)
    wout = f(wo.reshape(L, KD, 128, KD * 128))
    wi = np.asarray(inp["w_ffn_in"], np.float32).reshape(L, KD, 128, 2, NF, 128).transpose(0, 4, 2, 3, 1, 5)
    wfi = f(wi.reshape(L, NF, 128, 2 * KD * 128))
    wf = np.asarray(inp["w_ffn_out"], np.float32).reshape(L, NF, 128, KD, 128).transpose(0, 3, 2, 1, 4)
    wfo = f(wf.reshape(L, KD, 128, NF * 128))
    wa = np.asarray(inp["w_ada"], np.float32).reshape(L, KD, 128, 6 * KD, 128).transpose(0, 3, 2, 1, 4)
    ba = np.asarray(inp["b_ada"], np.float32).reshape(L, 6 * KD, 128)
    cTt = f(c.T.reshape(KD, 128, 2).transpose(1, 0, 2))
    common = dict(
        cT=cTt, n1g=colvec(inp["norm1_g"], KD), n2g=colvec(inp["norm2_g"], KD),
        hgog=colvec(inp["hg_out_g"], H), sbog=colvec(inp["sb_out_g"], H),
        sbqg=colvec(inp["sb_q_g"], 1), sbkg=colvec(inp["sb_k_g"], 1), lbl=colvec(inp["hg_lb_logits"], H),
        win_fm=win_fm, win_tm=win_tm, wout=wout, wfi=wfi, wfo=wfo,
    )
    s_idx = np.arange(128)
    mhg = ((s_idx[:, None] <= (s_idx[None, :] % 64)) & (s_idx[:, None] < 64)).astype(np.float32)
    msb = (s_idx[None, :] < s_idx[:, None]).astype(np.float32)
    ident = np.eye(128, dtype=np.float32)
    scan = np.ones((128, min(T, 1024)), np.float32)
    scan[:, ::64] = 0.0
    maps = []
    for r in range(NCORES):
        b, j = r // G, r % G
        xt = x[b, j * T:(j + 1) * T, :].T.reshape(KD, 128, T).transpose(1, 0, 2)
        fl = np.zeros((128, 32), np.float32)
        for o in range(G):
            fl[:, o] = 1.0 if o <= j else 0.0
            fl[:, 4 + o] = 1.0 if o < j else 0.0
            fl[:, 8 + o] = 1.0 if o == j else 0.0
            fl[:, 12 + o] = 1.0 if o == j - 1 else 0.0
            fl[:, 16 + o] = 0.0 if o <= j else BIGNEG
            fl[:, 20 + o] = 0.0 if o < j else BIGNEG
        fl[:, 24] = 1.0 if b == 0 else 0.0
        fl[:, 25] = 1.0 if b == 1 else 0.0
        wad = f(wa[:, j * NMB:(j + 1) * NMB].reshape(L, NMB, 128, KD * 128))
        bad = ba[:, j * NMB:(j + 1) * NMB, :].transpose(2, 0, 1)
        bad = f(np.repeat(bad[:, :, :, None], 2, axis=3).reshape(128, L * NMB * 2))
        m = dict(common)
        m.update(xT=f(xt), wada=wad, bada=bad, flags=fl, cmask_hg=mhg, cmask_sb=msb, cident=ident, cscan=scan)
        maps.append(m)
    return maps


def assemble(cfg, results):
    D, T, KD = cfg.D, cfg.T, cfg.KD
    out = np.zeros((2, cfg.S, D), np.float32)
    for r in range(NCORES):
        b, j = r // G, r % G
        y = np.asarray(results[r]["yT"], np.float32)
        out[b, j * T:(j + 1) * T, :] = y.transpose(1, 0, 2).reshape(D, T).T
    return out


_CACHE = {}


def kernel(**inputs):
    x = inputs["x"]
    B, Sq, D = x.shape
    L = inputs["w_in"].shape[0]
    DFF = inputs["w_ffn_out"].shape[1]
    cfg = Cfg(D, Sq, L, DFF)
    key = (D, Sq, L, DFF)
    if key not in _CACHE:
        _CACHE[key] = build(cfg)[0]
    nc = _CACHE[key]
    maps = prep_inputs(cfg, inputs)
    res = run_bass_kernel_spmd(nc, maps, core_ids=list(range(NCORES)))
    return assemble(cfg, res.results)
```
